# Optimizing a Trainium2 kernel written in Bass

```python
import math
import jax, jax.numpy as jnp
from jax import lax
import numpy as np

D_MODEL = 1024
BATCH = 1
SEQ = 16384
DEPTH = 2

HEAD_DIM = 64
FOX_HEADS = 8
FOX_WIDTH = FOX_HEADS * HEAD_DIM
DIFF_HEADS = 4
DIFF_QK_WIDTH = DIFF_HEADS * 2 * HEAD_DIM
DIFF_V_DIM = 2 * HEAD_DIM
DIFF_V_WIDTH = DIFF_HEADS * DIFF_V_DIM
IN_SPLIT_SIZES = (FOX_WIDTH, FOX_WIDTH, FOX_WIDTH, FOX_HEADS, DIFF_QK_WIDTH, DIFF_QK_WIDTH, DIFF_V_WIDTH, D_MODEL, D_MODEL)
IN_SPLIT_POINTS = tuple(int(p) for p in np.cumsum(IN_SPLIT_SIZES)[:-1])
N_IN = int(sum(IN_SPLIT_SIZES))
FOX_V_OFFSET = 2 * FOX_WIDTH
DIFF_V_OFFSET = 3 * FOX_WIDTH + FOX_HEADS + 2 * DIFF_QK_WIDTH
BLOCK_Q = 128
REL_BUCKETS = 32
REL_MAX_DISTANCE = 128
FGATE_BIAS_INIT = 3.0
D_FF = 2816
N_EXPERTS = 8
TOP_K = 2
D_FF_EXPERT = 3584
LN_EPS = 1e-5
SUBLN_EPS = 1e-5
DEEPNORM_ALPHA = (2.0 * DEPTH) ** 0.25
DEEPNORM_BETA = (8.0 * DEPTH) ** -0.25
N_DENSE = (DEPTH + 1) // 2
N_MOE = DEPTH // 2

kernel_name = 'hybrid_fox_diffattn_deepnorm_moe'


def layer_norm(x, g, b):
    x32 = x.astype(jnp.float32)
    mu = jnp.mean(x32, axis=-1, keepdims=True)
    xc = x32 - mu
    var = jnp.mean(xc * xc, axis=-1, keepdims=True)
    y = xc * lax.rsqrt(var + LN_EPS) * g.astype(jnp.float32) + b.astype(jnp.float32)
    return y.astype(x.dtype)


def t5_causal_bucket(dist):
    n = jnp.maximum(dist, 0)
    max_exact = REL_BUCKETS // 2
    nf = jnp.maximum(n, 1).astype(jnp.float32)
    log_part = jnp.log(nf / max_exact) / math.log(REL_MAX_DISTANCE / max_exact) * (REL_BUCKETS - max_exact)
    large = jnp.minimum(max_exact + log_part.astype(jnp.int32), REL_BUCKETS - 1)
    return jnp.where(n < max_exact, n, large)


def fox_attention(q, k, v, log_f):
    B, S, H, Dh = q.shape
    nb = S // BLOCK_Q
    scale = Dh ** -0.5
    cum = jnp.cumsum(log_f.astype(jnp.float32), axis=1)
    cum_k = jnp.transpose(cum, (0, 2, 1))
    kpos = jnp.arange(S)
    qb = jnp.moveaxis(q.reshape(B, nb, BLOCK_Q, H, Dh), 1, 0)
    cb = jnp.moveaxis(cum_k.reshape(B, H, nb, BLOCK_Q), 2, 0)

    def block(args):
        i, qi, ci = args
        qpos = i * BLOCK_Q + jnp.arange(BLOCK_Q)
        s = jnp.einsum('bqhd,bkhd->bhqk', qi, k, preferred_element_type=jnp.float32) * scale
        s = s + ci[..., :, None] - cum_k[:, :, None, :]
        s = jnp.where(kpos[None, :] <= qpos[:, None], s, -jnp.inf)
        p = jax.nn.softmax(s, axis=-1)
        return jnp.einsum('bhqk,bkhd->bqhd', p.astype(v.dtype), v, preferred_element_type=jnp.float32)

    out = lax.map(block, (jnp.arange(nb), qb, cb))
    return jnp.moveaxis(out, 0, 1).reshape(B, S, H, Dh)


def diff_attention(q, k, v, lam, rel_bias):
    B, S, H, _, Dh = q.shape
    nb = S // BLOCK_Q
    scale = Dh ** -0.5
    kpos = jnp.arange(S)
    table = rel_bias.astype(jnp.float32)
    qb = jnp.moveaxis(q.reshape(B, nb, BLOCK_Q, H, 2, Dh), 1, 0)

    def block(args):
        i, qi = args
        qpos = i * BLOCK_Q + jnp.arange(BLOCK_Q)
        dist = qpos[:, None] - kpos[None, :]
        bias = jnp.moveaxis(table[t5_causal_bucket(dist)], -1, 0)
        s = jnp.einsum('bqhcd,bkhcd->bhcqk', qi, k, preferred_element_type=jnp.float32) * scale
        s = s + bias[None, :, None]
        s = jnp.where(dist >= 0, s, -jnp.inf)
        p = jax.nn.softmax(s, axis=-1)
        a = p[:, :, 0] - lam * p[:, :, 1]
        return jnp.einsum('bhqk,bkhe->bqhe', a.astype(v.dtype), v, preferred_element_type=jnp.float32)

    out = lax.map(block, (jnp.arange(nb), qb))
    return jnp.moveaxis(out, 0, 1).reshape(B, S, H, v.shape[-1])


def token_mixer(u, w_in, b_f, lq1, lk1, lq2, lk2, subln_g, w_br_fox, w_br_diff, w_out, rel_bias, lam_init):
    B, S, _ = u.shape
    f32 = jnp.float32
    proj = jnp.einsum('bsd,dn->bsn', u, w_in)
    fq, fk, fv, fg, dq, dk, dv, ga, gb = jnp.split(proj, IN_SPLIT_POINTS, axis=-1)
    log_f = jax.nn.log_sigmoid(fg.astype(f32) + b_f.astype(f32))
    y_fox = fox_attention(fq.reshape(B, S, FOX_HEADS, HEAD_DIM), fk.reshape(B, S, FOX_HEADS, HEAD_DIM),
                          fv.reshape(B, S, FOX_HEADS, HEAD_DIM), log_f)
    y_fox = y_fox.reshape(B, S, FOX_WIDTH).astype(u.dtype)
    lam = (jnp.exp(jnp.sum(lq1.astype(f32) * lk1.astype(f32)))
           - jnp.exp(jnp.sum(lq2.astype(f32) * lk2.astype(f32))) + lam_init)
    o = diff_attention(dq.reshape(B, S, DIFF_HEADS, 2, HEAD_DIM), dk.reshape(B, S, DIFF_HEADS, 2, HEAD_DIM),
                       dv.reshape(B, S, DIFF_HEADS, DIFF_V_DIM), lam, rel_bias)
    o = o * lax.rsqrt(jnp.mean(o * o, axis=-1, keepdims=True) + SUBLN_EPS) * subln_g.astype(f32) * (1.0 - lam_init)
    y_diff = o.reshape(B, S, DIFF_V_WIDTH).astype(u.dtype)
    branch_fox = y_fox @ w_br_fox
    branch_diff = y_diff @ w_br_diff
    merged = jax.nn.sigmoid(ga) * branch_fox + jax.nn.sigmoid(gb) * branch_diff
    return merged @ w_out


def swiglu(x, w_gate_up, w_down):
    g, up = jnp.split(x @ w_gate_up, 2, axis=-1)
    return (jax.nn.silu(g) * up) @ w_down


def moe_swiglu(x, w_router, w_gate_up, w_down):
    logits = (x @ w_router).astype(jnp.float32)
    top_v, top_i = lax.top_k(logits, TOP_K)
    gates = jax.nn.softmax(top_v, axis=-1)
    combine = jnp.sum(jax.nn.one_hot(top_i, N_EXPERTS, dtype=jnp.float32) * gates[..., None], axis=-2)
    y = jnp.zeros_like(x)
    for e in range(N_EXPERTS):
        y = y + combine[..., e:e + 1].astype(x.dtype) * swiglu(x, w_gate_up[e], w_down[e])
    return y


def setup_inputs(seed: int = 0) -> dict:
    key = jax.random.key(seed)
    ks = jax.random.split(key, 32)
    f32 = jnp.float32

    def nrm(k, shape, scale):
        return jax.random.normal(k, shape, f32) * scale

    col_scale = np.ones((N_IN,), np.float32)
    col_scale[FOX_V_OFFSET:FOX_V_OFFSET + FOX_WIDTH] = DEEPNORM_BETA
    col_scale[DIFF_V_OFFSET:DIFF_V_OFFSET + DIFF_V_WIDTH] = DEEPNORM_BETA
    w_in = nrm(ks[3], (DEPTH, D_MODEL, N_IN), D_MODEL ** -0.5) * jnp.asarray(col_scale)
    return {
        'x': nrm(ks[0], (BATCH, SEQ, D_MODEL), 1.0),
        'ln_in_g': 1.0 + nrm(ks[1], (D_MODEL,), 0.02),
        'ln_in_b': nrm(ks[2], (D_MODEL,), 0.02),
        'w_in': w_in,
        'b_fgate': FGATE_BIAS_INIT + nrm(ks[4], (DEPTH, FOX_HEADS), 0.5),
        'lam_q1': nrm(ks[5], (DEPTH, HEAD_DIM), 0.1),
        'lam_k1': nrm(ks[6], (DEPTH, HEAD_DIM), 0.1),
        'lam_q2': nrm(ks[7], (DEPTH, HEAD_DIM), 0.1),
        'lam_k2': nrm(ks[8], (DEPTH, HEAD_DIM), 0.1),
        'subln_g': 1.0 + nrm(ks[9], (DEPTH, DIFF_V_DIM), 0.02),
        'w_branch_fox': nrm(ks[10], (DEPTH, FOX_WIDTH, D_MODEL), FOX_WIDTH ** -0.5 * DEEPNORM_BETA),
        'w_branch_diff': nrm(ks[11], (DEPTH, DIFF_V_WIDTH, D_MODEL), DIFF_V_WIDTH ** -0.5 * DEEPNORM_BETA),
        'w_out': nrm(ks[12], (DEPTH, D_MODEL, D_MODEL), D_MODEL ** -0.5 * DEEPNORM_BETA),
        'ln_mix_g': 1.0 + nrm(ks[13], (DEPTH, D_MODEL), 0.02),
        'ln_mix_b': nrm(ks[14], (DEPTH, D_MODEL), 0.02),
        'rel_bias': nrm(ks[15], (REL_BUCKETS, DIFF_HEADS), 0.5),
        'w_ffn_gate_up': nrm(ks[16], (N_DENSE, D_MODEL, 2 * D_FF), D_MODEL ** -0.5 * DEEPNORM_BETA),
        'w_ffn_down': nrm(ks[17], (N_DENSE, D_FF, D_MODEL), D_FF ** -0.5 * DEEPNORM_BETA),
        'w_router': nrm(ks[18], (N_MOE, D_MODEL, N_EXPERTS), D_MODEL ** -0.5),
        'w_expert_gate_up': nrm(ks[19], (N_MOE, N_EXPERTS, D_MODEL, 2 * D_FF_EXPERT), D_MODEL ** -0.5 * DEEPNORM_BETA),
        'w_expert_down': nrm(ks[20], (N_MOE, N_EXPERTS, D_FF_EXPERT, D_MODEL), D_FF_EXPERT ** -0.5 * DEEPNORM_BETA),
        'ln_ffn_g': 1.0 + nrm(ks[21], (DEPTH, D_MODEL), 0.02),
        'ln_ffn_b': nrm(ks[22], (DEPTH, D_MODEL), 0.02),
    }


def reference(x, ln_in_g, ln_in_b, w_in, b_fgate, lam_q1, lam_k1, lam_q2, lam_k2, subln_g,
              w_branch_fox, w_branch_diff, w_out, ln_mix_g, ln_mix_b, rel_bias,
              w_ffn_gate_up, w_ffn_down, w_router, w_expert_gate_up, w_expert_down,
              ln_ffn_g, ln_ffn_b):
    h = layer_norm(x, ln_in_g, ln_in_b)
    for l in range(DEPTH):
        lam_init = 0.8 - 0.6 * math.exp(-0.3 * l)
        mix = token_mixer(h, w_in[l], b_fgate[l], lam_q1[l], lam_k1[l], lam_q2[l], lam_k2[l], subln_g[l],
                          w_branch_fox[l], w_branch_diff[l], w_out[l], rel_bias, lam_init)
        h = layer_norm(DEEPNORM_ALPHA * h + mix, ln_mix_g[l], ln_mix_b[l])
        if l % 2 == 0:
            f = swiglu(h, w_ffn_gate_up[l // 2], w_ffn_down[l // 2])
        else:
            f = moe_swiglu(h, w_router[l // 2], w_expert_gate_up[l // 2], w_expert_down[l // 2])
        h = layer_norm(DEEPNORM_ALPHA * h + f, ln_ffn_g[l], ln_ffn_b[l])
    return h
```

```python
import math
from contextlib import ExitStack
import numpy as np
import ml_dtypes
import concourse.bass as bass
import concourse.mybir as mybir
from concourse.bass_utils import run_bass_kernel_spmd

F32 = mybir.dt.float32
BF16 = mybir.dt.bfloat16
AF = mybir.ActivationFunctionType
ALU = mybir.AluOpType
AX = mybir.AxisListType

ENGS = ("pe", "act", "dve", "pool", "sp")


class Sched:
    def __init__(self, nc, stack):
        self.nc = nc
        self.stack = stack
        self.sem = {e: stack.enter_context(nc.semaphore("s_" + e)) for e in ENGS}
        self.nsig = {e: 0 for e in ENGS}
        self.gcount = {e: 0 for e in ENGS}
        self.ops = {e: [] for e in ENGS}
        self.recs = {}
        self.ordinal = {}
        self.res = {}
        self.ch = {}

    def chan(self, name):
        if name not in self.ch:
            self.ch[name] = [self.stack.enter_context(self.nc.semaphore("c_" + name)), 0]
        return self.ch[name]

    def _state(self, key):
        st = self.res.get(key)
        if st is None:
            st = {"lw": None, "rd": []}
            self.res[key] = st
        return st

    def op(self, eng, fn, reads=(), writes=(), dma_ch=None):
        waits = []
        seen = set()

        def add(tok, raw):
            if tok is None or tok in seen:
                return
            seen.add(tok)
            if tok[0] == "eng" and tok[1] == eng and dma_ch is None:
                if not raw or eng == "pe":
                    return
            waits.append(tok)

        for k in reads:
            add(self._state(k)["lw"], True)
        for k in writes:
            st = self._state(k)
            add(st["lw"], False)
            for t in st["rd"]:
                add(t, False)
        gidx = self.gcount[eng]
        self.gcount[eng] += 1
        rec = {"fn": fn, "waits": waits, "sig": False, "dma": None, "g": gidx}
        if dma_ch is not None:
            c = self.chan(dma_ch)
            c[1] += 1
            rec["dma"] = (dma_ch, c[1])
            me = ("dma", dma_ch, c[1])
        else:
            me = ("eng", eng, gidx)
            self.recs[(eng, gidx)] = rec
        self.ops[eng].append(rec)
        for k in reads:
            st = self._state(k)
            if me[0] == "eng":
                st["rd"] = [t for t in st["rd"] if not (t[0] == "eng" and t[1] == eng)]
            st["rd"].append(me)
        for k in writes:
            st = self._state(k)
            st["lw"] = me
            st["rd"] = []
        return me

    def flush(self, final=False):
        nc = self.nc
        for e in ENGS:
            for rec in self.ops[e]:
                for t in rec["waits"]:
                    if t[0] == "eng" and (t[1], t[2]) in self.recs:
                        self.recs[(t[1], t[2])]["sig"] = True
        last = {}
        for e in ENGS:
            for rec in self.ops[e]:
                if rec["dma"] is None:
                    last[e] = rec
            if e in last:
                last[e]["sig"] = True
        for e in ENGS:
            n = self.nsig[e]
            for rec in self.ops[e]:
                if rec["sig"]:
                    n += 1
                    self.ordinal[(e, rec["g"])] = n
            self.nsig[e] = n
        ops, sem, ch, ordinal = self.ops, self.sem, self.ch, self.ordinal
        self.ops = {e: [] for e in ENGS}
        self.recs = {}
        def collapse(t):
            if t is not None and t[0] == "eng" and t[1] in last:
                return ("eng", t[1], last[t[1]]["g"])
            return t
        for st in self.res.values():
            st["lw"] = collapse(st["lw"])
            st["rd"] = list({collapse(t) for t in st["rd"]})
        final_ch = [(c[0], 16 * c[1]) for c in ch.values() if c[1] > 0] if final else []

        prev_bar = getattr(self, "bar", None)
        self.bar = ({e2: self.nsig[e2] for e2 in ENGS}, {k: 16 * c[1] for k, c in ch.items()})

        def run(e, engine):
            waited = {}
            if prev_bar is not None:
                for e2, v in prev_bar[0].items():
                    if e2 != e and v > 0:
                        engine.wait_ge(sem[e2], v)
                        waited["e" + e2] = v
                for k, v in prev_bar[1].items():
                    if v > 0:
                        engine.wait_ge(ch[k][0], v)
                        waited["c" + k] = v
            for rec in ops[e]:
                for t in rec["waits"]:
                    if t[0] == "eng":
                        s, v, key = sem[t[1]], ordinal[(t[1], t[2])], "e" + t[1]
                    else:
                        s, v, key = ch[t[1]][0], 16 * t[2], "c" + t[1]
                    if waited.get(key, 0) >= v:
                        continue
                    waited[key] = v
                    engine.wait_ge(s, v)
                ins = rec["fn"](engine)
                if rec["dma"] is not None:
                    ins.then_inc(ch[rec["dma"][0]][0], 16)
                elif rec["sig"]:
                    ins.then_inc(sem[e], 1)
            if e == "sp":
                for s, v in final_ch:
                    engine.wait_ge(s, v)

        with nc.Block() as block:
            @block.tensor
            def _(pe):
                run("pe", pe)

            @block.scalar
            def _(act):
                run("act", act)

            @block.vector
            def _(dve):
                run("dve", dve)

            @block.gpsimd
            def _(pool):
                run("pool", pool)

            @block.sync
            def _(sp):
                run("sp", sp)

D = 1024
S_ALL = 16384
NCORE = 8
TL = 2048
NIN = 5128
DFF = 2816
DFE = 3584
NE = 8
LN_EPS = 1e-5
ALPHA = 4.0 ** 0.25
NEG = -30000.0
NPADB = 28
C_FQ, C_FK, C_FV, C_FG, C_DQ, C_DK, C_DV, C_GA, C_GB = 0, 512, 1024, 1536, 1544, 2056, 2568, 3080, 4104


class Ctx:
    def __init__(self, nc, S):
        self.nc = nc
        self.S = S
        self.uid = 0
        self.dram = {}

    def phase(self):
        return Phase(self)


class Phase:
    def __init__(self, c):
        self.c = c
        self.st = ExitStack()

    def __enter__(self):
        self.st.__enter__()
        return self

    def sb(self, name, shape, dt):
        self.c.uid += 1
        return self.st.enter_context(self.c.nc.sbuf_tensor("%s_%d" % (name, self.c.uid), shape, dt))

    def ps(self, name, shape, dt=F32):
        self.c.uid += 1
        return self.st.enter_context(self.c.nc.psum_tensor("%s_%d" % (name, self.c.uid), shape, dt))

    def __exit__(self, *a):
        if a[0] is None:
            self.c.S.flush()
        return self.st.__exit__(*a)


def dma(S, eng, out, in_, reads, writes, ch):
    return S.op(eng, lambda e: e.dma_start(out=out, in_=in_), reads=reads, writes=writes, dma_ch=ch)


def bcast_rows(ap1d, n, parts=128):
    return ap1d.rearrange("(o n) -> o n", o=1).broadcast_to([parts, n])


class LNUnit:
    def __init__(self, c, ph, g_ap, b_ap, ident_ap, want_t32=False, tag="ln", with_t=True):
        S = c.S
        self.c, self.ph, self.tag = c, ph, tag
        self.gB = ph.sb("gB", [128, D], F32)
        self.bB = ph.sb("bB", [128, D], F32)
        self.ident = ph.sb("ident", [128, 128], BF16)
        dma(S, "sp", self.gB[:], bcast_rows(g_ap, D), [], [tag + "gB"], tag + "c0")
        dma(S, "sp", self.bB[:], bcast_rows(b_ap, D), [], [tag + "bB"], tag + "c0")
        dma(S, "pool", self.ident[:], ident_ap, [], [tag + "id"], tag + "c1")
        self.junk = ph.sb("junk", [128, D], F32)
        self.st = [ph.sb("st%d" % i, [128, 8], F32) for i in range(2)]
        self.y = [ph.sb("y%d" % i, [128, D], F32) for i in range(2)]
        if with_t:
            self.yb = [ph.sb("yb%d" % i, [128, D], BF16) for i in range(2)]
            self.tp = ph.ps("tp", [128, 8, 128], BF16)
            self.ts = [ph.sb("ts%d" % i, [128, 8, 128], BF16) for i in range(2)]
        else:
            self.yb = [None, None]
        self.want_t32 = want_t32
        if want_t32:
            self.ident32 = ph.sb("ident32", [128, 128], F32)
            S.op("dve", lambda e: e.tensor_copy(out=self.ident32[:], in_=self.ident[:]),
                 reads=[tag + "id"], writes=[tag + "id32"])
            self.tp32 = ph.ps("tp32", [128, 4, 128], F32)
            self.ts32 = [ph.sb("ts32%d" % i, [128, 8, 128], F32) for i in range(2)]
        self.n = 0

    def run(self, z, zkey, h_out_ap, hT_out, tok0, hT32_out=None, do_t=True):
        S, tag = self.c.S, self.tag
        i = self.n % 2
        self.n += 1
        st, y, yb, junk = self.st[i], self.y[i], self.yb[i], self.junk
        kst, ky, kyb = "%sst%d" % (tag, i), "%sy%d" % (tag, i), "%syb%d" % (tag, i)
        S.op("act", lambda e: e.activation(out=junk[:], in_=z, func=AF.Identity, accum_out=st[:, 0:1]),
             reads=[zkey], writes=[tag + "junk", kst])
        S.op("act", lambda e: e.activation(out=junk[:], in_=z, func=AF.Square, accum_out=st[:, 1:2]),
             reads=[zkey], writes=[tag + "junk", kst])
        S.op("dve", lambda e: e.tensor_scalar(out=st[:, 2:4], in0=st[:, 0:2], scalar1=1.0 / D, scalar2=None,
                                              op0=ALU.mult), reads=[kst], writes=[kst])
        S.op("dve", lambda e: e.tensor_tensor(out=st[:, 4:5], in0=st[:, 2:3], in1=st[:, 2:3], op=ALU.mult),
             reads=[kst], writes=[kst])
        S.op("dve", lambda e: e.tensor_tensor(out=st[:, 5:6], in0=st[:, 3:4], in1=st[:, 4:5], op=ALU.subtract),
             reads=[kst], writes=[kst])
        S.op("act", lambda e: e.activation(out=st[:, 6:7], in_=st[:, 5:6], func=AF.Sqrt, bias=self.epsc[:], scale=1.0),
             reads=[kst, tag + "eps"], writes=[kst])
        S.op("dve", lambda e: e.reciprocal(out=st[:, 6:7], in_=st[:, 6:7]), reads=[kst], writes=[kst])
        S.op("dve", lambda e: e.scalar_tensor_tensor(out=st[:, 7:8], in0=st[:, 2:3], scalar=-1.0, in1=st[:, 6:7],
                                                     op0=ALU.mult, op1=ALU.mult), reads=[kst], writes=[kst])
        S.op("act", lambda e: e.activation(out=y[:], in_=z, func=AF.Identity, bias=st[:, 7:8], scale=st[:, 6:7]),
             reads=[zkey, kst], writes=[ky])
        S.op("dve", lambda e: e.tensor_tensor(out=y[:], in0=y[:], in1=self.gB[:], op=ALU.mult),
             reads=[ky, tag + "gB"], writes=[ky])
        S.op("dve", lambda e: e.tensor_tensor(out=y[:], in0=y[:], in1=self.bB[:], op=ALU.add),
             reads=[ky, tag + "bB"], writes=[ky])
        dma(S, "sp", h_out_ap, y[:], [ky], [("hdram", id(h_out_ap))], tag + "ho%d" % i)
        if not do_t:
            return
        S.op("pool", lambda e: e.tensor_copy(out=yb[:], in_=y[:]), reads=[ky], writes=[kyb])
        for k in range(8):
            S.op("pe", lambda e, k=k: e.transpose(out=self.tp[:, k, :], in_=yb[:, k * 128:(k + 1) * 128],
                                                  identity=self.ident[:]),
                 reads=[kyb, tag + "id"], writes=[tag + "tp"])
        ts = self.ts[i]
        S.op("act", lambda e: e.copy(out=ts[:], in_=self.tp[:]), reads=[tag + "tp"], writes=[tag + "ts%d" % i])
        dma(S, "sp", hT_out.rearrange("(k p) t -> p k t", p=128)[:, :, tok0:tok0 + 128], ts[:],
            [tag + "ts%d" % i], [("hT", tok0)], tag + "to%d" % i)
        if self.want_t32 and hT32_out is not None:
            ts32 = self.ts32[i]
            for hf in range(2):
                for k in range(4):
                    kk = hf * 4 + k
                    S.op("pe", lambda e, k=k, kk=kk: e.transpose(out=self.tp32[:, k, :],
                                                                 in_=y[:, kk * 128:(kk + 1) * 128],
                                                                 identity=self.ident32[:]),
                         reads=[ky, tag + "id32"], writes=[tag + "tp32"])
                S.op("dve", lambda e, hf=hf: e.tensor_copy(out=ts32[:, hf * 4:(hf + 1) * 4, :], in_=self.tp32[:]),
                     reads=[tag + "tp32"], writes=[tag + "ts32%d" % i])
            dma(S, "sp", hT32_out.rearrange("(k p) t -> p k t", p=128)[:, :, tok0:tok0 + 128], ts32[:],
                [tag + "ts32%d" % i], [("hT32", tok0)], tag + "t32o%d" % i)

    def setup_eps(self):
        S, tag = self.c.S, self.tag
        self.epsc = self.ph.sb("epsc", [128, 1], F32)
        S.op("dve", lambda e: e.memset(self.epsc[:], LN_EPS), writes=[tag + "eps"])


def phase_ln0(c, T):
    S = c.S
    with c.phase() as ph:
        ln = LNUnit(c, ph, T["ln_in_g"], T["ln_in_b"], T["ident"], tag="l0")
        ln.setup_eps()
        xb = [ph.sb("xb%d" % i, [128, D], F32) for i in range(2)]
        for tb in range(16):
            i = tb % 2
            dma(S, "sp", xb[i][:], T["x"][tb * 128:(tb + 1) * 128, :], [], ["xb%d" % i], "l0x%d" % i)
            ln.run(xb[i][:], "xb%d" % i, T["h"][tb * 128:(tb + 1) * 128, :], T["hT"], tb * 128)


def phase_inproj(c, T, l):
    S = c.S
    w = T["w_in"][l]
    with c.phase() as ph:
        hTs = ph.sb("hTs", [128, 8, TL], BF16)
        dma(S, "sp", hTs[:], T["hT"].rearrange("(k p) t -> p k t", p=128), [("hT", "all")], ["hTs"], "ip_h")
        wb = [ph.sb("wb%d" % i, [128, 8, 512], BF16) for i in range(3)]
        stg = [ph.sb("stg%d" % i, [128, TL], BF16) for i in range(2)]
        vst = [ph.sb("vst%d" % i, [128, 512], BF16) for i in range(2)]
        pss = [ph.ps("ps%d" % i, [128, 512]) for i in range(4)]
        wi = [0]
        pi = [0]

        def load_w(c0, ncols):
            b = wi[0] % 3
            wi[0] += 1
            dma(S, "pool", wb[b][:, :, 0:ncols], w[:, c0:c0 + ncols].rearrange("(k p) n -> p k n", p=128),
                [], ["wb%d" % b], "ip_w%d" % b)
            return b

        si = [0]
        qflat = T["qT"].rearrange("m r t -> (m r) t")
        kflat = T["kT"].rearrange("m r t -> (m r) t")
        sgflat = T["sg"].rearrange("g r t -> (g r) t")
        segs = []
        for j in range(1):
            segs.append((C_FQ, qflat, 0, "q"))
            segs.append((C_FK, kflat, 0, "k"))
            segs.append((C_DQ, qflat, 512, "q"))
            segs.append((C_DK, kflat, 512, "k"))
            segs.append((C_GA, sgflat, 0, "g"))
            segs.append((C_GA + 512, sgflat, 512, "g"))
            segs.append((C_GB, sgflat, 1024, "g"))
            segs.append((C_GB + 512, sgflat, 1536, "g"))
        for (c0, dst, r0, kind) in segs:
            b = load_w(c0, 512)
            for m in range(4):
                sb_i = si[0] % 2
                si[0] += 1
                for tt in range(4):
                    p = pi[0] % 4
                    pi[0] += 1
                    for k in range(8):
                        S.op("pe", lambda e, b=b, k=k, m=m, p=p, tt=tt: e.matmul(
                            pss[p][:], lhsT=wb[b][:, k, m * 128:(m + 1) * 128],
                            rhs=hTs[:, k, tt * 512:(tt + 1) * 512], start=(k == 0), stop=(k == 7)),
                            reads=["hTs", "wb%d" % b], writes=["ipps%d" % p])
                    o = stg[sb_i][:, tt * 512:(tt + 1) * 512]
                    if kind == "q":
                        S.op("act", lambda e, o=o, p=p: e.activation(out=o, in_=pss[p][:], func=AF.Copy, scale=0.125),
                             reads=["ipps%d" % p], writes=["stg%d" % sb_i])
                    elif kind == "k":
                        S.op("dve", lambda e, o=o, p=p: e.tensor_copy(out=o, in_=pss[p][:]),
                             reads=["ipps%d" % p], writes=["stg%d" % sb_i])
                    else:
                        S.op("act", lambda e, o=o, p=p: e.activation(out=o, in_=pss[p][:], func=AF.Sigmoid),
                             reads=["ipps%d" % p], writes=["stg%d" % sb_i])
                rr = r0 + m * 128
                dma(S, "sp", dst[rr:rr + 128, :], stg[sb_i][:], ["stg%d" % sb_i], [("fm", id(dst), rr)],
                    "ip_s%d" % sb_i)
        vi = [0]
        for (c0, dst) in ((C_FV, T["vf"]), (C_DV, T["vd"])):
            b = load_w(c0, 512)
            for tb in range(16):
                p = pi[0] % 4
                pi[0] += 1
                for k in range(8):
                    S.op("pe", lambda e, b=b, k=k, p=p, tb=tb: e.matmul(
                        pss[p][:], lhsT=hTs[:, k, tb * 128:(tb + 1) * 128], rhs=wb[b][:, k, :],
                        start=(k == 0), stop=(k == 7)), reads=["hTs", "wb%d" % b], writes=["ipps%d" % p])
                v = vi[0] % 2
                vi[0] += 1
                eng = "act" if tb % 2 == 0 else "dve"
                if eng == "act":
                    S.op("act", lambda e, v=v, p=p: e.copy(out=vst[v][:], in_=pss[p][:]),
                         reads=["ipps%d" % p], writes=["vst%d" % v])
                else:
                    S.op("dve", lambda e, v=v, p=p: e.tensor_copy(out=vst[v][:], in_=pss[p][:]),
                         reads=["ipps%d" % p], writes=["vst%d" % v])
                dma(S, "sp", dst[tb * 128:(tb + 1) * 128, :], vst[v][:], ["vst%d" % v], [("tm", id(dst), tb)],
                    "ip_v%d" % v)
        b = load_w(C_FG, 8)
        bf = ph.sb("bf", [128, 8], F32)
        dma(S, "sp", bf[:], bcast_rows(T["b_fgate"][l], 8), [], ["bf"], "ip_bf")
        lfs = ph.sb("lfs", [128, 16, 8], F32)
        t1 = ph.sb("t1", [128, 16, 8], F32)
        for tb in range(16):
            p = pi[0] % 4
            pi[0] += 1
            for k in range(8):
                S.op("pe", lambda e, b=b, k=k, p=p, tb=tb: e.matmul(
                    pss[p][:, 0:8], lhsT=hTs[:, k, tb * 128:(tb + 1) * 128], rhs=wb[b][:, k, 0:8],
                    start=(k == 0), stop=(k == 7)), reads=["hTs", "wb%d" % b], writes=["ipps%d" % p])
            S.op("dve", lambda e, p=p, tb=tb: e.tensor_tensor(out=t1[:, tb, :], in0=pss[p][:, 0:8], in1=bf[:],
                                                              op=ALU.add),
                 reads=["ipps%d" % p, "bf"], writes=["t1"])
        S.op("act", lambda e: e.activation(out=t1[:], in_=t1[:], func=AF.Exp, scale=-1.0), reads=["t1"], writes=["t1"])
        S.op("act", lambda e: e.activation(out=t1[:], in_=t1[:], func=AF.Ln, bias=1.0), reads=["t1"], writes=["t1"])
        S.op("dve", lambda e: e.tensor_scalar(out=lfs[:], in0=t1[:], scalar1=-1.0, scalar2=None, op0=ALU.mult),
             reads=["t1"], writes=["lfs"])
        dma(S, "sp", T["lf"].rearrange("(tb p) h -> p tb h", p=128), lfs[:], ["lfs"], ["lfd"], "ip_lf")


def phase_cum(c, T):
    S = c.S
    with c.phase() as ph:
        L = ph.sb("L", [128, 128, 8], F32)
        dma(S, "sp", L[:], T["lfL"].rearrange("(j t) h -> j t h", t=128), [], ["L"], "cu0")
        pb = ph.sb("pb", [128, 128], F32)
        dma(S, "sp", pb[:], T["padb"].rearrange("(j t) -> j t", t=128), [], ["pb"], "cu0")
        su = ph.sb("su", [128, 128], F32)
        dma(S, "sp", su[:], T["su"], [], ["su"], "cu0")
        A = ph.sb("A", [128, 8, 128], F32)
        B = ph.sb("B", [128, 8, 128], F32)
        S.op("dve", lambda e: e.tensor_copy(out=A[:], in_=L[:].rearrange("j t h -> j h t")), reads=["L"], writes=["A"])
        cur, nxt, kc, kn = A, B, "A", "B"
        sft = 1
        while sft < 128:
            S.op("dve", lambda e, cur=cur, nxt=nxt, sft=sft: e.tensor_tensor(
                out=nxt[:, :, sft:128], in0=cur[:, :, sft:128], in1=cur[:, :, 0:128 - sft], op=ALU.add),
                reads=[kc], writes=[kn])
            S.op("dve", lambda e, cur=cur, nxt=nxt, sft=sft: e.tensor_copy(out=nxt[:, :, 0:sft], in_=cur[:, :, 0:sft]),
                 reads=[kc], writes=[kn])
            cur, nxt, kc, kn = nxt, cur, kn, kc
            sft *= 2
        bs = ph.sb("bs", [128, 8], F32)
        S.op("dve", lambda e: e.tensor_copy(out=bs[:], in_=cur[:, :, 127]), reads=[kc], writes=["bs"])
        pso = ph.ps("pso", [128, 8])
        S.op("pe", lambda e: e.matmul(pso[:], lhsT=su[:], rhs=bs[:], start=True, stop=True),
             reads=["su", "bs"], writes=["pso"])
        off = ph.sb("off", [128, 8], F32)
        S.op("dve", lambda e: e.tensor_copy(out=off[:], in_=pso[:]), reads=["pso"], writes=["off"])
        cum = nxt
        for h in range(8):
            S.op("dve", lambda e, h=h: e.tensor_scalar(out=cum[:, h, :], in0=cur[:, h, :], scalar1=off[:, h:h + 1],
                                                       scalar2=None, op0=ALU.add), reads=[kc, "off"], writes=[kn])
        negc = cur
        for h in range(8):
            S.op("dve", lambda e, h=h: e.scalar_tensor_tensor(out=negc[:, h, :], in0=cum[:, h, :], scalar=-1.0,
                                                              in1=pb[:], op0=ALU.mult, op1=ALU.add),
                 reads=[kn, "pb"], writes=[kc])
        KXs = ph.sb("KXs", [128, 8, 4, 128], BF16)
        r1 = ph.sb("r1", [128, 8, 128], F32)
        S.op("pool", lambda e: e.memset(KXs[:, :, 0, :], 1.0), writes=["KX0"])
        S.op("dve", lambda e: e.tensor_copy(out=KXs[:, :, 1, :], in_=negc[:]), reads=[kc], writes=["KX1"])
        S.op("dve", lambda e: e.tensor_tensor(out=r1[:], in0=negc[:], in1=KXs[:, :, 1, :], op=ALU.subtract),
             reads=[kc, "KX1"], writes=["r1"])
        S.op("dve", lambda e: e.tensor_copy(out=KXs[:, :, 2, :], in_=r1[:]), reads=["r1"], writes=["KX2"])
        S.op("dve", lambda e: e.tensor_tensor(out=r1[:], in0=r1[:], in1=KXs[:, :, 2, :], op=ALU.subtract),
             reads=["r1", "KX2"], writes=["r1"])
        S.op("dve", lambda e: e.tensor_copy(out=KXs[:, :, 3, :], in_=r1[:]), reads=["r1"], writes=["KX3"])
        dma(S, "sp", T["KX"].rearrange("h r (j t) -> j h r t", t=128), KXs[:], ["KX0", "KX1", "KX2", "KX3"],
            ["KXd"], "cu1")
        KD = ph.sb("KD", [128, 4, 128], BF16)
        S.op("pool", lambda e: e.memset(KD[:], 0.0), writes=["KD"])
        S.op("dve", lambda e: e.tensor_copy(out=KD[:, 0, :], in_=pb[:]), reads=["pb", "KD"], writes=["KD"])
        dma(S, "sp", T["KXD"].rearrange("r (j t) -> j r t", t=128), KD[:], ["KD"], ["KXDd"], "cu1")
        cb = ph.sb("cb", [128, 8, 128], BF16)
        S.op("dve", lambda e: e.tensor_copy(out=cb[:], in_=cum[:]), reads=[kn], writes=["cb"])
        for s in range(4):
            j0 = 32 * s + NPADB
            dma(S, "sp", T["QX"][:, s * 512:(s + 1) * 512].rearrange("h (i t) -> i h t", t=128),
                cb[j0:j0 + 4, :, :], ["cb"], ["QXd"], "cu1")


def phase_attn(c, T, l, lam_init):
    S = c.S
    with c.phase() as ph:
        kt = [ph.sb("kt%d" % i, [68, S_ALL], BF16) for i in range(2)]
        vt = [ph.sb("vt%d" % i, [128, 128, 128], BF16) for i in range(2)]
        qf = [ph.sb("qf%d" % i, [68, TL], BF16) for i in range(2)]
        pt = [ph.sb("pt%d" % i, [128, 1024], BF16) for i in range(4)]
        mk = ph.sb("mk", [128, 6, 512], F32)
        yst = [ph.sb("yst%d" % i, [128, TL], BF16) for i in range(2)]
        onesb = ph.sb("onesb", [128, 128], BF16)
        onesf = ph.sb("onesf", [128, 128], F32)
        rr = ph.sb("rr", [128, 512], F32)
        bcs = ph.sb("bcs", [128, 512], F32)
        a1 = ph.sb("a1", [128, 512], F32)
        a2 = ph.sb("a2", [128, 512], F32)
        sq = ph.sb("sq", [128, 512], F32)
        cols = ph.sb("cols", [128, 16], F32)
        lmt = ph.sb("lmt", [128, 4, 64], F32)
        lmj = ph.sb("lmj", [128, 64], F32)
        s2 = [ph.ps("s2%d" % i, [128, 1024]) for i in range(2)]
        p1 = [ph.ps("p1%d" % i, [128, 512]) for i in range(4)]
        S.op("pool", lambda e: e.memset(onesb[:], 1.0), writes=["onesb"])
        S.op("pool", lambda e: e.memset(onesf[:], 1.0), writes=["onesf"])
        for i in range(2):
            S.op("pool", lambda e, i=i: e.memset(vt[i][:, :, 64:65], 1.0), writes=["vt%d" % i])
            S.op("pool", lambda e, i=i: e.memset(qf[i][64:68, :], 1.0), writes=["qf%d" % i])
        for i, nm in enumerate(("lam_q1", "lam_k1", "lam_q2", "lam_k2")):
            dma(S, "sp", lmt[:, i, :], bcast_rows(T[nm][l], 64), [], ["lmt"], "at_c")
        dma(S, "sp", cols[:, 8:12], bcast_rows(T["rel_bias"][31], 4), [], ["cols_b"], "at_c")
        dma(S, "sp", cols[:, 4:5], T["subln_g"][l].rearrange("(p o) -> p o", o=1), [], ["cols_g"], "at_c")
        for i in range(2):
            S.op("dve", lambda e, i=i: e.tensor_tensor(out=lmj[:], in0=lmt[:, 2 * i, :], in1=lmt[:, 2 * i + 1, :],
                                                       op=ALU.mult), reads=["lmt"], writes=["lmj"])
            S.op("dve", lambda e, i=i: e.tensor_reduce(out=cols[:, i:i + 1], in_=lmj[:], axis=AX.X, op=ALU.add),
                 reads=["lmj"], writes=["cols_l"])
        S.op("act", lambda e: e.activation(out=cols[:, 0:2], in_=cols[:, 0:2], func=AF.Exp),
             reads=["cols_l"], writes=["cols_l"])
        S.op("dve", lambda e: e.tensor_tensor(out=cols[:, 2:3], in0=cols[:, 1:2], in1=cols[:, 0:1], op=ALU.subtract),
             reads=["cols_l"], writes=["cols_l"])
        S.op("dve", lambda e: e.tensor_scalar(out=cols[:, 3:4], in0=cols[:, 2:3], scalar1=-float(lam_init),
                                              scalar2=None, op0=ALU.add), reads=["cols_l"], writes=["cols_n"])
        S.op("dve", lambda e: e.tensor_scalar(out=cols[:, 5:6], in0=cols[:, 4:5], scalar1=float(1.0 - lam_init),
                                              scalar2=None, op0=ALU.mult), reads=["cols_g"], writes=["cols_g2"])
        S.op("dve", lambda e: e.memset(cols[:, 6:7], 1e-5), writes=["cols_e"])

        oi = [0]
        pti = [0]

        def load_k(buf, m, fox_h):
            dma(S, "sp", kt[buf][0:64, :], T["kL"][m], [], ["kt%d" % buf], "at_k%d" % buf)
            if fox_h is not None:
                dma(S, "sp", kt[buf][64:68, :], T["KX"][fox_h], [], ["kt%d" % buf], "at_k%d" % buf)
            else:
                dma(S, "sp", kt[buf][64:68, :], T["KXD"], [], ["kt%d" % buf], "at_k%d" % buf)

        def load_q(buf, m, fox_h):
            dma(S, "sp", qf[buf][0:64, :], T["qT"][m], [], ["qf%d" % buf], "at_q%d" % buf)
            if fox_h is not None:
                dma(S, "sp", qf[buf][64:65, :], T["QX"][fox_h:fox_h + 1, :], [], ["qf%d" % buf], "at_q%d" % buf)

        dma(S, "sp", mk[:, 1:6, :], T["maskF"].rearrange("m k q -> k m q"), [], ["mk"], "at_m")
        for hh in range(8):
            b = hh % 2
            load_k(b, hh, hh)
            load_q(b, hh, hh)
            dma(S, "sp", vt[b][:, :, 0:64], T["vfL"][:, hh * 64:(hh + 1) * 64].rearrange("(j p) d -> p j d", p=128),
                [], ["vt%d" % b], "at_v%d" % b)
            yb = hh % 2
            for s in range(4):
                NB = 32 * s + 32
                o = p1[oi[0] % 2]
                ok = "p1%d" % (oi[0] % 2)
                oi[0] += 1
                for pi in range(NB // 2):
                    sp_ = s2[pi % 2]
                    sk = "s2%d" % (pi % 2)
                    for hf in range(2):
                        blk = 2 * pi + hf
                        S.op("pe", lambda e, blk=blk, hf=hf, sp_=sp_, b=b, s=s: e.matmul(
                            sp_[:, hf * 512:(hf + 1) * 512], lhsT=kt[b][:, blk * 128:(blk + 1) * 128],
                            rhs=qf[b][:, s * 512:(s + 1) * 512], start=True, stop=True),
                            reads=["kt%d" % b, "qf%d" % b], writes=[sk])
                        mi = blk - (NB - 6)
                        if mi >= 1:
                            S.op("dve", lambda e, hf=hf, sp_=sp_, mi=mi: e.tensor_tensor(
                                out=sp_[:, hf * 512:(hf + 1) * 512], in0=sp_[:, hf * 512:(hf + 1) * 512],
                                in1=mk[:, mi, :], op=ALU.add), reads=[sk, "mk"], writes=[sk])
                    pb_ = pti[0] % 4
                    pti[0] += 1
                    S.op("act", lambda e, sp_=sp_, pb_=pb_: e.activation(out=pt[pb_][:], in_=sp_[:], func=AF.Exp),
                         reads=[sk], writes=["pt%d" % pb_])
                    for hf in range(2):
                        blk = 2 * pi + hf
                        S.op("pe", lambda e, blk=blk, hf=hf, pb_=pb_, o=o, b=b, NB=NB: e.matmul(
                            o[0:65, :], lhsT=vt[b][:, blk, 0:65], rhs=pt[pb_][:, hf * 512:(hf + 1) * 512],
                            start=(blk == 0), stop=(blk == NB - 1)),
                            reads=["vt%d" % b, "pt%d" % pb_], writes=[ok])
                S.op("dve", lambda e, o=o: e.reciprocal(out=rr[64:65, :], in_=o[64:65, :]), reads=[ok], writes=["rr"])
                bc = p1[2]
                S.op("pe", lambda e, bc=bc: e.matmul(bc[0:64, :], lhsT=onesf[64:65, 0:64], rhs=rr[64:65, :],
                                                     start=True, stop=True), reads=["onesf", "rr"], writes=["p12"])
                S.op("act", lambda e, bc=bc: e.copy(out=bcs[0:64, :], in_=bc[0:64, :]), reads=["p12"], writes=["bcs"])
                S.op("dve", lambda e, o=o, s=s, yb=yb: e.tensor_tensor(
                    out=yst[yb][0:64, s * 512:(s + 1) * 512], in0=o[0:64, :], in1=bcs[0:64, :], op=ALU.mult),
                    reads=[ok, "bcs"], writes=["yst%d" % yb])
            dma(S, "sp", T["yF"][hh], yst[yb][0:64, :], ["yst%d" % yb], ["yFd"], "at_y%d" % yb)

        for h in range(4):
            dma(S, "sp", mk[:], T["BT"][h].rearrange("m k q -> k m q"), [], ["mk"], "at_m")
            for i in range(2):
                load_k(i, 8 + 2 * h + i, None)
                load_q(i, 8 + 2 * h + i, None)
                if h == 0:
                    S.op("pool", lambda e, i=i: e.memset(qf[i][64:68, :], 1.0), reads=[], writes=["qf%d" % i])
            vb = h % 2
            dma(S, "sp", vt[vb][:], T["vdL"][:, h * 128:(h + 1) * 128].rearrange("(j p) d -> p j d", p=128),
                [], ["vt%d" % vb], "at_v%d" % vb)
            yb = h % 2
            for s in range(4):
                NB = 32 * s + 32
                for pi in range(NB // 2):
                    near = (2 * pi + 1) >= NB - 6
                    pbs = []
                    for i in range(2):
                        sp_, sk = s2[i], "s2%d" % i
                        for hf in range(2):
                            blk = 2 * pi + hf
                            S.op("pe", lambda e, blk=blk, hf=hf, sp_=sp_, i=i, s=s: e.matmul(
                                sp_[:, hf * 512:(hf + 1) * 512], lhsT=kt[i][:, blk * 128:(blk + 1) * 128],
                                rhs=qf[i][:, s * 512:(s + 1) * 512], start=True, stop=True),
                                reads=["kt%d" % i, "qf%d" % i], writes=[sk])
                            mi = blk - (NB - 6)
                            if near:
                                S.op("dve", lambda e, hf=hf, sp_=sp_, mi=mi: e.tensor_tensor(
                                    out=sp_[:, hf * 512:(hf + 1) * 512], in0=sp_[:, hf * 512:(hf + 1) * 512],
                                    in1=mk[:, mi, :], op=ALU.add), reads=[sk, "mk"], writes=[sk])
                        pb_ = pti[0] % 4
                        pti[0] += 1
                        pbs.append(pb_)
                        if near:
                            S.op("act", lambda e, sp_=sp_, pb_=pb_: e.activation(out=pt[pb_][:], in_=sp_[:], func=AF.Exp),
                                 reads=[sk], writes=["pt%d" % pb_])
                        else:
                            S.op("act", lambda e, sp_=sp_, pb_=pb_, h=h: e.activation(
                                out=pt[pb_][:], in_=sp_[:], func=AF.Exp, bias=cols[:, 8 + h:9 + h]),
                                reads=[sk, "cols_b"], writes=["pt%d" % pb_])
                    for i in range(2):
                        pb_ = pbs[i]
                        for hf in range(2):
                            blk = 2 * pi + hf
                            S.op("pe", lambda e, blk=blk, hf=hf, pb_=pb_, i=i, vb=vb, NB=NB: e.matmul(
                                p1[i][:], lhsT=vt[vb][:, blk, :], rhs=pt[pb_][:, hf * 512:(hf + 1) * 512],
                                start=(blk == 0), stop=(blk == NB - 1)),
                                reads=["vt%d" % vb, "pt%d" % pb_], writes=["p1%d" % i])
                            S.op("pe", lambda e, blk=blk, hf=hf, pb_=pb_, i=i, NB=NB: e.matmul(
                                p1[2 + i][:], lhsT=onesb[:], rhs=pt[pb_][:, hf * 512:(hf + 1) * 512],
                                start=(blk == 0), stop=(blk == NB - 1)),
                                reads=["onesb", "pt%d" % pb_], writes=["p1%d" % (2 + i)])
                S.op("dve", lambda e: e.reciprocal(out=rr[:], in_=p1[2][:]), reads=["p12"], writes=["rr"])
                S.op("dve", lambda e: e.tensor_tensor(out=a1[:], in0=p1[0][:], in1=rr[:], op=ALU.mult),
                     reads=["p10", "rr"], writes=["a1"])
                S.op("dve", lambda e: e.reciprocal(out=bcs[:], in_=p1[3][:]), reads=["p13"], writes=["bcs"])
                S.op("dve", lambda e: e.tensor_tensor(out=a2[:], in0=p1[1][:], in1=bcs[:], op=ALU.mult),
                     reads=["p11", "bcs"], writes=["a2"])
                S.op("dve", lambda e: e.scalar_tensor_tensor(out=a1[:], in0=a2[:], scalar=cols[:, 3:4], in1=a1[:],
                                                             op0=ALU.mult, op1=ALU.add),
                     reads=["a1", "a2", "cols_n"], writes=["a1"])
                S.op("act", lambda e: e.activation(out=sq[:], in_=a1[:], func=AF.Square), reads=["a1"], writes=["sq"])
                S.op("pe", lambda e: e.matmul(s2[0][:, 0:512], lhsT=onesf[:], rhs=sq[:], start=True, stop=True),
                     reads=["onesf", "sq"], writes=["s20"])
                S.op("act", lambda e: e.activation(out=a2[:], in_=s2[0][:, 0:512], func=AF.Sqrt, bias=cols[:, 6:7],
                                                   scale=1.0 / 128.0), reads=["s20", "cols_e"], writes=["a2"])
                S.op("dve", lambda e: e.reciprocal(out=a2[:], in_=a2[:]), reads=["a2"], writes=["a2"])
                S.op("dve", lambda e, s=s, yb=yb: e.scalar_tensor_tensor(
                    out=yst[yb][:, s * 512:(s + 1) * 512], in0=a1[:], scalar=cols[:, 5:6], in1=a2[:],
                    op0=ALU.mult, op1=ALU.mult), reads=["a1", "a2", "cols_g2"], writes=["yst%d" % yb])
            dma(S, "sp", T["yD"][h], yst[yb][:], ["yst%d" % yb], ["yDd"], "at_y%d" % yb)


def phase_post(c, T, l, want_t32):
    S = c.S
    with c.phase() as ph:
        ln = LNUnit(c, ph, T["ln_mix_g"][l], T["ln_mix_b"][l], T["ident"], want_t32=want_t32, tag="pm")
        ln.setup_eps()
        wbf = ph.sb("wbf", [64, 8, D], BF16)
        wbd = ph.sb("wbd", [128, 4, D], BF16)
        wo = ph.sb("wo", [128, 8, D], BF16)
        dma(S, "pool", wbf[:], T["w_branch_fox"][l].rearrange("(h d) n -> d h n", d=64), [], ["wbf"], "po_w")
        dma(S, "pool", wbd[:], T["w_branch_diff"][l].rearrange("(h d) n -> d h n", d=128), [], ["wbd"], "po_w")
        dma(S, "pool", wo[:], T["w_out"][l].rearrange("(k p) n -> p k n", p=128), [], ["wo"], "po_w")
        yF = ph.sb("yF", [64, 8, 512], BF16)
        yD = ph.sb("yD", [128, 4, 512], BF16)
        sga = ph.sb("sga", [128, 8, 512], BF16)
        sgb = ph.sb("sgb", [128, 8, 512], BF16)
        mg = ph.sb("mg", [128, 8, 512], BF16)
        t1 = ph.sb("t1", [128, 512], F32)
        t2 = ph.sb("t2", [128, 512], F32)
        hb = [ph.sb("hb%d" % i, [128, D], F32) for i in range(2)]
        pa = ph.ps("pa", [128, 512])
        pb = ph.ps("pb", [128, 512])
        pm = ph.ps("pm", [128, 1024])
        for tt in range(4):
            tsl = slice(tt * 512, (tt + 1) * 512)
            dma(S, "sp", yF[:], T["yF"][:, :, tsl].rearrange("h d t -> d h t"), [], ["yF"], "po_a")
            dma(S, "sp", yD[:], T["yD"][:, :, tsl].rearrange("h d t -> d h t"), [], ["yD"], "po_a")
            dma(S, "sp", sga[:], T["sg"][0][:, tsl].rearrange("(k p) t -> p k t", p=128), [], ["sga"], "po_a")
            dma(S, "sp", sgb[:], T["sg"][1][:, tsl].rearrange("(k p) t -> p k t", p=128), [], ["sgb"], "po_a")
            for n in range(8):
                for h in range(8):
                    S.op("pe", lambda e, h=h, n=n: e.matmul(pa[:], lhsT=wbf[:, h, n * 128:(n + 1) * 128], rhs=yF[:, h, :],
                                                            start=(h == 0), stop=(h == 7)),
                         reads=["wbf", "yF"], writes=["pa"])
                for h in range(4):
                    S.op("pe", lambda e, h=h, n=n: e.matmul(pb[:], lhsT=wbd[:, h, n * 128:(n + 1) * 128], rhs=yD[:, h, :],
                                                            start=(h == 0), stop=(h == 3)),
                         reads=["wbd", "yD"], writes=["pb"])
                S.op("dve", lambda e, n=n: e.tensor_tensor(out=t1[:], in0=sga[:, n, :], in1=pa[:], op=ALU.mult),
                     reads=["sga", "pa"], writes=["t1"])
                S.op("dve", lambda e, n=n: e.tensor_tensor(out=t2[:], in0=sgb[:, n, :], in1=pb[:], op=ALU.mult),
                     reads=["sgb", "pb"], writes=["t2"])
                S.op("pool", lambda e, n=n: e.tensor_tensor(out=mg[:, n, :], in0=t1[:], in1=t2[:], op=ALU.add),
                     reads=["t1", "t2"], writes=["mg"])
            for tb in range(4):
                tok0 = tt * 512 + tb * 128
                i = tb % 2
                dma(S, "sp", hb[i][:], T["h"][tok0:tok0 + 128, :], [], ["hb%d" % i], "po_h%d" % i)
                for hf in range(2):
                    for k in range(8):
                        S.op("pe", lambda e, k=k, hf=hf, tb=tb: e.matmul(
                            pm[:, hf * 512:(hf + 1) * 512], lhsT=mg[:, k, tb * 128:(tb + 1) * 128],
                            rhs=wo[:, k, hf * 512:(hf + 1) * 512], start=(k == 0), stop=(k == 7)),
                            reads=["mg", "wo"], writes=["pm"])
                S.op("dve", lambda e, i=i: e.scalar_tensor_tensor(out=hb[i][:], in0=hb[i][:], scalar=float(ALPHA),
                                                                  in1=pm[:], op0=ALU.mult, op1=ALU.add),
                     reads=["hb%d" % i, "pm"], writes=["hb%d" % i])
                ln.run(hb[i][:], "hb%d" % i, T["h"][tok0:tok0 + 128, :], T["hT"], tok0,
                       hT32_out=(T["hT32"] if want_t32 else None))


def ffn_core(c, ph, T, S, hTs, hkey, wgu, dff, wdown, nfc, fc0, aT, akey, wd, wdkey, stl, pg, pu, sgt, wgb, wcnt, tag):
    for f in range(nfc):
        fc = fc0 + f
        b = wcnt[0] % 3
        wcnt[0] += 1
        dma(S, "pool", wgb[b][:, 0, :, :], wgu[:, fc * 128:(fc + 1) * 128].rearrange("(k p) n -> p k n", p=128),
            [], [tag + "wg%d" % b], tag + "wg%d" % b)
        dma(S, "pool", wgb[b][:, 1, :, :], wgu[:, dff + fc * 128:dff + (fc + 1) * 128].rearrange("(k p) n -> p k n", p=128),
            [], [tag + "wg%d" % b], tag + "wg%d" % b)
        for t2 in range(2):
            for k in range(8):
                S.op("pe", lambda e, b=b, k=k, t2=t2: e.matmul(pg[:], lhsT=wgb[b][:, 0, k, :],
                                                               rhs=hTs[:, k, t2 * 512:(t2 + 1) * 512],
                                                               start=(k == 0), stop=(k == 7)),
                     reads=[hkey, tag + "wg%d" % b], writes=[tag + "pg"])
            for k in range(8):
                S.op("pe", lambda e, b=b, k=k, t2=t2: e.matmul(pu[:], lhsT=wgb[b][:, 1, k, :],
                                                               rhs=hTs[:, k, t2 * 512:(t2 + 1) * 512],
                                                               start=(k == 0), stop=(k == 7)),
                     reads=[hkey, tag + "wg%d" % b], writes=[tag + "pu"])
            S.op("act", lambda e: e.activation(out=sgt[:], in_=pg[:], func=AF.Silu), reads=[tag + "pg"], writes=[tag + "sgt"])
            S.op("dve", lambda e, f=f, t2=t2: e.tensor_tensor(out=aT[:, f, t2 * 512:(t2 + 1) * 512], in0=sgt[:], in1=pu[:],
                                                              op=ALU.mult),
                 reads=[tag + "sgt", tag + "pu"], writes=[akey])


def phase_ffn(c, T):
    S = c.S
    NF = DFF // 128
    with c.phase() as ph:
        ln = LNUnit(c, ph, T["ln_ffn_g"][0], T["ln_ffn_b"][0], T["ident"], tag="pf")
        ln.setup_eps()
        wd = ph.sb("wd", [128, NF, D], BF16)
        wdv = T["w_ffn_down"][0].rearrange("(f p) n -> p f n", p=128)
        for q in range(0, NF, 6):
            q1 = min(NF, q + 6)
            dma(S, "pool", wd[:, q:q1, :], wdv[:, q:q1, :], [], ["wd"], "ff_wd")
        hTs = ph.sb("hTs", [128, 8, 1024], BF16)
        aT = ph.sb("aT", [128, NF, 1024], BF16)
        wgb = [ph.sb("wgb%d" % i, [128, 2, 8, 128], BF16) for i in range(3)]
        sgt = ph.sb("sgt", [128, 512], F32)
        hb = [ph.sb("hb%d" % i, [128, D], F32) for i in range(2)]
        pg = ph.ps("pg", [128, 512])
        pu = ph.ps("pu", [128, 512])
        pm = ph.ps("pm", [128, 1024])
        wcnt = [0]
        for st in range(2):
            dma(S, "sp", hTs[:], T["hT"][:, st * 1024:(st + 1) * 1024].rearrange("(k p) t -> p k t", p=128),
                [], ["hTs"], "ff_h")
            ffn_core(c, ph, T, S, hTs, "hTs", T["w_ffn_gate_up"][0], DFF, None, NF, 0, aT, "aT", wd, "wd", st,
                     pg, pu, sgt, wgb, wcnt, "ff")
            for tb in range(8):
                tok0 = st * 1024 + tb * 128
                i = tb % 2
                dma(S, "sp", hb[i][:], T["h"][tok0:tok0 + 128, :], [], ["hb%d" % i], "ff_h%d" % i)
                for hf in range(2):
                    for f in range(NF):
                        S.op("pe", lambda e, f=f, hf=hf, tb=tb: e.matmul(
                            pm[:, hf * 512:(hf + 1) * 512], lhsT=aT[:, f, tb * 128:(tb + 1) * 128],
                            rhs=wd[:, f, hf * 512:(hf + 1) * 512], start=(f == 0), stop=(f == NF - 1)),
                            reads=["aT", "wd"], writes=["pm"])
                S.op("dve", lambda e, i=i: e.scalar_tensor_tensor(out=hb[i][:], in0=hb[i][:], scalar=float(ALPHA),
                                                                  in1=pm[:], op0=ALU.mult, op1=ALU.add),
                     reads=["hb%d" % i, "pm"], writes=["hb%d" % i])
                ln.run(hb[i][:], "hb%d" % i, T["h"][tok0:tok0 + 128, :], T["hT"], tok0)


def phase_moe(c, T):
    S = c.S
    NH = 14
    with c.phase() as ph:
        ln = LNUnit(c, ph, T["ln_ffn_g"][1], T["ln_ffn_b"][1], T["ident"], tag="pe", with_t=False)
        ln.setup_eps()
        hTs = ph.sb("hTs", [128, 8, 1024], BF16)
        aT = [ph.sb("aT%d" % i, [128, NH, 1024], BF16) for i in range(2)]
        wd = [ph.sb("wd%d" % i, [128, NH, D], BF16) for i in range(2)]
        wgb = [ph.sb("wgb%d" % i, [128, 2, 8, 128], BF16) for i in range(3)]
        acc = ph.sb("acc", [128, 8, D], F32)
        sgt = ph.sb("sgt", [128, 512], F32)
        h32 = [ph.sb("h32%d" % i, [128, 8, 128], F32) for i in range(2)]
        wr = ph.sb("wr", [128, 8, NE], F32)
        comb = ph.sb("comb", [128, 8, NE], F32)
        rt = ph.sb("rt", [128, 8, 8], F32)
        rs = ph.sb("rs", [128, 4], F32)
        pg = ph.ps("pg", [128, 512])
        pu = ph.ps("pu", [128, 512])
        pm = ph.ps("pm", [128, 1024])
        pr = ph.ps("pr", [128, 8])
        dma(S, "sp", wr[:], T["w_router"][0].rearrange("(k p) e -> p k e", p=128), [], ["wr"], "mo_c")
        wcnt = [0]
        hcnt = [0]
        for st in range(2):
            dma(S, "sp", hTs[:], T["hT"][:, st * 1024:(st + 1) * 1024].rearrange("(k p) t -> p k t", p=128),
                [], ["hTs"], "mo_h")
            for tb in range(8):
                tok0 = st * 1024 + tb * 128
                i = tb % 2
                dma(S, "sp", h32[i][:], T["hT32"].rearrange("(k p) t -> p k t", p=128)[:, :, tok0:tok0 + 128],
                    [], ["h32%d" % i], "mo_r%d" % i)
                for k in range(8):
                    S.op("pe", lambda e, k=k, i=i: e.matmul(pr[:], lhsT=h32[i][:, k, :], rhs=wr[:, k, :],
                                                            start=(k == 0), stop=(k == 7)),
                         reads=["h32%d" % i, "wr"], writes=["pr"])
                S.op("dve", lambda e: e.tensor_copy(out=rt[:, 0, :], in_=pr[:]), reads=["pr"], writes=["rt0"])
                S.op("dve", lambda e: e.max(out=rt[:, 1, :], in_=rt[:, 0, :]), reads=["rt0"], writes=["rt1"])
                S.op("dve", lambda e: e.tensor_scalar(out=rt[:, 2, :], in0=rt[:, 0, :], scalar1=rt[:, 1, 1:2], scalar2=None,
                                                      op0=ALU.is_ge), reads=["rt0", "rt1"], writes=["rt2"])
                S.op("dve", lambda e: e.tensor_scalar(out=rs[:, 0:1], in0=rt[:, 1, 0:1], scalar1=-1.0, scalar2=None,
                                                      op0=ALU.mult), reads=["rt1"], writes=["rs0"])
                S.op("act", lambda e: e.activation(out=rt[:, 3, :], in_=rt[:, 0, :], func=AF.Exp, bias=rs[:, 0:1]),
                     reads=["rt0", "rs0"], writes=["rt3"])
                S.op("dve", lambda e: e.tensor_tensor(out=rt[:, 4, :], in0=rt[:, 2, :], in1=rt[:, 3, :], op=ALU.mult),
                     reads=["rt2", "rt3"], writes=["rt4"])
                S.op("dve", lambda e: e.tensor_reduce(out=rs[:, 1:2], in_=rt[:, 4, :], axis=AX.X, op=ALU.add),
                     reads=["rt4"], writes=["rs1"])
                S.op("dve", lambda e: e.reciprocal(out=rs[:, 2:3], in_=rs[:, 1:2]), reads=["rs1"], writes=["rs2"])
                S.op("dve", lambda e, tb=tb: e.tensor_scalar(out=comb[:, tb, :], in0=rt[:, 4, :], scalar1=rs[:, 2:3],
                                                             scalar2=None, op0=ALU.mult),
                     reads=["rt4", "rs2"], writes=["comb"])
            for ex in range(NE):
                for hf in range(2):
                    hb_ = hcnt[0] % 2
                    hcnt[0] += 1
                    wdv = T["w_expert_down"][0][ex].rearrange("(f p) n -> p f n", p=128)
                    for q in range(0, NH, 7):
                        dma(S, "pool", wd[hb_][:, q:q + 7, :], wdv[:, hf * NH + q:hf * NH + q + 7, :], [],
                            ["wd%d" % hb_], "mo_wd%d" % hb_)
                    ffn_core(c, ph, T, S, hTs, "hTs", T["w_expert_gate_up"][0][ex], DFE, None, NH, hf * NH, aT[hb_],
                             "aT%d" % hb_, None, None, st, pg, pu, sgt, wgb, wcnt, "mo")
                    for tb in range(8):
                        for h2 in range(2):
                            for f in range(NH):
                                S.op("pe", lambda e, f=f, h2=h2, tb=tb, hb_=hb_: e.matmul(
                                    pm[:, h2 * 512:(h2 + 1) * 512], lhsT=aT[hb_][:, f, tb * 128:(tb + 1) * 128],
                                    rhs=wd[hb_][:, f, h2 * 512:(h2 + 1) * 512], start=(f == 0), stop=(f == NH - 1)),
                                    reads=["aT%d" % hb_, "wd%d" % hb_], writes=["pm"])
                        if ex == 0 and hf == 0:
                            S.op("dve", lambda e, tb=tb, ex=ex: e.tensor_scalar(
                                out=acc[:, tb, :], in0=pm[:], scalar1=comb[:, tb, ex:ex + 1], scalar2=None, op0=ALU.mult),
                                reads=["pm", "comb"], writes=[("acc", tb)])
                        else:
                            S.op("dve", lambda e, tb=tb, ex=ex: e.scalar_tensor_tensor(
                                out=acc[:, tb, :], in0=pm[:], scalar=comb[:, tb, ex:ex + 1], in1=acc[:, tb, :],
                                op0=ALU.mult, op1=ALU.add), reads=["pm", "comb", ("acc", tb)], writes=[("acc", tb)])
            for tb in range(8):
                tok0 = st * 1024 + tb * 128
                i = tb % 2
                dma(S, "sp", h32[i][:].rearrange("p a b -> p (a b)"), T["h"][tok0:tok0 + 128, :], [], ["h32%d" % i],
                    "mo_r%d" % i)
                S.op("dve", lambda e, i=i, tb=tb: e.scalar_tensor_tensor(
                    out=acc[:, tb, :], in0=h32[i][:].rearrange("p a b -> p (a b)"), scalar=float(ALPHA),
                    in1=acc[:, tb, :], op0=ALU.mult, op1=ALU.add),
                    reads=["h32%d" % i, ("acc", tb)], writes=[("acc", tb)])
                ln.run(acc[:, tb, :], ("acc", tb), T["out"][tok0:tok0 + 128, :], None, tok0, do_t=False)


W_SPECS = {
    "ln_in_g": ([D], F32), "ln_in_b": ([D], F32), "w_in": ([2, D, NIN], F32), "b_fgate": ([2, 8], F32),
    "lam_q1": ([2, 64], F32), "lam_k1": ([2, 64], F32), "lam_q2": ([2, 64], F32), "lam_k2": ([2, 64], F32),
    "subln_g": ([2, 128], F32), "w_branch_fox": ([2, 512, D], F32), "w_branch_diff": ([2, 512, D], F32),
    "w_out": ([2, D, D], F32), "ln_mix_g": ([2, D], F32), "ln_mix_b": ([2, D], F32), "rel_bias": ([32, 4], F32),
    "w_ffn_gate_up": ([1, D, 2 * DFF], F32), "w_ffn_down": ([1, DFF, D], F32), "w_router": ([1, D, NE], F32),
    "w_expert_gate_up": ([1, NE, D, 2 * DFE], F32), "w_expert_down": ([1, NE, DFE, D], F32),
    "ln_ffn_g": ([2, D], F32), "ln_ffn_b": ([2, D], F32),
}
A_SPECS = {
    "x": ([TL, D], F32), "h": ([TL, D], F32), "hT": ([D, TL], BF16), "hT32": ([D, TL], F32),
    "qT": ([16, 64, TL], BF16), "kT": ([16, 64, TL], BF16), "vf": ([TL, 512], BF16), "vd": ([TL, 512], BF16),
    "lf": ([TL, 8], F32), "sg": ([2, D, TL], BF16),
    "kL": ([16, 64, S_ALL], BF16), "vfL": ([S_ALL, 512], BF16), "vdL": ([S_ALL, 512], BF16),
    "lfL": ([S_ALL, 8], F32), "padb": ([S_ALL], F32), "su": ([128, 128], F32), "ident": ([128, 128], BF16),
    "maskF": ([5, 128, 512], F32), "BT": ([4, 6, 128, 512], F32),
    "KX": ([8, 4, S_ALL], BF16), "KXD": ([4, S_ALL], BF16), "QX": ([8, TL], BF16),
    "yF": ([8, 64, TL], BF16), "yD": ([4, 128, TL], BF16), "out": ([TL, D], F32),
}
LAUNCH = {
    1: dict(ins=["x", "ident", "ln_in_g", "ln_in_b", "w_in", "b_fgate"],
            outs=["h", "hT", "qT", "kT", "vf", "vd", "lf", "sg"], internal=[]),
    2: dict(ins=["h_in", "qT", "sg", "kL", "vfL", "vdL", "lfL", "padb", "su", "ident", "maskF", "BT",
                 "lam_q1", "lam_k1", "lam_q2", "lam_k2", "subln_g", "rel_bias", "w_branch_fox", "w_branch_diff",
                 "w_out", "ln_mix_g", "ln_mix_b", "w_ffn_gate_up", "w_ffn_down", "ln_ffn_g", "ln_ffn_b",
                 "w_in", "b_fgate"],
            outs=["h_o", "qT_o", "kT_o", "vf_o", "vd_o", "lf_o", "sg_o"],
            internal=["h", "hT", "KX", "KXD", "QX", "yF", "yD"]),
    3: dict(ins=["h_in", "qT", "sg", "kL", "vfL", "vdL", "lfL", "padb", "su", "ident", "maskF", "BT",
                 "lam_q1", "lam_k1", "lam_q2", "lam_k2", "subln_g", "rel_bias", "w_branch_fox", "w_branch_diff",
                 "w_out", "ln_mix_g", "ln_mix_b", "w_router", "w_expert_gate_up", "w_expert_down",
                 "ln_ffn_g", "ln_ffn_b"],
            outs=["out"], internal=["h", "hT", "hT32", "KX", "KXD", "QX", "yF", "yD"]),
}


def _spec(name):
    base = name[:-2] if name.endswith("_o") else ("h" if name == "h_in" else name)
    return W_SPECS[base] if base in W_SPECS else A_SPECS[base]


def build_launch(lid):
    nc = bass.Bass("TRN2", target_bir_lowering=False)
    L = LAUNCH[lid]
    T = {}
    for n in L["ins"]:
        sh, dt = _spec(n)
        T[n] = nc.dram_tensor(n, sh, dt, kind="ExternalInput").ap()
    for n in L["outs"]:
        sh, dt = _spec(n)
        T[n] = nc.dram_tensor(n, sh, dt, kind="ExternalOutput").ap()
    for n in L["internal"]:
        sh, dt = _spec(n)
        T[n] = nc.dram_tensor(n, sh, dt, kind="Internal").ap()
    with ExitStack() as st:
        S = Sched(nc, st)
        c = Ctx(nc, S)
        if lid == 1:
            phase_ln0(c, T)
            phase_inproj(c, T, 0)
        else:
            l = lid - 2
            lam_init = 0.8 - 0.6 * math.exp(-0.3 * l)
            with c.phase() as ph:
                dma(S, "sp", T["h"], T["h_in"], [], ["hcp"], "cp0")
            phase_cum(c, T)
            phase_attn(c, T, l, lam_init)
            phase_post(c, T, l, want_t32=(l == 1))
            if l == 0:
                phase_ffn(c, T)
                T2 = dict(T)
                for n in ("qT", "kT", "vf", "vd", "lf", "sg"):
                    T2[n] = T[n + "_o"]
                phase_inproj(c, T2, 1)
                with c.phase() as ph:
                    dma(S, "sp", T["h_o"], T["h"], [], ["hcp2"], "cp1")
            else:
                phase_moe(c, T)
        S.flush(final=True)
    return nc


_PROGS = {}


def _prog(lid):
    if lid not in _PROGS:
        _PROGS[lid] = build_launch(lid)
    return _PROGS[lid]


def _t5_bucket(n):
    n = np.maximum(n, 0)
    nf = np.maximum(n, 1).astype(np.float32)
    lp = (np.log(nf / np.float32(16)) / np.float32(math.log(8.0)) * np.float32(16)).astype(np.float32)
    large = np.minimum(16 + lp.astype(np.int32), 31)
    return np.where(n < 16, n, large)


def _static_masks(rel_bias):
    k = np.arange(128)[:, None]
    q = np.arange(512)[None, :]
    maskF = np.zeros((5, 128, 512), np.float32)
    BT = np.zeros((4, 6, 128, 512), np.float32)
    for mi in range(6):
        dist = q - k - (mi - 2) * 128
        ok = dist >= 0
        idx = _t5_bucket(dist)
        for h in range(4):
            g = rel_bias[:, h][idx]
            BT[h, mi] = np.where(ok, g, np.float32(NEG))
        if mi >= 1:
            maskF[mi - 1] = np.where(ok, np.float32(0), np.float32(NEG))
    return maskF, BT


def _gather_tokens(parts, axis):
    shp = list(parts[0].shape)
    shp[axis] = S_ALL
    out = np.zeros(shp, parts[0].dtype)
    for c in range(NCORE):
        for s in range(4):
            T_ = 8 * s + c
            src = [slice(None)] * len(shp)
            dst = [slice(None)] * len(shp)
            src[axis] = slice(s * 512, (s + 1) * 512)
            dst[axis] = slice(T_ * 512, (T_ + 1) * 512)
            out[tuple(dst)] = parts[c][tuple(src)]
    return out


def _local_view(glob, axis, c):
    shp = list(glob.shape)
    shp[axis] = NPADB * 128
    padded = np.concatenate([np.zeros(shp, glob.dtype), glob], axis=axis)
    sl = [slice(None)] * len(shp)
    sl[axis] = slice(4 * c * 128, 4 * c * 128 + S_ALL)
    return np.ascontiguousarray(padded[tuple(sl)])


def _run(lid, in_maps):
    nc = _prog(lid)
    res = run_bass_kernel_spmd(nc, in_maps, core_ids=list(range(NCORE)))
    return res.results


def kernel(**inp):
    inp = {k: np.ascontiguousarray(np.asarray(v)) for k, v in inp.items()}
    x = inp["x"].reshape(S_ALL, D)
    ident = np.eye(128, dtype=np.float32).astype(ml_dtypes.bfloat16)
    su = np.triu(np.ones((128, 128), np.float32), 1)
    maskF, BT = _static_masks(inp["rel_bias"].astype(np.float32))
    xs = []
    for c in range(NCORE):
        xs.append(np.concatenate([x[(8 * s + c) * 512:(8 * s + c + 1) * 512] for s in range(4)], axis=0))
    wl = lambda names: {n: inp[n] for n in names if n in W_SPECS}
    L = LAUNCH[1]
    maps = [dict(wl(L["ins"]), x=xs[c], ident=ident) for c in range(NCORE)]
    r = _run(1, maps)
    out = None
    for lid in (2, 3):
        kg = _gather_tokens([r[c]["kT"] if lid == 2 else r[c]["kT_o"] for c in range(NCORE)], 2)
        sfx = "" if lid == 2 else "_o"
        vfg = _gather_tokens([r[c]["vf" + sfx] for c in range(NCORE)], 0)
        vdg = _gather_tokens([r[c]["vd" + sfx] for c in range(NCORE)], 0)
        lfg = _gather_tokens([r[c]["lf" + sfx] for c in range(NCORE)], 0)
        L = LAUNCH[lid]
        maps = []
        for c in range(NCORE):
            m = dict(wl(L["ins"]))
            m["h_in"] = r[c]["h" if lid == 2 else "h_o"]
            m["qT"] = r[c]["qT" + sfx]
            m["sg"] = r[c]["sg" + sfx]
            m["kL"] = _local_view(kg, 2, c)
            m["vfL"] = _local_view(vfg, 0, c)
            m["vdL"] = _local_view(vdg, 0, c)
            m["lfL"] = _local_view(lfg, 0, c)
            pb = np.zeros((S_ALL,), np.float32)
            pb[:max(0, (NPADB - 4 * c)) * 128] = NEG
            m["padb"] = pb
            m["su"], m["ident"], m["maskF"], m["BT"] = su, ident, maskF, BT
            maps.append(m)
        r = _run(lid, maps)
    full = np.zeros((S_ALL, D), np.float32)
    for c in range(NCORE):
        o = np.asarray(r[c]["out"], np.float32)
        for s in range(4):
            T_ = 8 * s + c
            full[T_ * 512:(T_ + 1) * 512] = o[s * 512:(s + 1) * 512]
    return full.reshape(1, S_ALL, D)
```

```python
import math
from contextlib import ExitStack
import numpy as np
import ml_dtypes
import concourse.bass as bass
import concourse.mybir as mybir
from concourse.bass_utils import run_bass_kernel_spmd

F32 = mybir.dt.float32
BF16 = mybir.dt.bfloat16
AF = mybir.ActivationFunctionType
ALU = mybir.AluOpType
AX = mybir.AxisListType

ENGS = ("pe", "act", "dve", "pool", "sp")


class Sched:
    def __init__(self, nc, stack):
        self.nc = nc
        self.stack = stack
        self.sem = {e: stack.enter_context(nc.semaphore("s_" + e)) for e in ENGS}
        self.nsig = {e: 0 for e in ENGS}
        self.gcount = {e: 0 for e in ENGS}
        self.ops = {e: [] for e in ENGS}
        self.recs = {}
        self.ordinal = {}
        self.res = {}
        self.ch = {}
        self._pid = {}

    def pid(self, engine):
        k = id(engine)
        if k not in self._pid:
            self._pid[k] = engine.snap(engine.partition_id())
        return self._pid[k]

    def chan(self, name, unit=16):
        if name not in self.ch:
            self.ch[name] = [self.stack.enter_context(self.nc.semaphore("c_" + name)), 0, unit]
        return self.ch[name]

    def _state(self, key):
        st = self.res.get(key)
        if st is None:
            st = {"lw": None, "rd": []}
            self.res[key] = st
        return st

    def op(self, eng, fn, reads=(), writes=(), dma_ch=None, unit=16):
        waits = []
        seen = set()

        def add(tok, raw):
            if tok is None or tok in seen:
                return
            seen.add(tok)
            if tok[0] == "eng" and tok[1] == eng and dma_ch is None:
                if not raw or eng == "pe":
                    return
            waits.append(tok)

        for k in reads:
            add(self._state(k)["lw"], True)
        for k in writes:
            st = self._state(k)
            add(st["lw"], False)
            for t in st["rd"]:
                add(t, False)
        gidx = self.gcount[eng]
        self.gcount[eng] += 1
        rec = {"fn": fn, "waits": waits, "sig": False, "dma": None, "g": gidx}
        if dma_ch is not None:
            c = self.chan(dma_ch, unit)
            c[1] += 1
            rec["dma"] = (dma_ch, c[1])
            me = ("dma", dma_ch, c[1])
        else:
            me = ("eng", eng, gidx)
            self.recs[(eng, gidx)] = rec
        self.ops[eng].append(rec)
        for k in reads:
            st = self._state(k)
            if me[0] == "eng":
                st["rd"] = [t for t in st["rd"] if not (t[0] == "eng" and t[1] == eng)]
            st["rd"].append(me)
        for k in writes:
            st = self._state(k)
            st["lw"] = me
            st["rd"] = []
        return me

    def flush(self, final=False):
        nc = self.nc
        for e in ENGS:
            for rec in self.ops[e]:
                for t in rec["waits"]:
                    if t[0] == "eng" and (t[1], t[2]) in self.recs:
                        self.recs[(t[1], t[2])]["sig"] = True
        last = {}
        for e in ENGS:
            for rec in self.ops[e]:
                if rec["dma"] is None:
                    last[e] = rec
            if e in last:
                last[e]["sig"] = True
        for e in ENGS:
            n = self.nsig[e]
            for rec in self.ops[e]:
                if rec["sig"]:
                    n += 1
                    self.ordinal[(e, rec["g"])] = n
            self.nsig[e] = n
        ops, sem, ch, ordinal = self.ops, self.sem, self.ch, self.ordinal
        self.ops = {e: [] for e in ENGS}
        self.recs = {}
        def collapse(t):
            if t is not None and t[0] == "eng" and t[1] in last:
                return ("eng", t[1], last[t[1]]["g"])
            return t
        for st in self.res.values():
            st["lw"] = collapse(st["lw"])
            st["rd"] = list({collapse(t) for t in st["rd"]})
        final_ch = [(c[0], c[2] * c[1]) for c in ch.values() if c[1] > 0] if final else []

        prev_bar = getattr(self, "bar", None)
        self.bar = ({e2: self.nsig[e2] for e2 in ENGS}, {k: c[2] * c[1] for k, c in ch.items()})

        def run(e, engine):
            waited = {}
            if prev_bar is not None:
                for e2, v in prev_bar[0].items():
                    if e2 != e and v > 0:
                        engine.wait_ge(sem[e2], v)
                        waited["e" + e2] = v
                for k, v in prev_bar[1].items():
                    if v > 0:
                        engine.wait_ge(ch[k][0], v)
                        waited["c" + k] = v
            for rec in ops[e]:
                for t in rec["waits"]:
                    if t[0] == "eng":
                        s, v, key = sem[t[1]], ordinal[(t[1], t[2])], "e" + t[1]
                    else:
                        s, v, key = ch[t[1]][0], ch[t[1]][2] * t[2], "c" + t[1]
                    if waited.get(key, 0) >= v:
                        continue
                    waited[key] = v
                    engine.wait_ge(s, v)
                ins = rec["fn"](engine)
                if rec["dma"] is not None:
                    ins.then_inc(ch[rec["dma"][0]][0], ch[rec["dma"][0]][2])
                elif rec["sig"]:
                    ins.then_inc(sem[e], 1)
            if e == "sp":
                for s, v in final_ch:
                    engine.wait_ge(s, v)

        with nc.Block() as block:
            @block.tensor
            def _(pe):
                run("pe", pe)

            @block.scalar
            def _(act):
                run("act", act)

            @block.vector
            def _(dve):
                run("dve", dve)

            @block.gpsimd
            def _(pool):
                run("pool", pool)

            @block.sync
            def _(sp):
                run("sp", sp)

D = 1024
S_ALL = 16384
NCORE = 8
TL = 2048
NIN = 5128
DFF = 2816
DFE = 3584
NE = 8
LN_EPS = 1e-5
ALPHA = 4.0 ** 0.25
NEG = -30000.0
NPADB = 28
C_FQ, C_FK, C_FV, C_FG, C_DQ, C_DK, C_DV, C_GA, C_GB = 0, 512, 1024, 1536, 1544, 2056, 2568, 3080, 4104


class Ctx:
    def __init__(self, nc, S):
        self.nc = nc
        self.S = S
        self.uid = 0
        self.dram = {}

    def phase(self):
        return Phase(self)


class Phase:
    def __init__(self, c):
        self.c = c
        self.st = ExitStack()

    def __enter__(self):
        self.st.__enter__()
        return self

    def sb(self, name, shape, dt):
        self.c.uid += 1
        return self.st.enter_context(self.c.nc.sbuf_tensor("%s_%d" % (name, self.c.uid), shape, dt))

    def ps(self, name, shape, dt=F32):
        self.c.uid += 1
        return self.st.enter_context(self.c.nc.psum_tensor("%s_%d" % (name, self.c.uid), shape, dt))

    def __exit__(self, *a):
        if a[0] is None:
            self.c.S.flush()
        return self.st.__exit__(*a)


def dma(S, eng, out, in_, reads, writes, ch):
    def fn(e):
        o = out(e) if callable(out) else out
        i = in_(e) if callable(in_) else in_
        return e.dma_start(out=o, in_=i)
    return S.op(eng, fn, reads=reads, writes=writes, dma_ch=ch)


def bcast_rows(ap1d, n, parts=128):
    return ap1d.rearrange("(o n) -> o n", o=1).broadcast_to([parts, n])


class LNUnit:
    def __init__(self, c, ph, g_ap, b_ap, ident_ap, want_t32=False, tag="ln", with_t=True):
        S = c.S
        self.c, self.ph, self.tag = c, ph, tag
        self.gB = ph.sb("gB", [128, D], F32)
        self.bB = ph.sb("bB", [128, D], F32)
        self.ident = ph.sb("ident", [128, 128], BF16)
        dma(S, "sp", self.gB[:], bcast_rows(g_ap, D), [], [tag + "gB"], tag + "c0")
        dma(S, "sp", self.bB[:], bcast_rows(b_ap, D), [], [tag + "bB"], tag + "c0")
        dma(S, "pool", self.ident[:], ident_ap, [], [tag + "id"], tag + "c1")
        self.junk = ph.sb("junk", [128, D], F32)
        self.st = [ph.sb("st%d" % i, [128, 8], F32) for i in range(2)]
        self.y = [ph.sb("y%d" % i, [128, D], F32) for i in range(2)]
        if with_t:
            self.yb = [ph.sb("yb%d" % i, [128, D], BF16) for i in range(2)]
            self.tp = ph.ps("tp", [128, 8, 128], BF16)
            self.ts = [ph.sb("ts%d" % i, [128, 8, 128], BF16) for i in range(2)]
        else:
            self.yb = [None, None]
        self.want_t32 = want_t32
        if want_t32:
            self.ident32 = ph.sb("ident32", [128, 128], F32)
            S.op("dve", lambda e: e.tensor_copy(out=self.ident32[:], in_=self.ident[:]),
                 reads=[tag + "id"], writes=[tag + "id32"])
            self.tp32 = ph.ps("tp32", [128, 4, 128], F32)
            self.ts32 = [ph.sb("ts32%d" % i, [128, 8, 128], F32) for i in range(2)]
        self.n = 0

    def run(self, z, zkey, h_out_ap, hT_out, tok0, hT32_out=None, do_t=True):
        S, tag = self.c.S, self.tag
        i = self.n % 2
        self.n += 1
        st, y, yb, junk = self.st[i], self.y[i], self.yb[i], self.junk
        kst, ky, kyb = "%sst%d" % (tag, i), "%sy%d" % (tag, i), "%syb%d" % (tag, i)
        S.op("act", lambda e: e.activation(out=junk[:], in_=z, func=AF.Identity, accum_out=st[:, 0:1]),
             reads=[zkey], writes=[tag + "junk", kst])
        S.op("act", lambda e: e.activation(out=junk[:], in_=z, func=AF.Square, accum_out=st[:, 1:2]),
             reads=[zkey], writes=[tag + "junk", kst])
        S.op("dve", lambda e: e.tensor_scalar(out=st[:, 2:4], in0=st[:, 0:2], scalar1=1.0 / D, scalar2=None,
                                              op0=ALU.mult), reads=[kst], writes=[kst])
        S.op("dve", lambda e: e.tensor_tensor(out=st[:, 4:5], in0=st[:, 2:3], in1=st[:, 2:3], op=ALU.mult),
             reads=[kst], writes=[kst])
        S.op("dve", lambda e: e.tensor_tensor(out=st[:, 5:6], in0=st[:, 3:4], in1=st[:, 4:5], op=ALU.subtract),
             reads=[kst], writes=[kst])
        S.op("act", lambda e: e.activation(out=st[:, 6:7], in_=st[:, 5:6], func=AF.Sqrt, bias=self.epsc[:], scale=1.0),
             reads=[kst, tag + "eps"], writes=[kst])
        S.op("dve", lambda e: e.reciprocal(out=st[:, 6:7], in_=st[:, 6:7]), reads=[kst], writes=[kst])
        S.op("dve", lambda e: e.scalar_tensor_tensor(out=st[:, 7:8], in0=st[:, 2:3], scalar=-1.0, in1=st[:, 6:7],
                                                     op0=ALU.mult, op1=ALU.mult), reads=[kst], writes=[kst])
        S.op("act", lambda e: e.activation(out=y[:], in_=z, func=AF.Identity, bias=st[:, 7:8], scale=st[:, 6:7]),
             reads=[zkey, kst], writes=[ky])
        S.op("dve", lambda e: e.tensor_tensor(out=y[:], in0=y[:], in1=self.gB[:], op=ALU.mult),
             reads=[ky, tag + "gB"], writes=[ky])
        S.op("dve", lambda e: e.tensor_tensor(out=y[:], in0=y[:], in1=self.bB[:], op=ALU.add),
             reads=[ky, tag + "bB"], writes=[ky])
        dma(S, "sp", h_out_ap, y[:], [ky], [("hdram", id(h_out_ap))], tag + "ho%d" % i)
        if not do_t:
            return
        S.op("pool", lambda e: e.tensor_copy(out=yb[:], in_=y[:]), reads=[ky], writes=[kyb])
        for k in range(8):
            S.op("pe", lambda e, k=k: e.transpose(out=self.tp[:, k, :], in_=yb[:, k * 128:(k + 1) * 128],
                                                  identity=self.ident[:]),
                 reads=[kyb, tag + "id"], writes=[tag + "tp"])
        ts = self.ts[i]
        S.op("act", lambda e: e.copy(out=ts[:], in_=self.tp[:]), reads=[tag + "tp"], writes=[tag + "ts%d" % i])
        dma(S, "sp", hT_out.rearrange("(k p) t -> p k t", p=128)[:, :, tok0:tok0 + 128], ts[:],
            [tag + "ts%d" % i], [("hT", tok0)], tag + "to%d" % i)
        if self.want_t32 and hT32_out is not None:
            ts32 = self.ts32[i]
            for hf in range(2):
                for k in range(4):
                    kk = hf * 4 + k
                    S.op("pe", lambda e, k=k, kk=kk: e.transpose(out=self.tp32[:, k, :],
                                                                 in_=y[:, kk * 128:(kk + 1) * 128],
                                                                 identity=self.ident32[:]),
                         reads=[ky, tag + "id32"], writes=[tag + "tp32"])
                S.op("dve", lambda e, hf=hf: e.tensor_copy(out=ts32[:, hf * 4:(hf + 1) * 4, :], in_=self.tp32[:]),
                     reads=[tag + "tp32"], writes=[tag + "ts32%d" % i])
            dma(S, "sp", hT32_out.rearrange("(k p) t -> p k t", p=128)[:, :, tok0:tok0 + 128], ts32[:],
                [tag + "ts32%d" % i], [("hT32", tok0)], tag + "t32o%d" % i)

    def setup_eps(self):
        S, tag = self.c.S, self.tag
        self.epsc = self.ph.sb("epsc", [128, 1], F32)
        S.op("dve", lambda e: e.memset(self.epsc[:], LN_EPS), writes=[tag + "eps"])


def phase_ln0(c, T):
    S = c.S
    with c.phase() as ph:
        ln = LNUnit(c, ph, T["ln_in_g"], T["ln_in_b"], T["ident"], tag="l0")
        ln.setup_eps()
        xb = [ph.sb("xb%d" % i, [128, D], F32) for i in range(2)]
        for tb in range(16):
            i = tb % 2
            dma(S, "sp", xb[i][:], T["x"][tb * 128:(tb + 1) * 128, :], [], ["xb%d" % i], "l0x%d" % i)
            ln.run(xb[i][:], "xb%d" % i, T["h"][tb * 128:(tb + 1) * 128, :], T["hT"], tb * 128)


def phase_inproj(c, T, l):
    S = c.S
    w = T["w_in"][l]
    with c.phase() as ph:
        hTs = ph.sb("hTs", [128, 8, TL], BF16)
        dma(S, "sp", hTs[:], T["hT"].rearrange("(k p) t -> p k t", p=128), [("hT", "all")], ["hTs"], "ip_h")
        wb = [ph.sb("wb%d" % i, [128, 8, 512], BF16) for i in range(3)]
        stg = [ph.sb("stg%d" % i, [128, TL], BF16) for i in range(2)]
        vst = [ph.sb("vst%d" % i, [128, 512], BF16) for i in range(2)]
        pss = [ph.ps("ps%d" % i, [128, 512]) for i in range(4)]
        wi = [0]
        pi = [0]

        def load_w(c0, ncols):
            b = wi[0] % 3
            wi[0] += 1
            dma(S, "pool", wb[b][:, :, 0:ncols], w[:, c0:c0 + ncols].rearrange("(k p) n -> p k n", p=128),
                [], ["wb%d" % b], "ip_w%d" % b)
            return b

        si = [0]
        qflat = T["qT"].rearrange("m r t -> (m r) t")
        kflat = T["kT"].rearrange("m r t -> (m r) t") if "kT" in T else None
        sgflat = T["sg"].rearrange("g r t -> (g r) t")
        segs = []
        for j in range(1):
            segs.append((C_FQ, qflat, 0, "q"))
            segs.append((C_FK, kflat, 0, "k"))
            segs.append((C_DQ, qflat, 512, "q"))
            segs.append((C_DK, kflat, 512, "k"))
            segs.append((C_GA, sgflat, 0, "g"))
            segs.append((C_GA + 512, sgflat, 512, "g"))
            segs.append((C_GB, sgflat, 1024, "g"))
            segs.append((C_GB + 512, sgflat, 1536, "g"))
        for (c0, dst, r0, kind) in segs:
            b = load_w(c0, 512)
            for m in range(4):
                sb_i = si[0] % 2
                si[0] += 1
                for tt in range(4):
                    p = pi[0] % 4
                    pi[0] += 1
                    for k in range(8):
                        S.op("pe", lambda e, b=b, k=k, m=m, p=p, tt=tt: e.matmul(
                            pss[p][:], lhsT=wb[b][:, k, m * 128:(m + 1) * 128],
                            rhs=hTs[:, k, tt * 512:(tt + 1) * 512], start=(k == 0), stop=(k == 7)),
                            reads=["hTs", "wb%d" % b], writes=["ipps%d" % p])
                    o = stg[sb_i][:, tt * 512:(tt + 1) * 512]
                    if kind == "q":
                        S.op("act", lambda e, o=o, p=p: e.activation(out=o, in_=pss[p][:], func=AF.Copy, scale=0.125),
                             reads=["ipps%d" % p], writes=["stg%d" % sb_i])
                    elif kind == "k":
                        S.op("dve", lambda e, o=o, p=p: e.tensor_copy(out=o, in_=pss[p][:]),
                             reads=["ipps%d" % p], writes=["stg%d" % sb_i])
                    else:
                        S.op("act", lambda e, o=o, p=p: e.activation(out=o, in_=pss[p][:], func=AF.Sigmoid),
                             reads=["ipps%d" % p], writes=["stg%d" % sb_i])
                rr = r0 + m * 128
                if kind == "k" and "pack" in T:
                    dma(S, "sp", T["pack"][:, rr:rr + 128, :].rearrange("s r i -> r s i"),
                        stg[sb_i][:].rearrange("p (s i) -> p s i", s=4), ["stg%d" % sb_i], [("fm", "pack", rr)],
                        "ip_s%d" % sb_i)
                else:
                    dma(S, "sp", dst[rr:rr + 128, :], stg[sb_i][:], ["stg%d" % sb_i], [("fm", id(dst), rr)],
                        "ip_s%d" % sb_i)
        vi = [0]
        for vi_, (c0, dst) in enumerate(((C_FV, T.get("vf")), (C_DV, T.get("vd")))):
            b = load_w(c0, 512)
            for tb in range(16):
                p = pi[0] % 4
                pi[0] += 1
                for k in range(8):
                    S.op("pe", lambda e, b=b, k=k, p=p, tb=tb: e.matmul(
                        pss[p][:], lhsT=hTs[:, k, tb * 128:(tb + 1) * 128], rhs=wb[b][:, k, :],
                        start=(k == 0), stop=(k == 7)), reads=["hTs", "wb%d" % b], writes=["ipps%d" % p])
                v = vi[0] % 2
                vi[0] += 1
                eng = "act" if tb % 2 == 0 else "dve"
                if eng == "act":
                    S.op("act", lambda e, v=v, p=p: e.copy(out=vst[v][:], in_=pss[p][:]),
                         reads=["ipps%d" % p], writes=["vst%d" % v])
                else:
                    S.op("dve", lambda e, v=v, p=p: e.tensor_copy(out=vst[v][:], in_=pss[p][:]),
                         reads=["ipps%d" % p], writes=["vst%d" % v])
                if "pack" in T:
                    r0_ = 1024 + 512 * vi_ + (tb % 4) * 128
                    dma(S, "sp", T["pack"][tb // 4, r0_:r0_ + 128, :], vst[v][:], ["vst%d" % v], [("tm", vi_, tb)],
                        "ip_v%d" % v)
                else:
                    dma(S, "sp", dst[tb * 128:(tb + 1) * 128, :], vst[v][:], ["vst%d" % v], [("tm", id(dst), tb)],
                        "ip_v%d" % v)
        b = load_w(C_FG, 8)
        bf = ph.sb("bf", [128, 8], F32)
        dma(S, "sp", bf[:], bcast_rows(T["b_fgate"][l], 8), [], ["bf"], "ip_bf")
        lfs = ph.sb("lfs", [128, 16, 8], F32)
        t1 = ph.sb("t1", [128, 16, 8], F32)
        for tb in range(16):
            p = pi[0] % 4
            pi[0] += 1
            for k in range(8):
                S.op("pe", lambda e, b=b, k=k, p=p, tb=tb: e.matmul(
                    pss[p][:, 0:8], lhsT=hTs[:, k, tb * 128:(tb + 1) * 128], rhs=wb[b][:, k, 0:8],
                    start=(k == 0), stop=(k == 7)), reads=["hTs", "wb%d" % b], writes=["ipps%d" % p])
            S.op("dve", lambda e, p=p, tb=tb: e.tensor_tensor(out=t1[:, tb, :], in0=pss[p][:, 0:8], in1=bf[:],
                                                              op=ALU.add),
                 reads=["ipps%d" % p, "bf"], writes=["t1"])
        S.op("act", lambda e: e.activation(out=t1[:], in_=t1[:], func=AF.Exp, scale=-1.0), reads=["t1"], writes=["t1"])
        S.op("act", lambda e: e.activation(out=t1[:], in_=t1[:], func=AF.Ln, bias=1.0), reads=["t1"], writes=["t1"])
        S.op("dve", lambda e: e.tensor_scalar(out=lfs[:], in0=t1[:], scalar1=-1.0, scalar2=None, op0=ALU.mult),
             reads=["t1"], writes=["lfs"])
        dma(S, "sp", T["lf"].rearrange("(tb p) h -> p tb h", p=128), lfs[:], ["lfs"], ["lfd"], "ip_lf")


def phase_cum(c, T):
    S = c.S
    with c.phase() as ph:
        L = ph.sb("L", [128, 128, 8], F32)
        if "lfg" in T:
            dma(S, "act", L[:], lambda e: T["lfg"][bass.ds(S.pid(e) * 512, S_ALL), :].rearrange(
                "(j t) h -> j t h", t=128), [], ["L"], "cu0")
        else:
            dma(S, "sp", L[:], T["lfL"].rearrange("(j t) h -> j t h", t=128), [], ["L"], "cu0")
        pb = ph.sb("pb", [128, 128], F32)
        if "padsrc" in T:
            dma(S, "act", pb[:], lambda e: T["padsrc"][bass.ds(S.pid(e) * 512, S_ALL)].rearrange(
                "(j t) -> j t", t=128), [], ["pb"], "cu0")
        else:
            dma(S, "sp", pb[:], T["padb"].rearrange("(j t) -> j t", t=128), [], ["pb"], "cu0")
        su = ph.sb("su", [128, 128], F32)
        dma(S, "sp", su[:], T["su"], [], ["su"], "cu0")
        A = ph.sb("A", [128, 8, 128], F32)
        B = ph.sb("B", [128, 8, 128], F32)
        S.op("dve", lambda e: e.tensor_copy(out=A[:], in_=L[:].rearrange("j t h -> j h t")), reads=["L"], writes=["A"])
        cur, nxt, kc, kn = A, B, "A", "B"
        sft = 1
        while sft < 128:
            S.op("dve", lambda e, cur=cur, nxt=nxt, sft=sft: e.tensor_tensor(
                out=nxt[:, :, sft:128], in0=cur[:, :, sft:128], in1=cur[:, :, 0:128 - sft], op=ALU.add),
                reads=[kc], writes=[kn])
            S.op("dve", lambda e, cur=cur, nxt=nxt, sft=sft: e.tensor_copy(out=nxt[:, :, 0:sft], in_=cur[:, :, 0:sft]),
                 reads=[kc], writes=[kn])
            cur, nxt, kc, kn = nxt, cur, kn, kc
            sft *= 2
        bs = ph.sb("bs", [128, 8], F32)
        S.op("dve", lambda e: e.tensor_copy(out=bs[:], in_=cur[:, :, 127]), reads=[kc], writes=["bs"])
        pso = ph.ps("pso", [128, 8])
        S.op("pe", lambda e: e.matmul(pso[:], lhsT=su[:], rhs=bs[:], start=True, stop=True),
             reads=["su", "bs"], writes=["pso"])
        off = ph.sb("off", [128, 8], F32)
        S.op("dve", lambda e: e.tensor_copy(out=off[:], in_=pso[:]), reads=["pso"], writes=["off"])
        cum = nxt
        for h in range(8):
            S.op("dve", lambda e, h=h: e.tensor_scalar(out=cum[:, h, :], in0=cur[:, h, :], scalar1=off[:, h:h + 1],
                                                       scalar2=None, op0=ALU.add), reads=[kc, "off"], writes=[kn])
        negc = cur
        for h in range(8):
            S.op("dve", lambda e, h=h: e.scalar_tensor_tensor(out=negc[:, h, :], in0=cum[:, h, :], scalar=-1.0,
                                                              in1=pb[:], op0=ALU.mult, op1=ALU.add),
                 reads=[kn, "pb"], writes=[kc])
        KXs = ph.sb("KXs", [128, 8, 4, 128], BF16)
        r1 = ph.sb("r1", [128, 8, 128], F32)
        S.op("pool", lambda e: e.memset(KXs[:, :, 0, :], 1.0), writes=["KX0"])
        S.op("dve", lambda e: e.tensor_copy(out=KXs[:, :, 1, :], in_=negc[:]), reads=[kc], writes=["KX1"])
        S.op("dve", lambda e: e.tensor_tensor(out=r1[:], in0=negc[:], in1=KXs[:, :, 1, :], op=ALU.subtract),
             reads=[kc, "KX1"], writes=["r1"])
        S.op("dve", lambda e: e.tensor_copy(out=KXs[:, :, 2, :], in_=r1[:]), reads=["r1"], writes=["KX2"])
        S.op("dve", lambda e: e.tensor_tensor(out=r1[:], in0=r1[:], in1=KXs[:, :, 2, :], op=ALU.subtract),
             reads=["r1", "KX2"], writes=["r1"])
        S.op("dve", lambda e: e.tensor_copy(out=KXs[:, :, 3, :], in_=r1[:]), reads=["r1"], writes=["KX3"])
        dma(S, "sp", T["KX"].rearrange("h r (j t) -> j h r t", t=128), KXs[:], ["KX0", "KX1", "KX2", "KX3"],
            ["KXd"], "cu1")
        KD = ph.sb("KD", [128, 4, 128], BF16)
        S.op("pool", lambda e: e.memset(KD[:], 0.0), writes=["KD"])
        S.op("dve", lambda e: e.tensor_copy(out=KD[:, 0, :], in_=pb[:]), reads=["pb", "KD"], writes=["KD"])
        dma(S, "sp", T["KXD"].rearrange("r (j t) -> j r t", t=128), KD[:], ["KD"], ["KXDd"], "cu1")
        cb = ph.sb("cb", [128, 8, 128], BF16)
        S.op("dve", lambda e: e.tensor_copy(out=cb[:], in_=cum[:]), reads=[kn], writes=["cb"])
        for s in range(4):
            j0 = 32 * s + NPADB
            dma(S, "sp", T["QX"][:, s * 512:(s + 1) * 512].rearrange("h (i t) -> i h t", t=128),
                cb[j0:j0 + 4, :, :], ["cb"], ["QXd"], "cu1")


def phase_attn(c, T, l, lam_init):
    S = c.S
    with c.phase() as ph:
        kt = [ph.sb("kt%d" % i, [128, S_ALL], BF16) for i in range(2)]
        vt = [ph.sb("vt%d" % i, [128, 128, 128], BF16) for i in range(2)]
        qf = [ph.sb("qf%d" % i, [128, TL], BF16) for i in range(2)]
        pt = [ph.sb("pt%d" % i, [128, 1024], BF16) for i in range(4)]
        mk = ph.sb("mk", [128, 6, 512], F32)
        yst = [ph.sb("yst%d" % i, [128, TL], BF16) for i in range(2)]
        onesf = ph.sb("onesf", [128, 128], F32)
        rr = ph.sb("rr", [128, 512], F32)
        bcs = ph.sb("bcs", [128, 512], F32)
        a1 = ph.sb("a1", [128, 512], F32)
        a2 = ph.sb("a2", [128, 512], F32)
        sq = ph.sb("sq", [128, 512], F32)
        accL = [ph.sb("accL%d" % i, [128, 1024], F32) for i in range(2)]
        cols = ph.sb("cols", [128, 16], F32)
        lmt = ph.sb("lmt", [128, 4, 64], F32)
        lmj = ph.sb("lmj", [128, 64], F32)
        s2 = [ph.ps("s2%d" % i, [128, 1024]) for i in range(2)]
        p1 = [ph.ps("p1%d" % i, [128, 512]) for i in range(4)]
        S.op("pool", lambda e: e.memset(onesf[:], 1.0), writes=["onesf"])
        for i in range(2):
            S.op("pool", lambda e, i=i: e.memset(vt[i][:, :, 64:128], 0.0), writes=["vt%d" % i])
            S.op("pool", lambda e, i=i: e.memset(vt[i][:, :, 64:65], 1.0), writes=["vt%d" % i])
            S.op("pool", lambda e, i=i: e.memset(qf[i][64:128, :], 0.0), writes=["qf%d" % i])
            S.op("pool", lambda e, i=i: e.memset(qf[i][64:68, :], 1.0), writes=["qf%d" % i])
            S.op("pool", lambda e, i=i: e.memset(kt[i][64:128, :], 0.0), writes=["kt%d" % i])
        for i, nm in enumerate(("lam_q1", "lam_k1", "lam_q2", "lam_k2")):
            dma(S, "sp", lmt[:, i, :], bcast_rows(T[nm][l], 64), [], ["lmt"], "at_c")
        dma(S, "sp", cols[:, 8:12], bcast_rows(T["rel_bias"][31], 4), [], ["cols_b"], "at_c")
        dma(S, "sp", cols[:, 4:5], T["subln_g"][l].rearrange("(p o) -> p o", o=1), [], ["cols_g"], "at_c")
        for i in range(2):
            S.op("dve", lambda e, i=i: e.tensor_tensor(out=lmj[:], in0=lmt[:, 2 * i, :], in1=lmt[:, 2 * i + 1, :],
                                                       op=ALU.mult), reads=["lmt"], writes=["lmj"])
            S.op("dve", lambda e, i=i: e.tensor_reduce(out=cols[:, i:i + 1], in_=lmj[:], axis=AX.X, op=ALU.add),
                 reads=["lmj"], writes=["cols_l"])
        S.op("act", lambda e: e.activation(out=cols[:, 0:2], in_=cols[:, 0:2], func=AF.Exp),
             reads=["cols_l"], writes=["cols_l"])
        S.op("dve", lambda e: e.tensor_tensor(out=cols[:, 2:3], in0=cols[:, 1:2], in1=cols[:, 0:1], op=ALU.subtract),
             reads=["cols_l"], writes=["cols_l"])
        S.op("dve", lambda e: e.tensor_scalar(out=cols[:, 3:4], in0=cols[:, 2:3], scalar1=-float(lam_init),
                                              scalar2=None, op0=ALU.add), reads=["cols_l"], writes=["cols_n"])
        S.op("dve", lambda e: e.tensor_scalar(out=cols[:, 5:6], in0=cols[:, 4:5], scalar1=float(1.0 - lam_init),
                                              scalar2=None, op0=ALU.mult), reads=["cols_g"], writes=["cols_g2"])
        S.op("dve", lambda e: e.memset(cols[:, 6:7], 1e-5), writes=["cols_e"])

        def load_k(buf, m, fox_h):
            if "GL" in T:
                dma(S, "sp", kt[buf][0:64, :].rearrange("d (t i) -> d t i", i=512),
                    T["GL"][:, m * 64:(m + 1) * 64, :].rearrange("t d i -> d t i"), [], ["kt%d" % buf], "at_k%d" % buf)
            else:
                dma(S, "sp", kt[buf][0:64, :], T["kL"][m], [], ["kt%d" % buf], "at_k%d" % buf)
            if fox_h is not None:
                dma(S, "sp", kt[buf][64:68, :], T["KX"][fox_h], [], ["kt%d" % buf], "at_k%d" % buf)
            else:
                dma(S, "sp", kt[buf][64:68, :], T["KXD"], [], ["kt%d" % buf], "at_k%d" % buf)

        def load_q(buf, m, fox_h):
            dma(S, "sp", qf[buf][0:64, :], T["qT"][m], [], ["qf%d" % buf], "at_q%d" % buf)
            if fox_h is not None:
                dma(S, "sp", qf[buf][64:65, :], T["QX"][fox_h:fox_h + 1, :], [], ["qf%d" % buf], "at_q%d" % buf)

        def load_v(buf, col0, ncol, row0):
            if "GL" in T:
                for bb in range(4):
                    dma(S, "sp", vt[buf][:, :, 0:ncol].rearrange("p (t b) d -> p t b d", b=4)[:, :, bb, :],
                        T["GL"][:, row0 + bb * 128:row0 + 128 + bb * 128, col0:col0 + ncol].rearrange("t p d -> p t d"),
                        [], ["vt%d" % buf], "at_v%d" % buf)
            else:
                src = T["vfL"] if row0 == 1024 else T["vdL"]
                dma(S, "sp", vt[buf][:, :, 0:ncol], src[:, col0:col0 + ncol].rearrange("(j p) d -> p j d", p=128),
                    [], ["vt%d" % buf], "at_v%d" % buf)

        jc = [0]

        def run_pipeline(jobs):
            n = len(jobs)
            for j in range(min(2, n)):
                jobs[j]["S"]()
                jobs[j]["A"]()
            for j in range(n):
                jobs[j]["PV"]()
                if j + 2 < n:
                    jobs[j + 2]["S"]()
                    jobs[j + 2]["A"]()
                if "end" in jobs[j]:
                    jobs[j]["end"]()

        def mk_job(kb, qb, s, pi, NB, near_from, far_bias, pv_fn, end_fn=None):
            j = jc[0]
            jc[0] += 1
            sp_, sk = s2[j % 2], "s2%d" % (j % 2)
            pb_ = j % 4
            blks = (2 * pi, 2 * pi + 1)
            is_near = blks[1] >= near_from

            def fS():
                for hf, blk in enumerate(blks):
                    S.op("pe", lambda e, blk=blk, hf=hf: e.matmul(
                        sp_[:, hf * 512:(hf + 1) * 512], lhsT=kt[kb][:, blk * 128:(blk + 1) * 128],
                        rhs=qf[qb][:, s * 512:(s + 1) * 512], start=True, stop=True),
                        reads=["kt%d" % kb, "qf%d" % qb], writes=[sk])
                    mi = blk - (NB - 6)
                    if blk >= near_from:
                        S.op("dve", lambda e, hf=hf, mi=mi: e.tensor_tensor(
                            out=sp_[:, hf * 512:(hf + 1) * 512], in0=sp_[:, hf * 512:(hf + 1) * 512],
                            in1=mk[:, mi, :], op=ALU.add), reads=[sk, "mk"], writes=[sk])

            def fA():
                if far_bias is None or is_near:
                    S.op("act", lambda e: e.activation(out=pt[pb_][:], in_=sp_[:], func=AF.Exp),
                         reads=[sk], writes=["pt%d" % pb_])
                else:
                    S.op("act", lambda e: e.activation(out=pt[pb_][:], in_=sp_[:], func=AF.Exp, bias=far_bias),
                         reads=[sk, "cols_b"], writes=["pt%d" % pb_])

            d = {"S": fS, "A": fA, "PV": lambda: pv_fn(pb_, blks)}
            if end_fn is not None:
                d["end"] = end_fn
            return d

        dma(S, "sp", mk[:, 1:6, :], T["maskF"].rearrange("m k q -> k m q"), [], ["mk"], "at_m")
        oi = [0]
        for hh in range(8):
            b = hh % 2
            load_k(b, hh, hh)
            load_q(b, hh, hh)
            load_v(b, hh * 64, 64, 1024)
            yb = hh % 2
            jobs = []
            for s in range(4):
                NB = 32 * s + 32
                o = p1[oi[0] % 2]
                ok = "p1%d" % (oi[0] % 2)
                oi[0] += 1

                def pv(pb_, blks, o=o, ok=ok, b=b, NB=NB):
                    for hf, blk in enumerate(blks):
                        S.op("pe", lambda e, blk=blk, hf=hf: e.matmul(
                            o[:, :], lhsT=vt[b][:, blk, :], rhs=pt[pb_][:, hf * 512:(hf + 1) * 512],
                            start=(blk == 0), stop=(blk == NB - 1)),
                            reads=["vt%d" % b, "pt%d" % pb_], writes=[ok])

                def end(o=o, ok=ok, s=s, yb=yb):
                    S.op("dve", lambda e: e.reciprocal(out=rr[64:65, :], in_=o[64:65, :]), reads=[ok], writes=["rr"])
                    bc = p1[2]
                    S.op("pe", lambda e: e.matmul(bc[0:64, :], lhsT=onesf[64:65, 0:64], rhs=rr[64:65, :],
                                                  start=True, stop=True), reads=["onesf", "rr"], writes=["p12"])
                    S.op("act", lambda e: e.copy(out=bcs[0:64, :], in_=bc[0:64, :]), reads=["p12"], writes=["bcs"])
                    S.op("dve", lambda e: e.tensor_tensor(
                        out=yst[yb][0:64, s * 512:(s + 1) * 512], in0=o[0:64, :], in1=bcs[0:64, :], op=ALU.mult),
                        reads=[ok, "bcs"], writes=["yst%d" % yb])

                for pi in range(NB // 2):
                    last = (pi == NB // 2 - 1)
                    jobs.append(mk_job(b, b, s, pi, NB, NB - 5, None, pv, end if last else None))
            run_pipeline(jobs)
            dma(S, "sp", T["yF"][hh], yst[yb][0:64, :], ["yst%d" % yb], ["yFd"], "at_y%d" % yb)

        for h in range(4):
            dma(S, "sp", mk[:], T["BT"][h].rearrange("m k q -> k m q"), [], ["mk"], "at_m")
            for i in range(2):
                load_k(i, 8 + 2 * h + i, None)
                load_q(i, 8 + 2 * h + i, None)
                if h == 0:
                    S.op("pool", lambda e, i=i: e.memset(qf[i][64:68, :], 1.0), reads=[], writes=["qf%d" % i])
            vb = h % 2
            load_v(vb, h * 128, 128, 1536)
            yb = h % 2
            jobs = []
            for s in range(4):
                NB = 32 * s + 32

                def mkpv(i, NB=NB, vb=vb):
                    def pv(pb_, blks):
                        for hf, blk in enumerate(blks):
                            S.op("pe", lambda e, blk=blk, hf=hf: e.matmul(
                                p1[i][:], lhsT=vt[vb][:, blk, :], rhs=pt[pb_][:, hf * 512:(hf + 1) * 512],
                                start=(blk == 0), stop=(blk == NB - 1)),
                                reads=["vt%d" % vb, "pt%d" % pb_], writes=["p1%d" % i])
                        eng = "dve" if i == 0 else "pool"
                        if blks[0] == 0:
                            S.op(eng, lambda e: e.tensor_copy(out=accL[i][:], in_=pt[pb_][:]),
                                 reads=["pt%d" % pb_], writes=["accL%d" % i])
                        else:
                            S.op(eng, lambda e: e.tensor_tensor(out=accL[i][:], in0=accL[i][:], in1=pt[pb_][:], op=ALU.add),
                                 reads=["pt%d" % pb_, "accL%d" % i], writes=["accL%d" % i])
                    return pv

                def end(s=s, yb=yb):
                    for i in range(2):
                        S.op("dve", lambda e, i=i: e.tensor_tensor(out=sq[:], in0=accL[i][:, 0:512], in1=accL[i][:, 512:1024],
                                                                   op=ALU.add), reads=["accL%d" % i], writes=["sq"])
                        S.op("pe", lambda e, i=i: e.matmul(p1[2 + i][:], lhsT=onesf[:], rhs=sq[:], start=True, stop=True),
                             reads=["onesf", "sq"], writes=["p1%d" % (2 + i)])
                    S.op("dve", lambda e: e.reciprocal(out=rr[:], in_=p1[2][:]), reads=["p12"], writes=["rr"])
                    S.op("dve", lambda e: e.tensor_tensor(out=a1[:], in0=p1[0][:], in1=rr[:], op=ALU.mult),
                         reads=["p10", "rr"], writes=["a1"])
                    S.op("dve", lambda e: e.reciprocal(out=bcs[:], in_=p1[3][:]), reads=["p13"], writes=["bcs"])
                    S.op("dve", lambda e: e.tensor_tensor(out=a2[:], in0=p1[1][:], in1=bcs[:], op=ALU.mult),
                         reads=["p11", "bcs"], writes=["a2"])
                    S.op("dve", lambda e: e.scalar_tensor_tensor(out=a1[:], in0=a2[:], scalar=cols[:, 3:4], in1=a1[:],
                                                                 op0=ALU.mult, op1=ALU.add),
                         reads=["a1", "a2", "cols_n"], writes=["a1"])
                    S.op("act", lambda e: e.activation(out=sq[:], in_=a1[:], func=AF.Square), reads=["a1"], writes=["sq"])
                    S.op("pe", lambda e: e.matmul(p1[2][:], lhsT=onesf[:], rhs=sq[:], start=True, stop=True),
                         reads=["onesf", "sq"], writes=["p12"])
                    S.op("act", lambda e: e.activation(out=a2[:], in_=p1[2][:], func=AF.Sqrt, bias=cols[:, 6:7],
                                                       scale=1.0 / 128.0), reads=["p12", "cols_e"], writes=["a2"])
                    S.op("dve", lambda e: e.reciprocal(out=a2[:], in_=a2[:]), reads=["a2"], writes=["a2"])
                    S.op("dve", lambda e: e.scalar_tensor_tensor(
                        out=yst[yb][:, s * 512:(s + 1) * 512], in0=a1[:], scalar=cols[:, 5:6], in1=a2[:],
                        op0=ALU.mult, op1=ALU.mult), reads=["a1", "a2", "cols_g2"], writes=["yst%d" % yb])

                for pi in range(NB // 2):
                    last = (pi == NB // 2 - 1)
                    for i in range(2):
                        jobs.append(mk_job(i, i, s, pi, NB, NB - 6, cols[:, 8 + h:9 + h], mkpv(i),
                                           end if (last and i == 1) else None))
            run_pipeline(jobs)
            dma(S, "sp", T["yD"][h], yst[yb][:], ["yst%d" % yb], ["yDd"], "at_y%d" % yb)


def phase_post(c, T, l, want_t32):
    S = c.S
    with c.phase() as ph:
        ln = LNUnit(c, ph, T["ln_mix_g"][l], T["ln_mix_b"][l], T["ident"], want_t32=want_t32, tag="pm")
        ln.setup_eps()
        wbf = ph.sb("wbf", [64, 8, D], BF16)
        wbd = ph.sb("wbd", [128, 4, D], BF16)
        wo = ph.sb("wo", [128, 8, D], BF16)
        dma(S, "pool", wbf[:], T["w_branch_fox"][l].rearrange("(h d) n -> d h n", d=64), [], ["wbf"], "po_w")
        dma(S, "pool", wbd[:], T["w_branch_diff"][l].rearrange("(h d) n -> d h n", d=128), [], ["wbd"], "po_w")
        dma(S, "pool", wo[:], T["w_out"][l].rearrange("(k p) n -> p k n", p=128), [], ["wo"], "po_w")
        yF = ph.sb("yF", [64, 8, 512], BF16)
        yD = ph.sb("yD", [128, 4, 512], BF16)
        sga = ph.sb("sga", [128, 8, 512], BF16)
        sgb = ph.sb("sgb", [128, 8, 512], BF16)
        mg = ph.sb("mg", [128, 8, 512], BF16)
        t1 = ph.sb("t1", [128, 512], F32)
        t2 = ph.sb("t2", [128, 512], F32)
        hb = [ph.sb("hb%d" % i, [128, D], F32) for i in range(2)]
        pa = ph.ps("pa", [128, 512])
        pb = ph.ps("pb", [128, 512])
        pm = ph.ps("pm", [128, 1024])
        for tt in range(4):
            tsl = slice(tt * 512, (tt + 1) * 512)
            dma(S, "sp", yF[:], T["yF"][:, :, tsl].rearrange("h d t -> d h t"), [], ["yF"], "po_a")
            dma(S, "sp", yD[:], T["yD"][:, :, tsl].rearrange("h d t -> d h t"), [], ["yD"], "po_a")
            dma(S, "sp", sga[:], T["sg"][0][:, tsl].rearrange("(k p) t -> p k t", p=128), [], ["sga"], "po_a")
            dma(S, "sp", sgb[:], T["sg"][1][:, tsl].rearrange("(k p) t -> p k t", p=128), [], ["sgb"], "po_a")
            for n in range(8):
                for h in range(8):
                    S.op("pe", lambda e, h=h, n=n: e.matmul(pa[:], lhsT=wbf[:, h, n * 128:(n + 1) * 128], rhs=yF[:, h, :],
                                                            start=(h == 0), stop=(h == 7)),
                         reads=["wbf", "yF"], writes=["pa"])
                for h in range(4):
                    S.op("pe", lambda e, h=h, n=n: e.matmul(pb[:], lhsT=wbd[:, h, n * 128:(n + 1) * 128], rhs=yD[:, h, :],
                                                            start=(h == 0), stop=(h == 3)),
                         reads=["wbd", "yD"], writes=["pb"])
                S.op("dve", lambda e, n=n: e.tensor_tensor(out=t1[:], in0=sga[:, n, :], in1=pa[:], op=ALU.mult),
                     reads=["sga", "pa"], writes=["t1"])
                S.op("dve", lambda e, n=n: e.tensor_tensor(out=t2[:], in0=sgb[:, n, :], in1=pb[:], op=ALU.mult),
                     reads=["sgb", "pb"], writes=["t2"])
                S.op("pool", lambda e, n=n: e.tensor_tensor(out=mg[:, n, :], in0=t1[:], in1=t2[:], op=ALU.add),
                     reads=["t1", "t2"], writes=["mg"])
            for tb in range(4):
                tok0 = tt * 512 + tb * 128
                i = tb % 2
                dma(S, "sp", hb[i][:], T["h"][tok0:tok0 + 128, :], [], ["hb%d" % i], "po_h%d" % i)
                for hf in range(2):
                    for k in range(8):
                        S.op("pe", lambda e, k=k, hf=hf, tb=tb: e.matmul(
                            pm[:, hf * 512:(hf + 1) * 512], lhsT=mg[:, k, tb * 128:(tb + 1) * 128],
                            rhs=wo[:, k, hf * 512:(hf + 1) * 512], start=(k == 0), stop=(k == 7)),
                            reads=["mg", "wo"], writes=["pm"])
                S.op("dve", lambda e, i=i: e.scalar_tensor_tensor(out=hb[i][:], in0=hb[i][:], scalar=float(ALPHA),
                                                                  in1=pm[:], op0=ALU.mult, op1=ALU.add),
                     reads=["hb%d" % i, "pm"], writes=["hb%d" % i])
                ln.run(hb[i][:], "hb%d" % i, T["h"][tok0:tok0 + 128, :], T["hT"], tok0,
                       hT32_out=(T["hT32"] if want_t32 else None))


def ffn_core(c, ph, T, S, hTs, hkey, wgu, dff, wdown, nfc, fc0, aT, akey, wd, wdkey, stl, pg, pu, sgt, wgb, wcnt, tag):
    for f in range(nfc):
        fc = fc0 + f
        b = wcnt[0] % 3
        wcnt[0] += 1
        dma(S, "pool", wgb[b][:, 0, :, :], wgu[:, fc * 128:(fc + 1) * 128].rearrange("(k p) n -> p k n", p=128),
            [], [tag + "wg%d" % b], tag + "wg%d" % b)
        dma(S, "pool", wgb[b][:, 1, :, :], wgu[:, dff + fc * 128:dff + (fc + 1) * 128].rearrange("(k p) n -> p k n", p=128),
            [], [tag + "wg%d" % b], tag + "wg%d" % b)
        for t2 in range(2):
            for k in range(8):
                S.op("pe", lambda e, b=b, k=k, t2=t2: e.matmul(pg[:], lhsT=wgb[b][:, 0, k, :],
                                                               rhs=hTs[:, k, t2 * 512:(t2 + 1) * 512],
                                                               start=(k == 0), stop=(k == 7)),
                     reads=[hkey, tag + "wg%d" % b], writes=[tag + "pg"])
            for k in range(8):
                S.op("pe", lambda e, b=b, k=k, t2=t2: e.matmul(pu[:], lhsT=wgb[b][:, 1, k, :],
                                                               rhs=hTs[:, k, t2 * 512:(t2 + 1) * 512],
                                                               start=(k == 0), stop=(k == 7)),
                     reads=[hkey, tag + "wg%d" % b], writes=[tag + "pu"])
            S.op("act", lambda e: e.activation(out=sgt[:], in_=pg[:], func=AF.Silu), reads=[tag + "pg"], writes=[tag + "sgt"])
            S.op("dve", lambda e, f=f, t2=t2: e.tensor_tensor(out=aT[:, f, t2 * 512:(t2 + 1) * 512], in0=sgt[:], in1=pu[:],
                                                              op=ALU.mult),
                 reads=[tag + "sgt", tag + "pu"], writes=[akey])


def phase_ffn(c, T):
    S = c.S
    NF = DFF // 128
    with c.phase() as ph:
        ln = LNUnit(c, ph, T["ln_ffn_g"][0], T["ln_ffn_b"][0], T["ident"], tag="pf")
        ln.setup_eps()
        wd = ph.sb("wd", [128, NF, D], BF16)
        wdv = T["w_ffn_down"][0].rearrange("(f p) n -> p f n", p=128)
        for q in range(0, NF, 6):
            q1 = min(NF, q + 6)
            dma(S, "pool", wd[:, q:q1, :], wdv[:, q:q1, :], [], ["wd"], "ff_wd")
        hTs = ph.sb("hTs", [128, 8, 1024], BF16)
        aT = ph.sb("aT", [128, NF, 1024], BF16)
        wgb = [ph.sb("wgb%d" % i, [128, 2, 8, 128], BF16) for i in range(3)]
        sgt = ph.sb("sgt", [128, 512], F32)
        hb = [ph.sb("hb%d" % i, [128, D], F32) for i in range(2)]
        pg = ph.ps("pg", [128, 512])
        pu = ph.ps("pu", [128, 512])
        pm = ph.ps("pm", [128, 1024])
        wcnt = [0]
        for st in range(2):
            dma(S, "sp", hTs[:], T["hT"][:, st * 1024:(st + 1) * 1024].rearrange("(k p) t -> p k t", p=128),
                [], ["hTs"], "ff_h")
            ffn_core(c, ph, T, S, hTs, "hTs", T["w_ffn_gate_up"][0], DFF, None, NF, 0, aT, "aT", wd, "wd", st,
                     pg, pu, sgt, wgb, wcnt, "ff")
            for tb in range(8):
                tok0 = st * 1024 + tb * 128
                i = tb % 2
                dma(S, "sp", hb[i][:], T["h"][tok0:tok0 + 128, :], [], ["hb%d" % i], "ff_h%d" % i)
                for hf in range(2):
                    for f in range(NF):
                        S.op("pe", lambda e, f=f, hf=hf, tb=tb: e.matmul(
                            pm[:, hf * 512:(hf + 1) * 512], lhsT=aT[:, f, tb * 128:(tb + 1) * 128],
                            rhs=wd[:, f, hf * 512:(hf + 1) * 512], start=(f == 0), stop=(f == NF - 1)),
                            reads=["aT", "wd"], writes=["pm"])
                S.op("dve", lambda e, i=i: e.scalar_tensor_tensor(out=hb[i][:], in0=hb[i][:], scalar=float(ALPHA),
                                                                  in1=pm[:], op0=ALU.mult, op1=ALU.add),
                     reads=["hb%d" % i, "pm"], writes=["hb%d" % i])
                ln.run(hb[i][:], "hb%d" % i, T["h"][tok0:tok0 + 128, :], T["hT"], tok0)


def phase_moe(c, T):
    S = c.S
    NH = 14
    with c.phase() as ph:
        ln = LNUnit(c, ph, T["ln_ffn_g"][1], T["ln_ffn_b"][1], T["ident"], tag="pe", with_t=False)
        ln.setup_eps()
        hTs = ph.sb("hTs", [128, 8, 1024], BF16)
        aT = [ph.sb("aT%d" % i, [128, NH, 1024], BF16) for i in range(2)]
        wd = [ph.sb("wd%d" % i, [128, NH, D], BF16) for i in range(2)]
        wgb = [ph.sb("wgb%d" % i, [128, 2, 8, 128], BF16) for i in range(3)]
        acc = ph.sb("acc", [128, 8, D], F32)
        sgt = ph.sb("sgt", [128, 512], F32)
        h32 = [ph.sb("h32%d" % i, [128, 8, 128], F32) for i in range(2)]
        wr = ph.sb("wr", [128, 8, NE], F32)
        comb = ph.sb("comb", [128, 8, NE], F32)
        rt = ph.sb("rt", [128, 8, 8], F32)
        rs = ph.sb("rs", [128, 4], F32)
        pg = ph.ps("pg", [128, 512])
        pu = ph.ps("pu", [128, 512])
        pm = ph.ps("pm", [128, 1024])
        pr = ph.ps("pr", [128, 8])
        dma(S, "sp", wr[:], T["w_router"][0].rearrange("(k p) e -> p k e", p=128), [], ["wr"], "mo_c")
        wcnt = [0]
        hcnt = [0]
        for st in range(2):
            dma(S, "sp", hTs[:], T["hT"][:, st * 1024:(st + 1) * 1024].rearrange("(k p) t -> p k t", p=128),
                [], ["hTs"], "mo_h")
            for tb in range(8):
                tok0 = st * 1024 + tb * 128
                i = tb % 2
                dma(S, "sp", h32[i][:], T["hT32"].rearrange("(k p) t -> p k t", p=128)[:, :, tok0:tok0 + 128],
                    [], ["h32%d" % i], "mo_r%d" % i)
                for k in range(8):
                    S.op("pe", lambda e, k=k, i=i: e.matmul(pr[:], lhsT=h32[i][:, k, :], rhs=wr[:, k, :],
                                                            start=(k == 0), stop=(k == 7)),
                         reads=["h32%d" % i, "wr"], writes=["pr"])
                S.op("dve", lambda e: e.tensor_copy(out=rt[:, 0, :], in_=pr[:]), reads=["pr"], writes=["rt0"])
                S.op("dve", lambda e: e.max(out=rt[:, 1, :], in_=rt[:, 0, :]), reads=["rt0"], writes=["rt1"])
                S.op("dve", lambda e: e.tensor_scalar(out=rt[:, 2, :], in0=rt[:, 0, :], scalar1=rt[:, 1, 1:2], scalar2=None,
                                                      op0=ALU.is_ge), reads=["rt0", "rt1"], writes=["rt2"])
                S.op("dve", lambda e: e.tensor_scalar(out=rs[:, 0:1], in0=rt[:, 1, 0:1], scalar1=-1.0, scalar2=None,
                                                      op0=ALU.mult), reads=["rt1"], writes=["rs0"])
                S.op("act", lambda e: e.activation(out=rt[:, 3, :], in_=rt[:, 0, :], func=AF.Exp, bias=rs[:, 0:1]),
                     reads=["rt0", "rs0"], writes=["rt3"])
                S.op("dve", lambda e: e.tensor_tensor(out=rt[:, 4, :], in0=rt[:, 2, :], in1=rt[:, 3, :], op=ALU.mult),
                     reads=["rt2", "rt3"], writes=["rt4"])
                S.op("dve", lambda e: e.tensor_reduce(out=rs[:, 1:2], in_=rt[:, 4, :], axis=AX.X, op=ALU.add),
                     reads=["rt4"], writes=["rs1"])
                S.op("dve", lambda e: e.reciprocal(out=rs[:, 2:3], in_=rs[:, 1:2]), reads=["rs1"], writes=["rs2"])
                S.op("dve", lambda e, tb=tb: e.tensor_scalar(out=comb[:, tb, :], in0=rt[:, 4, :], scalar1=rs[:, 2:3],
                                                             scalar2=None, op0=ALU.mult),
                     reads=["rt4", "rs2"], writes=["comb"])
            for ex in range(NE):
                for hf in range(2):
                    hb_ = hcnt[0] % 2
                    hcnt[0] += 1
                    wdv = T["w_expert_down"][0][ex].rearrange("(f p) n -> p f n", p=128)
                    for q in range(0, NH, 7):
                        dma(S, "pool", wd[hb_][:, q:q + 7, :], wdv[:, hf * NH + q:hf * NH + q + 7, :], [],
                            ["wd%d" % hb_], "mo_wd%d" % hb_)
                    ffn_core(c, ph, T, S, hTs, "hTs", T["w_expert_gate_up"][0][ex], DFE, None, NH, hf * NH, aT[hb_],
                             "aT%d" % hb_, None, None, st, pg, pu, sgt, wgb, wcnt, "mo")
                    for tb in range(8):
                        for h2 in range(2):
                            for f in range(NH):
                                S.op("pe", lambda e, f=f, h2=h2, tb=tb, hb_=hb_: e.matmul(
                                    pm[:, h2 * 512:(h2 + 1) * 512], lhsT=aT[hb_][:, f, tb * 128:(tb + 1) * 128],
                                    rhs=wd[hb_][:, f, h2 * 512:(h2 + 1) * 512], start=(f == 0), stop=(f == NH - 1)),
                                    reads=["aT%d" % hb_, "wd%d" % hb_], writes=["pm"])
                        if ex == 0 and hf == 0:
                            S.op("dve", lambda e, tb=tb, ex=ex: e.tensor_scalar(
                                out=acc[:, tb, :], in0=pm[:], scalar1=comb[:, tb, ex:ex + 1], scalar2=None, op0=ALU.mult),
                                reads=["pm", "comb"], writes=[("acc", tb)])
                        else:
                            S.op("dve", lambda e, tb=tb, ex=ex: e.scalar_tensor_tensor(
                                out=acc[:, tb, :], in0=pm[:], scalar=comb[:, tb, ex:ex + 1], in1=acc[:, tb, :],
                                op0=ALU.mult, op1=ALU.add), reads=["pm", "comb", ("acc", tb)], writes=[("acc", tb)])
            for tb in range(8):
                tok0 = st * 1024 + tb * 128
                i = tb % 2
                dma(S, "sp", h32[i][:].rearrange("p a b -> p (a b)"), T["h"][tok0:tok0 + 128, :], [], ["h32%d" % i],
                    "mo_r%d" % i)
                S.op("dve", lambda e, i=i, tb=tb: e.scalar_tensor_tensor(
                    out=acc[:, tb, :], in0=h32[i][:].rearrange("p a b -> p (a b)"), scalar=float(ALPHA),
                    in1=acc[:, tb, :], op0=ALU.mult, op1=ALU.add),
                    reads=["h32%d" % i, ("acc", tb)], writes=[("acc", tb)])
                ln.run(acc[:, tb, :], ("acc", tb), T["out"][tok0:tok0 + 128, :], None, tok0, do_t=False)


W_SPECS = {
    "ln_in_g": ([D], F32), "ln_in_b": ([D], F32), "w_in": ([2, D, NIN], F32), "b_fgate": ([2, 8], F32),
    "lam_q1": ([2, 64], F32), "lam_k1": ([2, 64], F32), "lam_q2": ([2, 64], F32), "lam_k2": ([2, 64], F32),
    "subln_g": ([2, 128], F32), "w_branch_fox": ([2, 512, D], F32), "w_branch_diff": ([2, 512, D], F32),
    "w_out": ([2, D, D], F32), "ln_mix_g": ([2, D], F32), "ln_mix_b": ([2, D], F32), "rel_bias": ([32, 4], F32),
    "w_ffn_gate_up": ([1, D, 2 * DFF], F32), "w_ffn_down": ([1, DFF, D], F32), "w_router": ([1, D, NE], F32),
    "w_expert_gate_up": ([1, NE, D, 2 * DFE], F32), "w_expert_down": ([1, NE, DFE, D], F32),
    "ln_ffn_g": ([2, D], F32), "ln_ffn_b": ([2, D], F32),
}
A_SPECS = {
    "x": ([TL, D], F32), "h": ([TL, D], F32), "hT": ([D, TL], BF16), "hT32": ([D, TL], F32),
    "qT": ([16, 64, TL], BF16), "kT": ([16, 64, TL], BF16), "vf": ([TL, 512], BF16), "vd": ([TL, 512], BF16),
    "lf": ([TL, 8], F32), "sg": ([2, D, TL], BF16),
    "kL": ([16, 64, S_ALL], BF16), "vfL": ([S_ALL, 512], BF16), "vdL": ([S_ALL, 512], BF16),
    "lfL": ([S_ALL, 8], F32), "padb": ([S_ALL], F32), "su": ([128, 128], F32), "ident": ([128, 128], BF16),
    "maskF": ([5, 128, 512], F32), "BT": ([4, 6, 128, 512], F32),
    "KX": ([8, 4, S_ALL], BF16), "KXD": ([4, S_ALL], BF16), "QX": ([8, TL], BF16),
    "yF": ([8, 64, TL], BF16), "yD": ([4, 128, TL], BF16), "out": ([TL, D], F32),
}
LAUNCH = {
    1: dict(ins=["x", "ident", "ln_in_g", "ln_in_b", "w_in", "b_fgate"],
            outs=["h", "hT", "qT", "kT", "vf", "vd", "lf", "sg"], internal=[]),
    2: dict(ins=["h_in", "qT", "sg", "kL", "vfL", "vdL", "lfL", "padb", "su", "ident", "maskF", "BT",
                 "lam_q1", "lam_k1", "lam_q2", "lam_k2", "subln_g", "rel_bias", "w_branch_fox", "w_branch_diff",
                 "w_out", "ln_mix_g", "ln_mix_b", "w_ffn_gate_up", "w_ffn_down", "ln_ffn_g", "ln_ffn_b",
                 "w_in", "b_fgate"],
            outs=["h_o", "qT_o", "kT_o", "vf_o", "vd_o", "lf_o", "sg_o"],
            internal=["h", "hT", "KX", "KXD", "QX", "yF", "yD"]),
    3: dict(ins=["h_in", "qT", "sg", "kL", "vfL", "vdL", "lfL", "padb", "su", "ident", "maskF", "BT",
                 "lam_q1", "lam_k1", "lam_q2", "lam_k2", "subln_g", "rel_bias", "w_branch_fox", "w_branch_diff",
                 "w_out", "ln_mix_g", "ln_mix_b", "w_router", "w_expert_gate_up", "w_expert_down",
                 "ln_ffn_g", "ln_ffn_b"],
            outs=["out"], internal=["h", "hT", "hT32", "KX", "KXD", "QX", "yF", "yD"]),
}


def _spec(name):
    base = name[:-2] if name.endswith("_o") else ("h" if name == "h_in" else name)
    return W_SPECS[base] if base in W_SPECS else A_SPECS[base]


def build_launch(lid):
    nc = bass.Bass("TRN2", target_bir_lowering=False)
    L = LAUNCH[lid]
    T = {}
    for n in L["ins"]:
        sh, dt = _spec(n)
        T[n] = nc.dram_tensor(n, sh, dt, kind="ExternalInput").ap()
    for n in L["outs"]:
        sh, dt = _spec(n)
        T[n] = nc.dram_tensor(n, sh, dt, kind="ExternalOutput").ap()
    for n in L["internal"]:
        sh, dt = _spec(n)
        T[n] = nc.dram_tensor(n, sh, dt, kind="Internal").ap()
    with ExitStack() as st:
        S = Sched(nc, st)
        c = Ctx(nc, S)
        if lid == 1:
            phase_ln0(c, T)
            phase_inproj(c, T, 0)
        else:
            l = lid - 2
            lam_init = 0.8 - 0.6 * math.exp(-0.3 * l)
            with c.phase() as ph:
                dma(S, "sp", T["h"], T["h_in"], [], ["hcp"], "cp0")
            phase_cum(c, T)
            phase_attn(c, T, l, lam_init)
            phase_post(c, T, l, want_t32=(l == 1))
            if l == 0:
                phase_ffn(c, T)
                T2 = dict(T)
                for n in ("qT", "kT", "vf", "vd", "lf", "sg"):
                    T2[n] = T[n + "_o"]
                phase_inproj(c, T2, 1)
                with c.phase() as ph:
                    dma(S, "sp", T["h_o"], T["h"], [], ["hcp2"], "cp1")
            else:
                phase_moe(c, T)
        S.flush(final=True)
    return nc


_PROGS = {}


def _prog(lid):
    if lid not in _PROGS:
        _PROGS[lid] = build_launch(lid)
    return _PROGS[lid]


def _t5_bucket(n):
    n = np.maximum(n, 0)
    nf = np.maximum(n, 1).astype(np.float32)
    lp = (np.log(nf / np.float32(16)) / np.float32(math.log(8.0)) * np.float32(16)).astype(np.float32)
    large = np.minimum(16 + lp.astype(np.int32), 31)
    return np.where(n < 16, n, large)


def _static_masks(rel_bias):
    k = np.arange(128)[:, None]
    q = np.arange(512)[None, :]
    maskF = np.zeros((5, 128, 512), np.float32)
    BT = np.zeros((4, 6, 128, 512), np.float32)
    for mi in range(6):
        dist = q - k - (mi - 2) * 128
        ok = dist >= 0
        idx = _t5_bucket(dist)
        for h in range(4):
            g = rel_bias[:, h][idx]
            BT[h, mi] = np.where(ok, g, np.float32(NEG))
        if mi >= 1:
            maskF[mi - 1] = np.where(ok, np.float32(0), np.float32(NEG))
    return maskF, BT


def _gather_tokens(parts, axis):
    shp = list(parts[0].shape)
    shp[axis] = S_ALL
    out = np.zeros(shp, parts[0].dtype)
    for c in range(NCORE):
        for s in range(4):
            T_ = 8 * s + c
            src = [slice(None)] * len(shp)
            dst = [slice(None)] * len(shp)
            src[axis] = slice(s * 512, (s + 1) * 512)
            dst[axis] = slice(T_ * 512, (T_ + 1) * 512)
            out[tuple(dst)] = parts[c][tuple(src)]
    return out


def _local_view(glob, axis, c):
    shp = list(glob.shape)
    shp[axis] = NPADB * 128
    padded = np.concatenate([np.zeros(shp, glob.dtype), glob], axis=axis)
    sl = [slice(None)] * len(shp)
    sl[axis] = slice(4 * c * 128, 4 * c * 128 + S_ALL)
    return np.ascontiguousarray(padded[tuple(sl)])


def _run(lid, in_maps):
    nc = _prog(lid)
    res = run_bass_kernel_spmd(nc, in_maps, core_ids=list(range(NCORE)))
    return res.results


def kernel(**inp):
    inp = {k: np.ascontiguousarray(np.asarray(v)) for k, v in inp.items()}
    x = inp["x"].reshape(S_ALL, D)
    ident = np.eye(128, dtype=np.float32).astype(ml_dtypes.bfloat16)
    su = np.triu(np.ones((128, 128), np.float32), 1)
    maskF, BT = _static_masks(inp["rel_bias"].astype(np.float32))
    xs = []
    for c in range(NCORE):
        xs.append(np.concatenate([x[(8 * s + c) * 512:(8 * s + c + 1) * 512] for s in range(4)], axis=0))
    wl = lambda names: {n: inp[n] for n in names if n in W_SPECS}
    L = LAUNCH[1]
    maps = [dict(wl(L["ins"]), x=xs[c], ident=ident) for c in range(NCORE)]
    r = _run(1, maps)
    out = None
    for lid in (2, 3):
        kg = _gather_tokens([r[c]["kT"] if lid == 2 else r[c]["kT_o"] for c in range(NCORE)], 2)
        sfx = "" if lid == 2 else "_o"
        vfg = _gather_tokens([r[c]["vf" + sfx] for c in range(NCORE)], 0)
        vdg = _gather_tokens([r[c]["vd" + sfx] for c in range(NCORE)], 0)
        lfg = _gather_tokens([r[c]["lf" + sfx] for c in range(NCORE)], 0)
        L = LAUNCH[lid]
        maps = []
        for c in range(NCORE):
            m = dict(wl(L["ins"]))
            m["h_in"] = r[c]["h" if lid == 2 else "h_o"]
            m["qT"] = r[c]["qT" + sfx]
            m["sg"] = r[c]["sg" + sfx]
            m["kL"] = _local_view(kg, 2, c)
            m["vfL"] = _local_view(vfg, 0, c)
            m["vdL"] = _local_view(vdg, 0, c)
            m["lfL"] = _local_view(lfg, 0, c)
            pb = np.zeros((S_ALL,), np.float32)
            pb[:max(0, (NPADB - 4 * c)) * 128] = NEG
            m["padb"] = pb
            m["su"], m["ident"], m["maskF"], m["BT"] = su, ident, maskF, BT
            maps.append(m)
        r = _run(lid, maps)
    full = np.zeros((S_ALL, D), np.float32)
    for c in range(NCORE):
        o = np.asarray(r[c]["out"], np.float32)
        for s in range(4):
            T_ = 8 * s + c
            full[T_ * 512:(T_ + 1) * 512] = o[s * 512:(s + 1) * 512]
    return full.reshape(1, S_ALL, D)
```

```python
import math
from contextlib import ExitStack
import numpy as np
import ml_dtypes
import concourse.bass as bass
import concourse.mybir as mybir
from concourse.bass_utils import run_bass_kernel_spmd

F32 = mybir.dt.float32
BF16 = mybir.dt.bfloat16
AF = mybir.ActivationFunctionType
ALU = mybir.AluOpType
AX = mybir.AxisListType


ENGS = ("pe", "act", "dve", "pool", "sp")


class Sched:
    def __init__(self, nc, stack):
        self.nc = nc
        self.stack = stack
        self.sem = {e: stack.enter_context(nc.semaphore("s_" + e)) for e in ENGS}
        self.nsig = {e: 0 for e in ENGS}
        self.gcount = {e: 0 for e in ENGS}
        self.ops = {e: [] for e in ENGS}
        self.recs = {}
        self.ordinal = {}
        self.res = {}
        self.ch = {}
        self._pid = {}

    def pid(self, engine):
        k = id(engine)
        if k not in self._pid:
            self._pid[k] = engine.snap(engine.partition_id())
        return self._pid[k]

    def chan(self, name, unit=16):
        if name not in self.ch:
            self.ch[name] = [self.stack.enter_context(self.nc.semaphore("c_" + name)), 0, unit]
        return self.ch[name]

    def _state(self, key):
        st = self.res.get(key)
        if st is None:
            st = {"lw": None, "rd": []}
            self.res[key] = st
        return st

    def op(self, eng, fn, reads=(), writes=(), dma_ch=None, unit=16):
        waits = []
        seen = set()

        def add(tok, raw):
            if tok is None or tok in seen:
                return
            seen.add(tok)
            if tok[0] == "eng" and tok[1] == eng and dma_ch is None:
                if not raw or eng == "pe":
                    return
            waits.append(tok)

        for k in reads:
            add(self._state(k)["lw"], True)
        for k in writes:
            st = self._state(k)
            add(st["lw"], False)
            for t in st["rd"]:
                add(t, False)
        gidx = self.gcount[eng]
        self.gcount[eng] += 1
        rec = {"fn": fn, "waits": waits, "sig": False, "dma": None, "g": gidx}
        if dma_ch is not None:
            c = self.chan(dma_ch, unit)
            c[1] += 1
            rec["dma"] = (dma_ch, c[1])
            me = ("dma", dma_ch, c[1])
        else:
            me = ("eng", eng, gidx)
            self.recs[(eng, gidx)] = rec
        self.ops[eng].append(rec)
        for k in reads:
            st = self._state(k)
            if me[0] == "eng":
                st["rd"] = [t for t in st["rd"] if not (t[0] == "eng" and t[1] == eng)]
            st["rd"].append(me)
        for k in writes:
            st = self._state(k)
            st["lw"] = me
            st["rd"] = []
        return me

    def flush(self, final=False):
        nc = self.nc
        for e in ENGS:
            for rec in self.ops[e]:
                for t in rec["waits"]:
                    if t[0] == "eng" and (t[1], t[2]) in self.recs:
                        self.recs[(t[1], t[2])]["sig"] = True
        last = {}
        for e in ENGS:
            for rec in self.ops[e]:
                if rec["dma"] is None:
                    last[e] = rec
            if e in last:
                last[e]["sig"] = True
        for e in ENGS:
            n = self.nsig[e]
            for rec in self.ops[e]:
                if rec["sig"]:
                    n += 1
                    self.ordinal[(e, rec["g"])] = n
            self.nsig[e] = n
        ops, sem, ch, ordinal = self.ops, self.sem, self.ch, self.ordinal
        self.ops = {e: [] for e in ENGS}
        self.recs = {}
        def collapse(t):
            if t is not None and t[0] == "eng" and t[1] in last:
                return ("eng", t[1], last[t[1]]["g"])
            return t
        for st in self.res.values():
            st["lw"] = collapse(st["lw"])
            st["rd"] = list({collapse(t) for t in st["rd"]})
        final_ch = [(c[0], c[2] * c[1]) for c in ch.values() if c[1] > 0] if final else []

        prev_bar = getattr(self, "bar", None)
        self.bar = ({e2: self.nsig[e2] for e2 in ENGS}, {k: c[2] * c[1] for k, c in ch.items()})

        def run(e, engine):
            waited = {}
            if prev_bar is not None:
                for e2, v in prev_bar[0].items():
                    if e2 != e and v > 0:
                        engine.wait_ge(sem[e2], v)
                        waited["e" + e2] = v
                for k, v in prev_bar[1].items():
                    if v > 0:
                        engine.wait_ge(ch[k][0], v)
                        waited["c" + k] = v
            for rec in ops[e]:
                need = {}
                for t in rec["waits"]:
                    if t[0] == "eng":
                        s, v, key = sem[t[1]], ordinal[(t[1], t[2])], "e" + t[1]
                    else:
                        s, v, key = ch[t[1]][0], ch[t[1]][2] * t[2], "c" + t[1]
                    if waited.get(key, 0) >= v:
                        continue
                    if key not in need or need[key][1] < v:
                        need[key] = (s, v)
                for key, (s, v) in need.items():
                    waited[key] = v
                    engine.wait_ge(s, v)
                ins = rec["fn"](engine)
                if rec["dma"] is not None:
                    ins.then_inc(ch[rec["dma"][0]][0], ch[rec["dma"][0]][2])
                elif rec["sig"]:
                    ins.then_inc(sem[e], 1)
            if e == "sp":
                for s, v in final_ch:
                    engine.wait_ge(s, v)

        with nc.Block() as block:
            @block.tensor
            def _(pe):
                run("pe", pe)

            @block.scalar
            def _(act):
                run("act", act)

            @block.vector
            def _(dve):
                run("dve", dve)

            @block.gpsimd
            def _(pool):
                run("pool", pool)

            @block.sync
            def _(sp):
                run("sp", sp)

D = 1024
S_ALL = 16384
NCORE = 8
TL = 2048
NIN = 5128
DFF = 2816
DFE = 3584
NE = 8
LN_EPS = 1e-5
ALPHA = 4.0 ** 0.25
NEG = -30000.0
DBG = set()
NPADB = 28
C_FQ, C_FK, C_FV, C_FG, C_DQ, C_DK, C_DV, C_GA, C_GB = 0, 512, 1024, 1536, 1544, 2056, 2568, 3080, 4104


class Ctx:
    def __init__(self, nc, S):
        self.nc = nc
        self.S = S
        self.uid = 0
        self.dram = {}

    def phase(self):
        return Phase(self)


class Phase:
    def __init__(self, c):
        self.c = c
        self.st = ExitStack()

    def __enter__(self):
        self.st.__enter__()
        return self

    def sb(self, name, shape, dt):
        self.c.uid += 1
        return self.st.enter_context(self.c.nc.sbuf_tensor("%s_%d" % (name, self.c.uid), shape, dt))

    def ps(self, name, shape, dt=F32):
        self.c.uid += 1
        return self.st.enter_context(self.c.nc.psum_tensor("%s_%d" % (name, self.c.uid), shape, dt))

    def __exit__(self, *a):
        if a[0] is None:
            self.c.S.flush()
        return self.st.__exit__(*a)


def dma(S, eng, out, in_, reads, writes, ch):
    def fn(e):
        o = out(e) if callable(out) else out
        i = in_(e) if callable(in_) else in_
        return e.dma_start(out=o, in_=i)
    return S.op(eng, fn, reads=reads, writes=writes, dma_ch=ch)


def bcast_rows(ap1d, n, parts=128):
    return ap1d.rearrange("(o n) -> o n", o=1).broadcast_to([parts, n])


class LNUnit:
    def __init__(self, c, ph, g_ap, b_ap, ident_ap, want_t32=False, tag="ln", with_t=True):
        S = c.S
        self.c, self.ph, self.tag = c, ph, tag
        self.gB = ph.sb("gB", [128, D], F32)
        self.bB = ph.sb("bB", [128, D], F32)
        self.ident = ph.sb("ident", [128, 128], BF16)
        dma(S, "sp", self.gB[:], bcast_rows(g_ap, D), [], [tag + "gB"], tag + "c0")
        dma(S, "sp", self.bB[:], bcast_rows(b_ap, D), [], [tag + "bB"], tag + "c0")
        dma(S, "pool", self.ident[:], ident_ap, [], [tag + "id"], tag + "c1")
        self.junk = ph.sb("junk", [128, D], F32)
        self.st = [ph.sb("st%d" % i, [128, 8], F32) for i in range(2)]
        self.y = [ph.sb("y%d" % i, [128, D], F32) for i in range(2)]
        if with_t:
            self.yb = [ph.sb("yb%d" % i, [128, D], BF16) for i in range(2)]
            self.tp = ph.ps("tp", [128, 8, 128], BF16)
            self.ts = [ph.sb("ts%d" % i, [128, 8, 128], BF16) for i in range(2)]
        else:
            self.yb = [None, None]
        self.want_t32 = want_t32
        if want_t32:
            self.ident32 = ph.sb("ident32", [128, 128], F32)
            S.op("dve", lambda e: e.tensor_copy(out=self.ident32[:], in_=self.ident[:]),
                 reads=[tag + "id"], writes=[tag + "id32"])
            self.tp32 = ph.ps("tp32", [128, 4, 128], F32)
            self.ts32 = [ph.sb("ts32%d" % i, [128, 8, 128], F32) for i in range(2)]
        self.n = 0

    def run(self, z, zkey, h_out_ap, hT_out, tok0, hT32_out=None, do_t=True):
        S, tag = self.c.S, self.tag
        i = self.n % 2
        self.n += 1
        st, y, yb, junk = self.st[i], self.y[i], self.yb[i], self.junk
        kst, ky, kyb = "%sst%d" % (tag, i), "%sy%d" % (tag, i), "%syb%d" % (tag, i)
        S.op("act", lambda e: e.activation(out=junk[:], in_=z, func=AF.Identity, accum_out=st[:, 0:1]),
             reads=[zkey], writes=[tag + "junk", kst])
        S.op("act", lambda e: e.activation(out=junk[:], in_=z, func=AF.Square, accum_out=st[:, 1:2]),
             reads=[zkey], writes=[tag + "junk", kst])
        S.op("dve", lambda e: e.tensor_scalar(out=st[:, 2:4], in0=st[:, 0:2], scalar1=1.0 / D, scalar2=None,
                                              op0=ALU.mult), reads=[kst], writes=[kst])
        S.op("dve", lambda e: e.tensor_tensor(out=st[:, 4:5], in0=st[:, 2:3], in1=st[:, 2:3], op=ALU.mult),
             reads=[kst], writes=[kst])
        S.op("dve", lambda e: e.tensor_tensor(out=st[:, 5:6], in0=st[:, 3:4], in1=st[:, 4:5], op=ALU.subtract),
             reads=[kst], writes=[kst])
        S.op("act", lambda e: e.activation(out=st[:, 6:7], in_=st[:, 5:6], func=AF.Sqrt, bias=self.epsc[:], scale=1.0),
             reads=[kst, tag + "eps"], writes=[kst])
        S.op("dve", lambda e: e.reciprocal(out=st[:, 6:7], in_=st[:, 6:7]), reads=[kst], writes=[kst])
        S.op("dve", lambda e: e.scalar_tensor_tensor(out=st[:, 7:8], in0=st[:, 2:3], scalar=-1.0, in1=st[:, 6:7],
                                                     op0=ALU.mult, op1=ALU.mult), reads=[kst], writes=[kst])
        S.op("act", lambda e: e.activation(out=y[:], in_=z, func=AF.Identity, bias=st[:, 7:8], scale=st[:, 6:7]),
             reads=[zkey, kst], writes=[ky])
        S.op("dve", lambda e: e.tensor_tensor(out=y[:], in0=y[:], in1=self.gB[:], op=ALU.mult),
             reads=[ky, tag + "gB"], writes=[ky])
        S.op("dve", lambda e: e.tensor_tensor(out=y[:], in0=y[:], in1=self.bB[:], op=ALU.add),
             reads=[ky, tag + "bB"], writes=[ky])
        dma(S, "sp", h_out_ap, y[:], [ky], [("hdram", id(h_out_ap))], tag + "ho%d" % i)
        if not do_t:
            return
        S.op("pool", lambda e: e.tensor_copy(out=yb[:], in_=y[:]), reads=[ky], writes=[kyb])
        for k in range(8):
            S.op("pe", lambda e, k=k: e.transpose(out=self.tp[:, k, :], in_=yb[:, k * 128:(k + 1) * 128],
                                                  identity=self.ident[:]),
                 reads=[kyb, tag + "id"], writes=[tag + "tp"])
        ts = self.ts[i]
        S.op("act", lambda e: e.copy(out=ts[:], in_=self.tp[:]), reads=[tag + "tp"], writes=[tag + "ts%d" % i])
        dma(S, "sp", hT_out.rearrange("(k p) t -> p k t", p=128)[:, :, tok0:tok0 + 128], ts[:],
            [tag + "ts%d" % i], [("hT", tok0)], tag + "to%d" % i)
        if self.want_t32 and hT32_out is not None:
            ts32 = self.ts32[i]
            for hf in range(2):
                for k in range(4):
                    kk = hf * 4 + k
                    S.op("pe", lambda e, k=k, kk=kk: e.transpose(out=self.tp32[:, k, :],
                                                                 in_=y[:, kk * 128:(kk + 1) * 128],
                                                                 identity=self.ident32[:]),
                         reads=[ky, tag + "id32"], writes=[tag + "tp32"])
                S.op("dve", lambda e, hf=hf: e.tensor_copy(out=ts32[:, hf * 4:(hf + 1) * 4, :], in_=self.tp32[:]),
                     reads=[tag + "tp32"], writes=[tag + "ts32%d" % i])
            dma(S, "sp", hT32_out.rearrange("(k p) t -> p k t", p=128)[:, :, tok0:tok0 + 128], ts32[:],
                [tag + "ts32%d" % i], [("hT32", tok0)], tag + "t32o%d" % i)

    def setup_eps(self):
        S, tag = self.c.S, self.tag
        self.epsc = self.ph.sb("epsc", [128, 1], F32)
        S.op("dve", lambda e: e.memset(self.epsc[:], LN_EPS), writes=[tag + "eps"])


def phase_ln0(c, T):
    S = c.S
    with c.phase() as ph:
        ln = LNUnit(c, ph, T["ln_in_g"], T["ln_in_b"], T["ident"], tag="l0")
        ln.setup_eps()
        xb = [ph.sb("xb%d" % i, [128, D], F32) for i in range(2)]
        for tb in range(16):
            i = tb % 2
            dma(S, "sp", xb[i][:], T["x"][tb * 128:(tb + 1) * 128, :], [], ["xb%d" % i], "l0x%d" % i)
            ln.run(xb[i][:], "xb%d" % i, T["h"][tb * 128:(tb + 1) * 128, :], T["hT"], tb * 128)


def phase_inproj(c, T, l):
    S = c.S
    w = T["w_in"][l]
    with c.phase() as ph:
        hTs = ph.sb("hTs", [128, 8, TL], BF16)
        dma(S, "sp", hTs[:], T["hT"].rearrange("(k p) t -> p k t", p=128), [("hT", "all")], ["hTs"], "ip_h")
        wb = [ph.sb("wb%d" % i, [128, 8, 512], BF16) for i in range(3)]
        stg = [ph.sb("stg%d" % i, [128, TL], BF16) for i in range(2)]
        vst = [ph.sb("vst%d" % i, [128, 512], BF16) for i in range(2)]
        pss = [ph.ps("ps%d" % i, [128, 512]) for i in range(4)]
        wi = [0]
        pi = [0]

        def load_w(c0, ncols):
            b = wi[0] % 3
            wi[0] += 1
            dma(S, "pool", wb[b][:, :, 0:ncols], w[:, c0:c0 + ncols].rearrange("(k p) n -> p k n", p=128),
                [], ["wb%d" % b], "ip_w%d" % b)
            return b

        si = [0]
        qflat = T["qT"].rearrange("m r t -> (m r) t")
        kflat = T["kT"].rearrange("m r t -> (m r) t") if "kT" in T else None
        sgflat = T["sg"].rearrange("g r t -> (g r) t")
        segs = []
        for j in range(1):
            segs.append((C_FQ, qflat, 0, "q"))
            segs.append((C_FK, kflat, 0, "k"))
            segs.append((C_DQ, qflat, 512, "q"))
            segs.append((C_DK, kflat, 512, "k"))
            segs.append((C_GA, sgflat, 0, "g"))
            segs.append((C_GA + 512, sgflat, 512, "g"))
            segs.append((C_GB, sgflat, 1024, "g"))
            segs.append((C_GB + 512, sgflat, 1536, "g"))
        for (c0, dst, r0, kind) in segs:
            b = load_w(c0, 512)
            for m in range(4):
                sb_i = si[0] % 2
                si[0] += 1
                for tt in range(4):
                    p = pi[0] % 4
                    pi[0] += 1
                    for k in range(8):
                        S.op("pe", lambda e, b=b, k=k, m=m, p=p, tt=tt: e.matmul(
                            pss[p][:], lhsT=wb[b][:, k, m * 128:(m + 1) * 128],
                            rhs=hTs[:, k, tt * 512:(tt + 1) * 512], start=(k == 0), stop=(k == 7)),
                            reads=["hTs", "wb%d" % b], writes=["ipps%d" % p])
                    o = stg[sb_i][:, tt * 512:(tt + 1) * 512]
                    if kind == "q":
                        S.op("act", lambda e, o=o, p=p: e.activation(out=o, in_=pss[p][:], func=AF.Copy, scale=0.125),
                             reads=["ipps%d" % p], writes=["stg%d" % sb_i])
                    elif kind == "k":
                        S.op("dve", lambda e, o=o, p=p: e.tensor_copy(out=o, in_=pss[p][:]),
                             reads=["ipps%d" % p], writes=["stg%d" % sb_i])
                    else:
                        S.op("act", lambda e, o=o, p=p: e.activation(out=o, in_=pss[p][:], func=AF.Sigmoid),
                             reads=["ipps%d" % p], writes=["stg%d" % sb_i])
                rr = r0 + m * 128
                if kind == "k" and "pack" in T:
                    dma(S, "sp", T["pack"][:, rr:rr + 128, :].rearrange("s r i -> r s i"),
                        stg[sb_i][:].rearrange("p (s i) -> p s i", s=4), ["stg%d" % sb_i], [("fm", "pack", rr)],
                        "ip_s%d" % sb_i)
                else:
                    dma(S, "sp", dst[rr:rr + 128, :], stg[sb_i][:], ["stg%d" % sb_i], [("fm", id(dst), rr)],
                        "ip_s%d" % sb_i)
        vi = [0]
        for vi_, (c0, dst) in enumerate(((C_FV, T.get("vf")), (C_DV, T.get("vd")))):
            b = load_w(c0, 512)
            for tb in range(16):
                p = pi[0] % 4
                pi[0] += 1
                for k in range(8):
                    S.op("pe", lambda e, b=b, k=k, p=p, tb=tb: e.matmul(
                        pss[p][:], lhsT=hTs[:, k, tb * 128:(tb + 1) * 128], rhs=wb[b][:, k, :],
                        start=(k == 0), stop=(k == 7)), reads=["hTs", "wb%d" % b], writes=["ipps%d" % p])
                v = vi[0] % 2
                vi[0] += 1
                eng = "act" if tb % 2 == 0 else "dve"
                if eng == "act":
                    S.op("act", lambda e, v=v, p=p: e.copy(out=vst[v][:], in_=pss[p][:]),
                         reads=["ipps%d" % p], writes=["vst%d" % v])
                else:
                    S.op("dve", lambda e, v=v, p=p: e.tensor_copy(out=vst[v][:], in_=pss[p][:]),
                         reads=["ipps%d" % p], writes=["vst%d" % v])
                if "pack" in T:
                    r0_ = 1024 + 512 * vi_ + (tb % 4) * 128
                    dma(S, "sp", T["pack"][tb // 4, r0_:r0_ + 128, :], vst[v][:], ["vst%d" % v], [("tm", vi_, tb)],
                        "ip_v%d" % v)
                else:
                    dma(S, "sp", dst[tb * 128:(tb + 1) * 128, :], vst[v][:], ["vst%d" % v], [("tm", id(dst), tb)],
                        "ip_v%d" % v)
        b = load_w(C_FG, 8)
        bf = ph.sb("bf", [128, 8], F32)
        dma(S, "sp", bf[:], bcast_rows(T["b_fgate"][l], 8), [], ["bf"], "ip_bf")
        lfs = ph.sb("lfs", [128, 16, 8], F32)
        t1 = ph.sb("t1", [128, 16, 8], F32)
        for tb in range(16):
            p = pi[0] % 4
            pi[0] += 1
            for k in range(8):
                S.op("pe", lambda e, b=b, k=k, p=p, tb=tb: e.matmul(
                    pss[p][:, 0:8], lhsT=hTs[:, k, tb * 128:(tb + 1) * 128], rhs=wb[b][:, k, 0:8],
                    start=(k == 0), stop=(k == 7)), reads=["hTs", "wb%d" % b], writes=["ipps%d" % p])
            S.op("dve", lambda e, p=p, tb=tb: e.tensor_tensor(out=t1[:, tb, :], in0=pss[p][:, 0:8], in1=bf[:],
                                                              op=ALU.add),
                 reads=["ipps%d" % p, "bf"], writes=["t1"])
        S.op("act", lambda e: e.activation(out=t1[:], in_=t1[:], func=AF.Exp, scale=-1.0), reads=["t1"], writes=["t1"])
        S.op("act", lambda e: e.activation(out=t1[:], in_=t1[:], func=AF.Ln, bias=1.0), reads=["t1"], writes=["t1"])
        S.op("dve", lambda e: e.tensor_scalar(out=lfs[:], in0=t1[:], scalar1=-1.0, scalar2=None, op0=ALU.mult),
             reads=["t1"], writes=["lfs"])
        dma(S, "sp", T["lf"].rearrange("(tb p) h -> p tb h", p=128), lfs[:], ["lfs"], ["lfd"], "ip_lf")


def phase_cum(c, T):
    S = c.S
    with c.phase() as ph:
        L = ph.sb("L", [128, 128, 8], F32)
        if "lfg" in T:
            dma(S, "act", L[:], lambda e: T["lfg"][bass.ds(S.pid(e) * 512, S_ALL), :].rearrange(
                "(j t) h -> j t h", t=128), [], ["L"], "cu0")
        else:
            dma(S, "sp", L[:], T["lfL"].rearrange("(j t) h -> j t h", t=128), [], ["L"], "cu0")
        pb = ph.sb("pb", [128, 128], F32)
        if "padsrc" in T:
            dma(S, "act", pb[:], lambda e: T["padsrc"][bass.ds(S.pid(e) * 512, S_ALL)].rearrange(
                "(j t) -> j t", t=128), [], ["pb"], "cu0")
        else:
            dma(S, "sp", pb[:], T["padb"].rearrange("(j t) -> j t", t=128), [], ["pb"], "cu0")
        su = ph.sb("su", [128, 128], F32)
        dma(S, "sp", su[:], T["su"], [], ["su"], "cu0")
        A = ph.sb("A", [128, 8, 128], F32)
        B = ph.sb("B", [128, 8, 128], F32)
        S.op("dve", lambda e: e.tensor_copy(out=A[:], in_=L[:].rearrange("j t h -> j h t")), reads=["L"], writes=["A"])
        cur, nxt, kc, kn = A, B, "A", "B"
        sft = 1
        while sft < 128:
            S.op("dve", lambda e, cur=cur, nxt=nxt, sft=sft: e.tensor_tensor(
                out=nxt[:, :, sft:128], in0=cur[:, :, sft:128], in1=cur[:, :, 0:128 - sft], op=ALU.add),
                reads=[kc], writes=[kn])
            S.op("dve", lambda e, cur=cur, nxt=nxt, sft=sft: e.tensor_copy(out=nxt[:, :, 0:sft], in_=cur[:, :, 0:sft]),
                 reads=[kc], writes=[kn])
            cur, nxt, kc, kn = nxt, cur, kn, kc
            sft *= 2
        bs = ph.sb("bs", [128, 8], F32)
        S.op("dve", lambda e: e.tensor_copy(out=bs[:], in_=cur[:, :, 127]), reads=[kc], writes=["bs"])
        pso = ph.ps("pso", [128, 8])
        S.op("pe", lambda e: e.matmul(pso[:], lhsT=su[:], rhs=bs[:], start=True, stop=True),
             reads=["su", "bs"], writes=["pso"])
        off = ph.sb("off", [128, 8], F32)
        S.op("dve", lambda e: e.tensor_copy(out=off[:], in_=pso[:]), reads=["pso"], writes=["off"])
        cum = nxt
        for h in range(8):
            S.op("dve", lambda e, h=h: e.tensor_scalar(out=cum[:, h, :], in0=cur[:, h, :], scalar1=off[:, h:h + 1],
                                                       scalar2=None, op0=ALU.add), reads=[kc, "off"], writes=[kn])
        negc = cur
        for h in range(8):
            S.op("dve", lambda e, h=h: e.scalar_tensor_tensor(out=negc[:, h, :], in0=cum[:, h, :], scalar=-1.0,
                                                              in1=pb[:], op0=ALU.mult, op1=ALU.add),
                 reads=[kn, "pb"], writes=[kc])
        KXs = ph.sb("KXs", [128, 8, 4, 128], BF16)
        r1 = ph.sb("r1", [128, 8, 128], F32)
        S.op("pool", lambda e: e.memset(KXs[:, :, 0, :], 1.0), writes=["KX0"])
        S.op("dve", lambda e: e.tensor_copy(out=KXs[:, :, 1, :], in_=negc[:]), reads=[kc], writes=["KX1"])
        S.op("dve", lambda e: e.tensor_tensor(out=r1[:], in0=negc[:], in1=KXs[:, :, 1, :], op=ALU.subtract),
             reads=[kc, "KX1"], writes=["r1"])
        S.op("dve", lambda e: e.tensor_copy(out=KXs[:, :, 2, :], in_=r1[:]), reads=["r1"], writes=["KX2"])
        S.op("dve", lambda e: e.tensor_tensor(out=r1[:], in0=r1[:], in1=KXs[:, :, 2, :], op=ALU.subtract),
             reads=["r1", "KX2"], writes=["r1"])
        S.op("dve", lambda e: e.tensor_copy(out=KXs[:, :, 3, :], in_=r1[:]), reads=["r1"], writes=["KX3"])
        dma(S, "sp", T["KX"].rearrange("h r (j t) -> j h r t", t=128), KXs[:], ["KX0", "KX1", "KX2", "KX3"],
            ["KXd"], "cu1")
        KD = ph.sb("KD", [128, 4, 128], BF16)
        S.op("pool", lambda e: e.memset(KD[:], 0.0), writes=["KD"])
        S.op("dve", lambda e: e.tensor_copy(out=KD[:, 0, :], in_=pb[:]), reads=["pb", "KD"], writes=["KD"])
        dma(S, "sp", T["KXD"].rearrange("r (j t) -> j r t", t=128), KD[:], ["KD"], ["KXDd"], "cu1")
        cb = ph.sb("cb", [128, 8, 128], BF16)
        S.op("dve", lambda e: e.tensor_copy(out=cb[:], in_=cum[:]), reads=[kn], writes=["cb"])
        for s in range(4):
            j0 = 32 * s + NPADB
            dma(S, "sp", T["QX"][:, s * 512:(s + 1) * 512].rearrange("h (i t) -> i h t", t=128),
                cb[j0:j0 + 4, :, :], ["cb"], ["QXd"], "cu1")


def phase_attn(c, T, l, lam_init, fox_heads=tuple(range(8)), diff_heads=tuple(range(4)), loads_only=False):
    S = c.S
    with c.phase() as ph:
        kt = [ph.sb("kt%d" % i, [128, S_ALL], BF16) for i in range(2)]
        vt = [ph.sb("vt%d" % i, [128, 128, 128], BF16) for i in range(2)]
        qf = [ph.sb("qf%d" % i, [128, TL], BF16) for i in range(2)]
        pt = [ph.sb("pt%d" % i, [128, 1024], BF16) for i in range(4)]
        mk = ph.sb("mk", [128, 6, 512], F32)
        yst = [ph.sb("yst%d" % i, [128, TL], BF16) for i in range(2)]
        onesf = ph.sb("onesf", [128, 128], F32)
        rr = ph.sb("rr", [128, 512], F32)
        bcs = ph.sb("bcs", [128, 512], F32)
        a1 = ph.sb("a1", [128, 512], F32)
        a2 = ph.sb("a2", [128, 512], F32)
        sq = ph.sb("sq", [128, 512], F32)
        accL = [ph.sb("accL%d" % i, [128, 1024], F32) for i in range(2)]
        cols = ph.sb("cols", [128, 16], F32)
        lmt = ph.sb("lmt", [128, 4, 64], F32)
        lmj = ph.sb("lmj", [128, 64], F32)
        s2 = [ph.ps("s2%d" % i, [128, 1024]) for i in range(2)]
        p1 = [ph.ps("p1%d" % i, [128, 512]) for i in range(4)]
        S.op("pool", lambda e: e.memset(onesf[:], 1.0), writes=["onesf"])
        for i in range(2):
            S.op("pool", lambda e, i=i: e.memset(vt[i][:, :, 64:128], 0.0), writes=["vt%d" % i])
            S.op("pool", lambda e, i=i: e.memset(vt[i][:, :, 64:65], 1.0), writes=["vt%d" % i])
            S.op("pool", lambda e, i=i: e.memset(qf[i][64:128, :], 0.0), writes=["qf%d" % i])
            S.op("pool", lambda e, i=i: e.memset(qf[i][64:68, :], 1.0), writes=["qf%d" % i])
            S.op("pool", lambda e, i=i: e.memset(kt[i][64:128, :], 0.0), writes=["kt%d" % i])
        for i, nm in enumerate(("lam_q1", "lam_k1", "lam_q2", "lam_k2")):
            dma(S, "sp", lmt[:, i, :], bcast_rows(T[nm][l], 64), [], ["lmt"], "at_c")
        dma(S, "sp", cols[:, 8:12], bcast_rows(T["rel_bias"][31], 4), [], ["cols_b"], "at_c")
        dma(S, "sp", cols[:, 4:5], T["subln_g"][l].rearrange("(p o) -> p o", o=1), [], ["cols_g"], "at_c")
        for i in range(2):
            S.op("dve", lambda e, i=i: e.tensor_tensor(out=lmj[:], in0=lmt[:, 2 * i, :], in1=lmt[:, 2 * i + 1, :],
                                                       op=ALU.mult), reads=["lmt"], writes=["lmj"])
            S.op("dve", lambda e, i=i: e.tensor_reduce(out=cols[:, i:i + 1], in_=lmj[:], axis=AX.X, op=ALU.add),
                 reads=["lmj"], writes=["cols_l"])
        S.op("act", lambda e: e.activation(out=cols[:, 0:2], in_=cols[:, 0:2], func=AF.Exp),
             reads=["cols_l"], writes=["cols_l"])
        S.op("dve", lambda e: e.tensor_tensor(out=cols[:, 2:3], in0=cols[:, 1:2], in1=cols[:, 0:1], op=ALU.subtract),
             reads=["cols_l"], writes=["cols_l"])
        S.op("dve", lambda e: e.tensor_scalar(out=cols[:, 3:4], in0=cols[:, 2:3], scalar1=-float(lam_init),
                                              scalar2=None, op0=ALU.add), reads=["cols_l"], writes=["cols_n"])
        S.op("dve", lambda e: e.tensor_scalar(out=cols[:, 5:6], in0=cols[:, 4:5], scalar1=float(1.0 - lam_init),
                                              scalar2=None, op0=ALU.mult), reads=["cols_g"], writes=["cols_g2"])
        S.op("dve", lambda e: e.memset(cols[:, 6:7], 1e-5), writes=["cols_e"])

        def load_k(buf, m, fox_h):
            if "GL" in T:
                dma(S, "sp", kt[buf][0:64, :].rearrange("d (t i) -> d t i", i=512),
                    T["GL"][:, m * 64:(m + 1) * 64, :].rearrange("t d i -> d t i"), [], ["kt%d" % buf], "at_k%d" % buf)
            else:
                dma(S, "sp", kt[buf][0:64, :], T["kL"][m], [], ["kt%d" % buf], "at_k%d" % buf)
            if fox_h is not None:
                dma(S, "sp", kt[buf][64:68, :], T["KX"][fox_h], [], ["kt%d" % buf], "at_k%d" % buf)
            else:
                dma(S, "sp", kt[buf][64:68, :], T["KXD"], [], ["kt%d" % buf], "at_k%d" % buf)

        def load_q(buf, m, fox_h):
            dma(S, "sp", qf[buf][0:64, :], T["qT"][m], [], ["qf%d" % buf], "at_q%d" % buf)
            if fox_h is not None:
                dma(S, "sp", qf[buf][64:65, :], T["QX"][fox_h:fox_h + 1, :], [], ["qf%d" % buf], "at_q%d" % buf)

        def load_v(buf, col0, ncol, row0):
            if "GL" in T:
                for bb in range(4):
                    dma(S, "sp", vt[buf][:, :, 0:ncol].rearrange("p (t b) d -> p t b d", b=4)[:, :, bb, :],
                        T["GL"][:, row0 + bb * 128:row0 + 128 + bb * 128, col0:col0 + ncol].rearrange("t p d -> p t d"),
                        [], ["vt%d" % buf], "at_v%d" % buf)
            else:
                src = T["vfL"] if row0 == 1024 else T["vdL"]
                dma(S, "sp", vt[buf][:, :, 0:ncol], src[:, col0:col0 + ncol].rearrange("(j p) d -> p j d", p=128),
                    [], ["vt%d" % buf], "at_v%d" % buf)

        jc = [0]

        def run_pipeline(jobs):
            n = len(jobs)
            for j in range(min(2, n)):
                jobs[j]["S"]()
                jobs[j]["A"]()
            for j in range(n):
                jobs[j]["PV"]()
                if j + 2 < n:
                    jobs[j + 2]["S"]()
                    jobs[j + 2]["A"]()
                if "end" in jobs[j]:
                    jobs[j]["end"]()

        def mk_job(kb, qb, s, pi, NB, near_from, far_bias, pv_fn, end_fn=None):
            j = jc[0]
            jc[0] += 1
            sp_, sk = s2[j % 2], "s2%d" % (j % 2)
            pb_ = j % 4
            blks = (2 * pi, 2 * pi + 1)
            is_near = blks[1] >= near_from

            def fS():
                for hf, blk in enumerate(blks):
                    S.op("pe", lambda e, blk=blk, hf=hf: e.matmul(
                        sp_[:, hf * 512:(hf + 1) * 512], lhsT=kt[kb][:, blk * 128:(blk + 1) * 128],
                        rhs=qf[qb][:, s * 512:(s + 1) * 512], start=True, stop=True),
                        reads=["kt%d" % kb, "qf%d" % qb], writes=[sk])
                    mi = blk - (NB - 6)
                    if blk >= near_from:
                        S.op("dve", lambda e, hf=hf, mi=mi: e.tensor_tensor(
                            out=sp_[:, hf * 512:(hf + 1) * 512], in0=sp_[:, hf * 512:(hf + 1) * 512],
                            in1=mk[:, mi, :], op=ALU.add), reads=[sk, "mk"], writes=[sk])

            def fA():
                if True:
                    S.op("act", lambda e: e.activation(out=pt[pb_][:], in_=sp_[:], func=AF.Exp),
                         reads=[sk], writes=["pt%d" % pb_])
                else:
                    S.op("act", lambda e: e.activation(out=pt[pb_][:], in_=sp_[:], func=AF.Exp, bias=far_bias),
                         reads=[sk, "cols_b"], writes=["pt%d" % pb_])

            d = {"S": fS, "A": fA, "PV": lambda: pv_fn(pb_, blks)}
            if end_fn is not None:
                d["end"] = end_fn
            return d

        dma(S, "sp", mk[:, 1:6, :], T["maskF"].rearrange("m k q -> k m q"), [], ["mk"], "at_m")
        oi = [0]
        for hh in fox_heads:
            b = hh % 2
            load_k(b, hh, hh)
            load_q(b, hh, hh)
            load_v(b, hh * 64, 64, 1024)
            yb = hh % 2
            jobs = []
            for s in range(4):
                NB = 32 * s + 32
                o = p1[oi[0] % 2]
                ok = "p1%d" % (oi[0] % 2)
                oi[0] += 1

                def pv(pb_, blks, o=o, ok=ok, b=b, NB=NB):
                    for hf, blk in enumerate(blks):
                        S.op("pe", lambda e, blk=blk, hf=hf: e.matmul(
                            o[:, :], lhsT=vt[b][:, blk, :], rhs=pt[pb_][:, hf * 512:(hf + 1) * 512],
                            start=(blk == 0), stop=(blk == NB - 1)),
                            reads=["vt%d" % b, "pt%d" % pb_], writes=[ok])

                def end(o=o, ok=ok, s=s, yb=yb):
                    S.op("dve", lambda e: e.reciprocal(out=rr[64:65, :], in_=o[64:65, :]), reads=[ok], writes=["rr"])
                    bc = p1[2]
                    S.op("pe", lambda e: e.matmul(bc[0:64, :], lhsT=onesf[64:65, 0:64], rhs=rr[64:65, :],
                                                  start=True, stop=True), reads=["onesf", "rr"], writes=["p12"])
                    S.op("act", lambda e: e.copy(out=bcs[0:64, :], in_=bc[0:64, :]), reads=["p12"], writes=["bcs"])
                    S.op("dve", lambda e: e.tensor_tensor(
                        out=yst[yb][0:64, s * 512:(s + 1) * 512], in0=o[0:64, :], in1=bcs[0:64, :], op=ALU.mult),
                        reads=[ok, "bcs"], writes=["yst%d" % yb])

                for pi in range(NB // 2):
                    last = (pi == NB // 2 - 1)
                    jobs.append(mk_job(b, b, s, pi, NB, NB - 5, None, pv, end if last else None))
            if loads_only:
                continue
            run_pipeline(jobs)
            dma(S, "sp", T["yF"][hh], yst[yb][0:64, :], ["yst%d" % yb], ["yFd"], "at_y%d" % yb)

        for h in diff_heads:
            dma(S, "sp", mk[:], T["BT"][h].rearrange("m k q -> k m q"), [], ["mk"], "at_m")
            S.op("dve", lambda e, h=h: e.tensor_scalar(out=mk[:], in0=mk[:], scalar1=cols[:, 8 + h:9 + h], scalar2=None,
                                                       op0=ALU.subtract), reads=["mk", "cols_b"], writes=["mk"])
            for i in range(2):
                load_k(i, 8 + 2 * h + i, None)
                load_q(i, 8 + 2 * h + i, None)
                if h == diff_heads[0]:
                    S.op("pool", lambda e, i=i: e.memset(qf[i][64:68, :], 1.0), reads=[], writes=["qf%d" % i])
            vb = h % 2
            load_v(vb, h * 128, 128, 1536)
            yb = h % 2
            jobs = []
            for s in range(4):
                NB = 32 * s + 32

                def mkpv(i, NB=NB, vb=vb):
                    def pv(pb_, blks):
                        for hf, blk in enumerate(blks):
                            S.op("pe", lambda e, blk=blk, hf=hf: e.matmul(
                                p1[i][:], lhsT=vt[vb][:, blk, :], rhs=pt[pb_][:, hf * 512:(hf + 1) * 512],
                                start=(blk == 0), stop=(blk == NB - 1)),
                                reads=["vt%d" % vb, "pt%d" % pb_], writes=["p1%d" % i])
                        eng = "dve" if (i == 0 or "NO_POOL_ACC" in DBG) else "pool"
                        if blks[0] == 0:
                            S.op(eng, lambda e: e.tensor_copy(out=accL[i][:], in_=pt[pb_][:]),
                                 reads=["pt%d" % pb_], writes=["accL%d" % i])
                        else:
                            S.op(eng, lambda e: e.tensor_tensor(out=accL[i][:], in0=accL[i][:], in1=pt[pb_][:], op=ALU.add),
                                 reads=["pt%d" % pb_, "accL%d" % i], writes=["accL%d" % i])
                    return pv

                def end(s=s, yb=yb):
                    for i in range(2):
                        S.op("dve", lambda e, i=i: e.tensor_tensor(out=sq[:], in0=accL[i][:, 0:512], in1=accL[i][:, 512:1024],
                                                                   op=ALU.add), reads=["accL%d" % i], writes=["sq"])
                        S.op("pe", lambda e, i=i: e.matmul(p1[2 + i][:], lhsT=onesf[:], rhs=sq[:], start=True, stop=True),
                             reads=["onesf", "sq"], writes=["p1%d" % (2 + i)])
                    S.op("dve", lambda e: e.reciprocal(out=rr[:], in_=p1[2][:]), reads=["p12"], writes=["rr"])
                    S.op("dve", lambda e: e.tensor_tensor(out=a1[:], in0=p1[0][:], in1=rr[:], op=ALU.mult),
                         reads=["p10", "rr"], writes=["a1"])
                    S.op("dve", lambda e: e.reciprocal(out=bcs[:], in_=p1[3][:]), reads=["p13"], writes=["bcs"])
                    S.op("dve", lambda e: e.tensor_tensor(out=a2[:], in0=p1[1][:], in1=bcs[:], op=ALU.mult),
                         reads=["p11", "bcs"], writes=["a2"])
                    S.op("dve", lambda e: e.scalar_tensor_tensor(out=a1[:], in0=a2[:], scalar=cols[:, 3:4], in1=a1[:],
                                                                 op0=ALU.mult, op1=ALU.add),
                         reads=["a1", "a2", "cols_n"], writes=["a1"])
                    if "NO_ACT_SQRT" in DBG:
                        S.op("dve", lambda e: e.tensor_tensor(out=sq[:], in0=a1[:], in1=a1[:], op=ALU.mult), reads=["a1"], writes=["sq"])
                    else:
                        S.op("act", lambda e: e.activation(out=sq[:], in_=a1[:], func=AF.Square), reads=["a1"], writes=["sq"])
                    S.op("pe", lambda e: e.matmul(p1[2][:], lhsT=onesf[:], rhs=sq[:], start=True, stop=True),
                         reads=["onesf", "sq"], writes=["p12"])
                    if "NO_ACT_SQRT" in DBG:
                        S.op("act", lambda e: e.activation(out=a2[:], in_=p1[2][:], func=AF.Ln, bias=cols[:, 6:7],
                                                           scale=1.0 / 128.0), reads=["p12", "cols_e"], writes=["a2"])
                        S.op("act", lambda e: e.activation(out=a2[:], in_=a2[:], func=AF.Exp, scale=-0.5),
                             reads=["a2"], writes=["a2"])
                    else:
                        S.op("act", lambda e: e.activation(out=a2[:], in_=p1[2][:], func=AF.Sqrt, bias=cols[:, 6:7],
                                                           scale=1.0 / 128.0), reads=["p12", "cols_e"], writes=["a2"])
                        S.op("dve", lambda e: e.reciprocal(out=a2[:], in_=a2[:]), reads=["a2"], writes=["a2"])
                    S.op("dve", lambda e: e.scalar_tensor_tensor(
                        out=yst[yb][:, s * 512:(s + 1) * 512], in0=a1[:], scalar=cols[:, 5:6], in1=a2[:],
                        op0=ALU.mult, op1=ALU.mult), reads=["a1", "a2", "cols_g2"], writes=["yst%d" % yb])

                for pi in range(NB // 2):
                    last = (pi == NB // 2 - 1)
                    for i in range(2):
                        jobs.append(mk_job(i, i, s, pi, NB, NB - 6, cols[:, 8 + h:9 + h], mkpv(i),
                                           end if (last and i == 1) else None))
            if loads_only:
                continue
            run_pipeline(jobs)
            dma(S, "sp", T["yD"][h], yst[yb][:], ["yst%d" % yb], ["yDd"], "at_y%d" % yb)


def phase_post(c, T, l, want_t32):
    S = c.S
    with c.phase() as ph:
        ln = LNUnit(c, ph, T["ln_mix_g"][l], T["ln_mix_b"][l], T["ident"], want_t32=want_t32, tag="pm")
        ln.setup_eps()
        wbf = ph.sb("wbf", [64, 8, D], BF16)
        wbd = ph.sb("wbd", [128, 4, D], BF16)
        wo = ph.sb("wo", [128, 8, D], BF16)
        dma(S, "pool", wbf[:], T["w_branch_fox"][l].rearrange("(h d) n -> d h n", d=64), [], ["wbf"], "po_w")
        dma(S, "pool", wbd[:], T["w_branch_diff"][l].rearrange("(h d) n -> d h n", d=128), [], ["wbd"], "po_w")
        dma(S, "pool", wo[:], T["w_out"][l].rearrange("(k p) n -> p k n", p=128), [], ["wo"], "po_w")
        yF = ph.sb("yF", [64, 8, 512], BF16)
        yD = ph.sb("yD", [128, 4, 512], BF16)
        sga = ph.sb("sga", [128, 8, 512], BF16)
        sgb = ph.sb("sgb", [128, 8, 512], BF16)
        mg = ph.sb("mg", [128, 8, 512], BF16)
        t1 = ph.sb("t1", [128, 512], F32)
        t2 = ph.sb("t2", [128, 512], F32)
        hb = [ph.sb("hb%d" % i, [128, D], F32) for i in range(2)]
        pa = ph.ps("pa", [128, 512])
        pb = ph.ps("pb", [128, 512])
        pm = ph.ps("pm", [128, 1024])
        for tt in range(4):
            tsl = slice(tt * 512, (tt + 1) * 512)
            dma(S, "sp", yF[:], T["yF"][:, :, tsl].rearrange("h d t -> d h t"), [], ["yF"], "po_a")
            dma(S, "sp", yD[:], T["yD"][:, :, tsl].rearrange("h d t -> d h t"), [], ["yD"], "po_a")
            dma(S, "sp", sga[:], T["sg"][0][:, tsl].rearrange("(k p) t -> p k t", p=128), [], ["sga"], "po_a")
            dma(S, "sp", sgb[:], T["sg"][1][:, tsl].rearrange("(k p) t -> p k t", p=128), [], ["sgb"], "po_a")
            for n in range(8):
                for h in range(8):
                    S.op("pe", lambda e, h=h, n=n: e.matmul(pa[:], lhsT=wbf[:, h, n * 128:(n + 1) * 128], rhs=yF[:, h, :],
                                                            start=(h == 0), stop=(h == 7)),
                         reads=["wbf", "yF"], writes=["pa"])
                for h in range(4):
                    S.op("pe", lambda e, h=h, n=n: e.matmul(pb[:], lhsT=wbd[:, h, n * 128:(n + 1) * 128], rhs=yD[:, h, :],
                                                            start=(h == 0), stop=(h == 3)),
                         reads=["wbd", "yD"], writes=["pb"])
                S.op("dve", lambda e, n=n: e.tensor_tensor(out=t1[:], in0=sga[:, n, :], in1=pa[:], op=ALU.mult),
                     reads=["sga", "pa"], writes=["t1"])
                S.op("dve", lambda e, n=n: e.tensor_tensor(out=t2[:], in0=sgb[:, n, :], in1=pb[:], op=ALU.mult),
                     reads=["sgb", "pb"], writes=["t2"])
                S.op("pool", lambda e, n=n: e.tensor_tensor(out=mg[:, n, :], in0=t1[:], in1=t2[:], op=ALU.add),
                     reads=["t1", "t2"], writes=["mg"])
            for tb in range(4):
                tok0 = tt * 512 + tb * 128
                i = tb % 2
                dma(S, "sp", hb[i][:], T["h"][tok0:tok0 + 128, :], [], ["hb%d" % i], "po_h%d" % i)
                for hf in range(2):
                    for k in range(8):
                        S.op("pe", lambda e, k=k, hf=hf, tb=tb: e.matmul(
                            pm[:, hf * 512:(hf + 1) * 512], lhsT=mg[:, k, tb * 128:(tb + 1) * 128],
                            rhs=wo[:, k, hf * 512:(hf + 1) * 512], start=(k == 0), stop=(k == 7)),
                            reads=["mg", "wo"], writes=["pm"])
                S.op("dve", lambda e, i=i: e.scalar_tensor_tensor(out=hb[i][:], in0=hb[i][:], scalar=float(ALPHA),
                                                                  in1=pm[:], op0=ALU.mult, op1=ALU.add),
                     reads=["hb%d" % i, "pm"], writes=["hb%d" % i])
                ln.run(hb[i][:], "hb%d" % i, T["h"][tok0:tok0 + 128, :], T["hT"], tok0,
                       hT32_out=(T["hT32"] if want_t32 else None))


def ffn_core(c, ph, T, S, hTs, hkey, wgu, dff, wdown, nfc, fc0, aT, akey, wd, wdkey, stl, pg, pu, sgt, wgb, wcnt, tag):
    for f in range(nfc):
        fc = fc0 + f
        b = wcnt[0] % 3
        wcnt[0] += 1
        dma(S, "pool", wgb[b][:, 0, :, :], wgu[:, fc * 128:(fc + 1) * 128].rearrange("(k p) n -> p k n", p=128),
            [], [tag + "wg%d" % b], tag + "wg%d" % b)
        dma(S, "pool", wgb[b][:, 1, :, :], wgu[:, dff + fc * 128:dff + (fc + 1) * 128].rearrange("(k p) n -> p k n", p=128),
            [], [tag + "wg%d" % b], tag + "wg%d" % b)
        for t2 in range(2):
            for k in range(8):
                S.op("pe", lambda e, b=b, k=k, t2=t2: e.matmul(pg[:], lhsT=wgb[b][:, 0, k, :],
                                                               rhs=hTs[:, k, t2 * 512:(t2 + 1) * 512],
                                                               start=(k == 0), stop=(k == 7)),
                     reads=[hkey, tag + "wg%d" % b], writes=[tag + "pg"])
            for k in range(8):
                S.op("pe", lambda e, b=b, k=k, t2=t2: e.matmul(pu[:], lhsT=wgb[b][:, 1, k, :],
                                                               rhs=hTs[:, k, t2 * 512:(t2 + 1) * 512],
                                                               start=(k == 0), stop=(k == 7)),
                     reads=[hkey, tag + "wg%d" % b], writes=[tag + "pu"])
            S.op("act", lambda e: e.activation(out=sgt[:], in_=pg[:], func=AF.Silu), reads=[tag + "pg"], writes=[tag + "sgt"])
            S.op("dve", lambda e, f=f, t2=t2: e.tensor_tensor(out=aT[:, f, t2 * 512:(t2 + 1) * 512], in0=sgt[:], in1=pu[:],
                                                              op=ALU.mult),
                 reads=[tag + "sgt", tag + "pu"], writes=[akey])


def phase_ffn(c, T):
    S = c.S
    NF = DFF // 128
    with c.phase() as ph:
        ln = LNUnit(c, ph, T["ln_ffn_g"][0], T["ln_ffn_b"][0], T["ident"], tag="pf")
        ln.setup_eps()
        wd = ph.sb("wd", [128, NF, D], BF16)
        wdv = T["w_ffn_down"][0].rearrange("(f p) n -> p f n", p=128)
        for q in range(0, NF, 6):
            q1 = min(NF, q + 6)
            dma(S, "pool", wd[:, q:q1, :], wdv[:, q:q1, :], [], ["wd"], "ff_wd")
        hTs = ph.sb("hTs", [128, 8, 1024], BF16)
        aT = ph.sb("aT", [128, NF, 1024], BF16)
        wgb = [ph.sb("wgb%d" % i, [128, 2, 8, 128], BF16) for i in range(3)]
        sgt = ph.sb("sgt", [128, 512], F32)
        hb = [ph.sb("hb%d" % i, [128, D], F32) for i in range(2)]
        pg = ph.ps("pg", [128, 512])
        pu = ph.ps("pu", [128, 512])
        pm = ph.ps("pm", [128, 1024])
        wcnt = [0]
        for st in range(2):
            dma(S, "sp", hTs[:], T["hT"][:, st * 1024:(st + 1) * 1024].rearrange("(k p) t -> p k t", p=128),
                [], ["hTs"], "ff_h")
            ffn_core(c, ph, T, S, hTs, "hTs", T["w_ffn_gate_up"][0], DFF, None, NF, 0, aT, "aT", wd, "wd", st,
                     pg, pu, sgt, wgb, wcnt, "ff")
            for tb in range(8):
                tok0 = st * 1024 + tb * 128
                i = tb % 2
                dma(S, "sp", hb[i][:], T["h"][tok0:tok0 + 128, :], [], ["hb%d" % i], "ff_h%d" % i)
                for hf in range(2):
                    for f in range(NF):
                        S.op("pe", lambda e, f=f, hf=hf, tb=tb: e.matmul(
                            pm[:, hf * 512:(hf + 1) * 512], lhsT=aT[:, f, tb * 128:(tb + 1) * 128],
                            rhs=wd[:, f, hf * 512:(hf + 1) * 512], start=(f == 0), stop=(f == NF - 1)),
                            reads=["aT", "wd"], writes=["pm"])
                S.op("dve", lambda e, i=i: e.scalar_tensor_tensor(out=hb[i][:], in0=hb[i][:], scalar=float(ALPHA),
                                                                  in1=pm[:], op0=ALU.mult, op1=ALU.add),
                     reads=["hb%d" % i, "pm"], writes=["hb%d" % i])
                ln.run(hb[i][:], "hb%d" % i, T["h"][tok0:tok0 + 128, :], T["hT"], tok0)


def phase_moe(c, T):
    S = c.S
    NH = 14
    with c.phase() as ph:
        ln = LNUnit(c, ph, T["ln_ffn_g"][1], T["ln_ffn_b"][1], T["ident"], tag="pe", with_t=False)
        ln.setup_eps()
        hTs = ph.sb("hTs", [128, 8, 1024], BF16)
        aT = [ph.sb("aT%d" % i, [128, NH, 1024], BF16) for i in range(2)]
        wd = [ph.sb("wd%d" % i, [128, NH, D], BF16) for i in range(2)]
        wgb = [ph.sb("wgb%d" % i, [128, 2, 8, 128], BF16) for i in range(3)]
        acc = ph.sb("acc", [128, 8, D], F32)
        sgt = ph.sb("sgt", [128, 512], F32)
        h32 = [ph.sb("h32%d" % i, [128, 8, 128], F32) for i in range(2)]
        wr = ph.sb("wr", [128, 8, NE], F32)
        comb = ph.sb("comb", [128, 8, NE], F32)
        rt = ph.sb("rt", [128, 8, 8], F32)
        rs = ph.sb("rs", [128, 4], F32)
        pg = ph.ps("pg", [128, 512])
        pu = ph.ps("pu", [128, 512])
        pm = ph.ps("pm", [128, 1024])
        pr = ph.ps("pr", [128, 8])
        dma(S, "sp", wr[:], T["w_router"][0].rearrange("(k p) e -> p k e", p=128), [], ["wr"], "mo_c")
        wcnt = [0]
        hcnt = [0]
        for st in range(2):
            dma(S, "sp", hTs[:], T["hT"][:, st * 1024:(st + 1) * 1024].rearrange("(k p) t -> p k t", p=128),
                [], ["hTs"], "mo_h")
            for tb in range(8):
                tok0 = st * 1024 + tb * 128
                i = tb % 2
                dma(S, "sp", h32[i][:], T["hT32"].rearrange("(k p) t -> p k t", p=128)[:, :, tok0:tok0 + 128],
                    [], ["h32%d" % i], "mo_r%d" % i)
                for k in range(8):
                    S.op("pe", lambda e, k=k, i=i: e.matmul(pr[:], lhsT=h32[i][:, k, :], rhs=wr[:, k, :],
                                                            start=(k == 0), stop=(k == 7)),
                         reads=["h32%d" % i, "wr"], writes=["pr"])
                S.op("dve", lambda e: e.tensor_copy(out=rt[:, 0, :], in_=pr[:]), reads=["pr"], writes=["rt0"])
                S.op("dve", lambda e: e.max(out=rt[:, 1, :], in_=rt[:, 0, :]), reads=["rt0"], writes=["rt1"])
                S.op("dve", lambda e: e.tensor_scalar(out=rt[:, 2, :], in0=rt[:, 0, :], scalar1=rt[:, 1, 1:2], scalar2=None,
                                                      op0=ALU.is_ge), reads=["rt0", "rt1"], writes=["rt2"])
                S.op("dve", lambda e: e.tensor_scalar(out=rs[:, 0:1], in0=rt[:, 1, 0:1], scalar1=-1.0, scalar2=None,
                                                      op0=ALU.mult), reads=["rt1"], writes=["rs0"])
                S.op("act", lambda e: e.activation(out=rt[:, 3, :], in_=rt[:, 0, :], func=AF.Exp, bias=rs[:, 0:1]),
                     reads=["rt0", "rs0"], writes=["rt3"])
                S.op("dve", lambda e: e.tensor_tensor(out=rt[:, 4, :], in0=rt[:, 2, :], in1=rt[:, 3, :], op=ALU.mult),
                     reads=["rt2", "rt3"], writes=["rt4"])
                S.op("dve", lambda e: e.tensor_reduce(out=rs[:, 1:2], in_=rt[:, 4, :], axis=AX.X, op=ALU.add),
                     reads=["rt4"], writes=["rs1"])
                S.op("dve", lambda e: e.reciprocal(out=rs[:, 2:3], in_=rs[:, 1:2]), reads=["rs1"], writes=["rs2"])
                S.op("dve", lambda e, tb=tb: e.tensor_scalar(out=comb[:, tb, :], in0=rt[:, 4, :], scalar1=rs[:, 2:3],
                                                             scalar2=None, op0=ALU.mult),
                     reads=["rt4", "rs2"], writes=["comb"])
            for ex in range(NE):
                for hf in range(2):
                    hb_ = hcnt[0] % 2
                    hcnt[0] += 1
                    wdv = T["w_expert_down"][0][ex].rearrange("(f p) n -> p f n", p=128)
                    for q in range(0, NH, 7):
                        dma(S, "pool", wd[hb_][:, q:q + 7, :], wdv[:, hf * NH + q:hf * NH + q + 7, :], [],
                            ["wd%d" % hb_], "mo_wd%d" % hb_)
                    ffn_core(c, ph, T, S, hTs, "hTs", T["w_expert_gate_up"][0][ex], DFE, None, NH, hf * NH, aT[hb_],
                             "aT%d" % hb_, None, None, st, pg, pu, sgt, wgb, wcnt, "mo")
                    for tb in range(8):
                        for h2 in range(2):
                            for f in range(NH):
                                S.op("pe", lambda e, f=f, h2=h2, tb=tb, hb_=hb_: e.matmul(
                                    pm[:, h2 * 512:(h2 + 1) * 512], lhsT=aT[hb_][:, f, tb * 128:(tb + 1) * 128],
                                    rhs=wd[hb_][:, f, h2 * 512:(h2 + 1) * 512], start=(f == 0), stop=(f == NH - 1)),
                                    reads=["aT%d" % hb_, "wd%d" % hb_], writes=["pm"])
                        if ex == 0 and hf == 0:
                            S.op("dve", lambda e, tb=tb, ex=ex: e.tensor_scalar(
                                out=acc[:, tb, :], in0=pm[:], scalar1=comb[:, tb, ex:ex + 1], scalar2=None, op0=ALU.mult),
                                reads=["pm", "comb"], writes=[("acc", tb)])
                        else:
                            S.op("dve", lambda e, tb=tb, ex=ex: e.scalar_tensor_tensor(
                                out=acc[:, tb, :], in0=pm[:], scalar=comb[:, tb, ex:ex + 1], in1=acc[:, tb, :],
                                op0=ALU.mult, op1=ALU.add), reads=["pm", "comb", ("acc", tb)], writes=[("acc", tb)])
            for tb in range(8):
                tok0 = st * 1024 + tb * 128
                i = tb % 2
                dma(S, "sp", h32[i][:].rearrange("p a b -> p (a b)"), T["h"][tok0:tok0 + 128, :], [], ["h32%d" % i],
                    "mo_r%d" % i)
                S.op("dve", lambda e, i=i, tb=tb: e.scalar_tensor_tensor(
                    out=acc[:, tb, :], in0=h32[i][:].rearrange("p a b -> p (a b)"), scalar=float(ALPHA),
                    in1=acc[:, tb, :], op0=ALU.mult, op1=ALU.add),
                    reads=["h32%d" % i, ("acc", tb)], writes=[("acc", tb)])
                ln.run(acc[:, tb, :], ("acc", tb), T["out"][tok0:tok0 + 128, :], None, tok0, do_t=False)


W_SPECS = {
    "ln_in_g": ([D], F32), "ln_in_b": ([D], F32), "w_in": ([2, D, NIN], F32), "b_fgate": ([2, 8], F32),
    "lam_q1": ([2, 64], F32), "lam_k1": ([2, 64], F32), "lam_q2": ([2, 64], F32), "lam_k2": ([2, 64], F32),
    "subln_g": ([2, 128], F32), "w_branch_fox": ([2, 512, D], F32), "w_branch_diff": ([2, 512, D], F32),
    "w_out": ([2, D, D], F32), "ln_mix_g": ([2, D], F32), "ln_mix_b": ([2, D], F32), "rel_bias": ([32, 4], F32),
    "w_ffn_gate_up": ([1, D, 2 * DFF], F32), "w_ffn_down": ([1, DFF, D], F32), "w_router": ([1, D, NE], F32),
    "w_expert_gate_up": ([1, NE, D, 2 * DFE], F32), "w_expert_down": ([1, NE, DFE, D], F32),
    "ln_ffn_g": ([2, D], F32), "ln_ffn_b": ([2, D], F32),
}
A_SPECS = {
    "x": ([TL, D], F32), "h": ([TL, D], F32), "hT": ([D, TL], BF16), "hT32": ([D, TL], F32),
    "qT": ([16, 64, TL], BF16), "kT": ([16, 64, TL], BF16), "vf": ([TL, 512], BF16), "vd": ([TL, 512], BF16),
    "lf": ([TL, 8], F32), "sg": ([2, D, TL], BF16),
    "kL": ([16, 64, S_ALL], BF16), "vfL": ([S_ALL, 512], BF16), "vdL": ([S_ALL, 512], BF16),
    "lfL": ([S_ALL, 8], F32), "padb": ([S_ALL], F32), "su": ([128, 128], F32), "ident": ([128, 128], BF16),
    "maskF": ([5, 128, 512], F32), "BT": ([4, 6, 128, 512], F32),
    "KX": ([8, 4, S_ALL], BF16), "KXD": ([4, S_ALL], BF16), "QX": ([8, TL], BF16),
    "yF": ([8, 64, TL], BF16), "yD": ([4, 128, TL], BF16), "out": ([TL, D], F32),
}
LAUNCH = {
    1: dict(ins=["x", "ident", "ln_in_g", "ln_in_b", "w_in", "b_fgate"],
            outs=["h", "hT", "qT", "kT", "vf", "vd", "lf", "sg"], internal=[]),
    2: dict(ins=["h_in", "qT", "sg", "kL", "vfL", "vdL", "lfL", "padb", "su", "ident", "maskF", "BT",
                 "lam_q1", "lam_k1", "lam_q2", "lam_k2", "subln_g", "rel_bias", "w_branch_fox", "w_branch_diff",
                 "w_out", "ln_mix_g", "ln_mix_b", "w_ffn_gate_up", "w_ffn_down", "ln_ffn_g", "ln_ffn_b",
                 "w_in", "b_fgate"],
            outs=["h_o", "qT_o", "kT_o", "vf_o", "vd_o", "lf_o", "sg_o"],
            internal=["h", "hT", "KX", "KXD", "QX", "yF", "yD"]),
    3: dict(ins=["h_in", "qT", "sg", "kL", "vfL", "vdL", "lfL", "padb", "su", "ident", "maskF", "BT",
                 "lam_q1", "lam_k1", "lam_q2", "lam_k2", "subln_g", "rel_bias", "w_branch_fox", "w_branch_diff",
                 "w_out", "ln_mix_g", "ln_mix_b", "w_router", "w_expert_gate_up", "w_expert_down",
                 "ln_ffn_g", "ln_ffn_b"],
            outs=["out"], internal=["h", "hT", "hT32", "KX", "KXD", "QX", "yF", "yD"]),
}


def _spec(name):
    base = name[:-2] if name.endswith("_o") else ("h" if name == "h_in" else name)
    return W_SPECS[base] if base in W_SPECS else A_SPECS[base]


def build_launch(lid):
    nc = bass.Bass("TRN2", target_bir_lowering=False)
    L = LAUNCH[lid]
    T = {}
    for n in L["ins"]:
        sh, dt = _spec(n)
        T[n] = nc.dram_tensor(n, sh, dt, kind="ExternalInput").ap()
    for n in L["outs"]:
        sh, dt = _spec(n)
        T[n] = nc.dram_tensor(n, sh, dt, kind="ExternalOutput").ap()
    for n in L["internal"]:
        sh, dt = _spec(n)
        T[n] = nc.dram_tensor(n, sh, dt, kind="Internal").ap()
    with ExitStack() as st:
        S = Sched(nc, st)
        c = Ctx(nc, S)
        if lid == 1:
            phase_ln0(c, T)
            phase_inproj(c, T, 0)
        else:
            l = lid - 2
            lam_init = 0.8 - 0.6 * math.exp(-0.3 * l)
            with c.phase() as ph:
                dma(S, "sp", T["h"], T["h_in"], [], ["hcp"], "cp0")
            phase_cum(c, T)
            phase_attn(c, T, l, lam_init)
            phase_post(c, T, l, want_t32=(l == 1))
            if l == 0:
                phase_ffn(c, T)
                T2 = dict(T)
                for n in ("qT", "kT", "vf", "vd", "lf", "sg"):
                    T2[n] = T[n + "_o"]
                phase_inproj(c, T2, 1)
                with c.phase() as ph:
                    dma(S, "sp", T["h_o"], T["h"], [], ["hcp2"], "cp1")
            else:
                phase_moe(c, T)
        S.flush(final=True)
    return nc


_PROGS = {}


def _prog(lid):
    if lid not in _PROGS:
        _PROGS[lid] = build_launch(lid)
    return _PROGS[lid]


def _t5_bucket(n):
    n = np.maximum(n, 0)
    nf = np.maximum(n, 1).astype(np.float32)
    lp = (np.log(nf / np.float32(16)) / np.float32(math.log(8.0)) * np.float32(16)).astype(np.float32)
    large = np.minimum(16 + lp.astype(np.int32), 31)
    return np.where(n < 16, n, large)


def _static_masks(rel_bias):
    k = np.arange(128)[:, None]
    q = np.arange(512)[None, :]
    maskF = np.zeros((5, 128, 512), np.float32)
    BT = np.zeros((4, 6, 128, 512), np.float32)
    for mi in range(6):
        dist = q - k - (mi - 2) * 128
        ok = dist >= 0
        idx = _t5_bucket(dist)
        for h in range(4):
            g = rel_bias[:, h][idx]
            BT[h, mi] = np.where(ok, g, np.float32(NEG))
        if mi >= 1:
            maskF[mi - 1] = np.where(ok, np.float32(0), np.float32(NEG))
    return maskF, BT


def _gather_tokens(parts, axis):
    shp = list(parts[0].shape)
    shp[axis] = S_ALL
    out = np.zeros(shp, parts[0].dtype)
    for c in range(NCORE):
        for s in range(4):
            T_ = 8 * s + c
            src = [slice(None)] * len(shp)
            dst = [slice(None)] * len(shp)
            src[axis] = slice(s * 512, (s + 1) * 512)
            dst[axis] = slice(T_ * 512, (T_ + 1) * 512)
            out[tuple(dst)] = parts[c][tuple(src)]
    return out


def _local_view(glob, axis, c):
    shp = list(glob.shape)
    shp[axis] = NPADB * 128
    padded = np.concatenate([np.zeros(shp, glob.dtype), glob], axis=axis)
    sl = [slice(None)] * len(shp)
    sl[axis] = slice(4 * c * 128, 4 * c * 128 + S_ALL)
    return np.ascontiguousarray(padded[tuple(sl)])


def _run(lid, in_maps):
    nc = _prog(lid)
    res = run_bass_kernel_spmd(nc, in_maps, core_ids=list(range(NCORE)))
    return res.results


def kernel(**inp):
    inp = {k: np.ascontiguousarray(np.asarray(v)) for k, v in inp.items()}
    x = inp["x"].reshape(S_ALL, D)
    ident = np.eye(128, dtype=np.float32).astype(ml_dtypes.bfloat16)
    su = np.triu(np.ones((128, 128), np.float32), 1)
    maskF, BT = _static_masks(inp["rel_bias"].astype(np.float32))
    xs = []
    for c in range(NCORE):
        xs.append(np.concatenate([x[(8 * s + c) * 512:(8 * s + c + 1) * 512] for s in range(4)], axis=0))
    wl = lambda names: {n: inp[n] for n in names if n in W_SPECS}
    L = LAUNCH[1]
    maps = [dict(wl(L["ins"]), x=xs[c], ident=ident) for c in range(NCORE)]
    r = _run(1, maps)
    out = None
    for lid in (2, 3):
        kg = _gather_tokens([r[c]["kT"] if lid == 2 else r[c]["kT_o"] for c in range(NCORE)], 2)
        sfx = "" if lid == 2 else "_o"
        vfg = _gather_tokens([r[c]["vf" + sfx] for c in range(NCORE)], 0)
        vdg = _gather_tokens([r[c]["vd" + sfx] for c in range(NCORE)], 0)
        lfg = _gather_tokens([r[c]["lf" + sfx] for c in range(NCORE)], 0)
        L = LAUNCH[lid]
        maps = []
        for c in range(NCORE):
            m = dict(wl(L["ins"]))
            m["h_in"] = r[c]["h" if lid == 2 else "h_o"]
            m["qT"] = r[c]["qT" + sfx]
            m["sg"] = r[c]["sg" + sfx]
            m["kL"] = _local_view(kg, 2, c)
            m["vfL"] = _local_view(vfg, 0, c)
            m["vdL"] = _local_view(vdg, 0, c)
            m["lfL"] = _local_view(lfg, 0, c)
            pb = np.zeros((S_ALL,), np.float32)
            pb[:max(0, (NPADB - 4 * c)) * 128] = NEG
            m["padb"] = pb
            m["su"], m["ident"], m["maskF"], m["BT"] = su, ident, maskF, BT
            maps.append(m)
        r = _run(lid, maps)
    full = np.zeros((S_ALL, D), np.float32)
    for c in range(NCORE):
        o = np.asarray(r[c]["out"], np.float32)
        for s in range(4):
            T_ = 8 * s + c
            full[T_ * 512:(T_ + 1) * 512] = o[s * 512:(s + 1) * 512]
    return full.reshape(1, S_ALL, D)
```
